# Optimizing a Trainium2 kernel written in Bass

```python
import jax, jax.numpy as jnp
from jax import lax
import numpy as np

D_MODEL = 1024
BATCH = 16
SEQ = 4096
DEPTH = 2
DEC_BATCH = 8
DEC_SEQ = 16
PAST_LEN = 2048

CHUNK = 64
GMLP_CHUNK = 128
D_A = D_MODEL
A_GROUPS = 8
A_GROUP_W = D_A // A_GROUPS
B_HEADS = 8
B_DK = 128
B_DV = 128
D_B = B_HEADS * B_DV
CONV_W = 4
D_FF = 4 * D_MODEL
EPS = 1e-6
SPLIT_SIZES = (D_A, D_A, 3 * D_B, D_B, B_HEADS, B_HEADS, D_MODEL, D_MODEL)
N_IN = sum(SPLIT_SIZES)

kernel_name = "gmlp_gated_deltanet_streaming_encoder"


def rmsnorm(x, g):
    xf = x.astype(jnp.float32)
    y = xf * lax.rsqrt(jnp.mean(xf * xf, axis=-1, keepdims=True) + EPS)
    return (y * g.astype(jnp.float32)).astype(x.dtype)


def layernorm(x, g, b):
    xf = x.astype(jnp.float32)
    mu = jnp.mean(xf, axis=-1, keepdims=True)
    var = jnp.mean(jnp.square(xf - mu), axis=-1, keepdims=True)
    y = (xf - mu) * lax.rsqrt(var + EPS)
    return (y * g.astype(jnp.float32) + b.astype(jnp.float32)).astype(x.dtype)


def l2norm(x):
    return x * lax.rsqrt(jnp.sum(x * x, axis=-1, keepdims=True) + EPS)


def causal_conv(x, prev, w):
    L = x.shape[1]
    xp = jnp.concatenate([prev.astype(x.dtype), x], axis=1)
    y = xp[:, 0:L] * w[:, 0]
    for j in range(1, CONV_W):
        y = y + xp[:, j:j + L] * w[:, j]
    return y, xp[:, -(CONV_W - 1):]


def gmlp_spatial(vn, w_s, b_s):
    B, L, _ = vn.shape
    c = min(L, GMLP_CHUNK)
    n = L // c
    mask = jnp.tril(jnp.ones((c, c), dtype=bool))
    w = jnp.where(mask, w_s[:, :c, :c], 0.0).astype(vn.dtype)
    v5 = vn.reshape(B, n, c, A_GROUPS, A_GROUP_W)
    s = jnp.einsum('gts,bnsgc->bntgc', w, v5)
    s = s + jnp.transpose(b_s[:, :c])[None, None, :, :, None].astype(vn.dtype)
    return s.reshape(B, L, D_A)


def gated_delta_rule(q, k, v, g, beta, s0):
    B, L, H, _ = q.shape
    c = min(L, CHUNK)
    n = L // c
    qb = q.reshape(B, n, c, H, B_DK).transpose(1, 0, 3, 2, 4)
    kb_ = k.reshape(B, n, c, H, B_DK).transpose(1, 0, 3, 2, 4)
    vb = v.reshape(B, n, c, H, B_DV).transpose(1, 0, 3, 2, 4)
    gb = g.reshape(B, n, c, H).transpose(1, 0, 3, 2)
    bb = beta.reshape(B, n, c, H).transpose(1, 0, 3, 2)
    gam = jnp.cumsum(gb, axis=-1)
    causal = jnp.tril(jnp.ones((c, c), dtype=bool))
    strict = jnp.tril(jnp.ones((c, c), dtype=bool), -1)
    decay = jnp.exp(jnp.where(causal, gam[..., :, None] - gam[..., None, :], -jnp.inf))
    k_beta = kb_ * bb[..., None]
    A = jnp.where(strict, jnp.einsum('nbhid,nbhjd->nbhij', k_beta, kb_) * decay, 0.0)
    T = A + jnp.eye(c, dtype=A.dtype)
    U = lax.linalg.triangular_solve(T, vb * bb[..., None], left_side=True, lower=True, unit_diagonal=True)
    W = lax.linalg.triangular_solve(T, k_beta * jnp.exp(gam)[..., None], left_side=True, lower=True, unit_diagonal=True)
    P = jnp.einsum('nbhid,nbhjd->nbhij', qb, kb_) * decay
    qg = qb * jnp.exp(gam)[..., None]
    kg = kb_ * jnp.exp(gam[..., -1:] - gam)[..., None]
    glast = jnp.exp(gam[..., -1])

    def step(S, xs):
        U_, W_, P_, qg_, kg_, gl_ = xs
        vnew = U_ - jnp.einsum('bhcd,bhde->bhce', W_, S)
        o = jnp.einsum('bhcd,bhde->bhce', qg_, S) + jnp.einsum('bhij,bhje->bhie', P_, vnew)
        S = S * gl_[..., None, None] + jnp.einsum('bhcd,bhce->bhde', kg_, vnew)
        return S, o

    S, o = lax.scan(step, s0, (U, W, P, qg, kg, glast))
    o = o.transpose(1, 0, 3, 2, 4).reshape(B, L, H, B_DV)
    return o, S


def trunk_layer(x, conv_prev, s0, ln1, w_in, a_ln_g, a_ln_b, w_s, b_s, conv_w, a_log, dt_bias,
                o_norm, p_a, p_b, w_o, ln2, w_up, w_down):
    B, L, _ = x.shape
    h = rmsnorm(x, ln1)
    proj = h @ w_in
    idx = list(np.cumsum(SPLIT_SIZES)[:-1])
    u, va, qkv, z, b_raw, a_raw, ga, gb = jnp.split(proj, idx, axis=-1)

    u = jax.nn.gelu(u, approximate=False)
    vn = layernorm(jax.nn.gelu(va, approximate=False), a_ln_g, a_ln_b)
    z_a = u * gmlp_spatial(vn, w_s, b_s)

    qkv, conv_state = causal_conv(qkv, conv_prev, conv_w)
    qkv = jax.nn.silu(qkv).astype(jnp.float32)
    q, k, v = jnp.split(qkv, 3, axis=-1)
    q = l2norm(q.reshape(B, L, B_HEADS, B_DK)) * (B_DK ** -0.5)
    k = l2norm(k.reshape(B, L, B_HEADS, B_DK))
    v = v.reshape(B, L, B_HEADS, B_DV)
    beta = jax.nn.sigmoid(b_raw.astype(jnp.float32))
    g = -jnp.exp(a_log.astype(jnp.float32)) * jax.nn.softplus(a_raw.astype(jnp.float32) + dt_bias.astype(jnp.float32))
    o, S = gated_delta_rule(q, k, v, g, beta, s0.astype(jnp.float32))
    o = rmsnorm(o, o_norm) * jax.nn.silu(z.astype(jnp.float32).reshape(B, L, B_HEADS, B_DV))
    z_b = o.reshape(B, L, D_B).astype(x.dtype)

    mix = jax.nn.sigmoid(ga) * (z_a @ p_a) + jax.nn.sigmoid(gb) * (z_b @ p_b)
    x = x + mix @ w_o
    h2 = rmsnorm(x, ln2)
    x = x + jnp.square(jax.nn.relu(h2 @ w_up)) @ w_down
    return x, conv_state, S.astype(x.dtype), vn


def setup_inputs(seed: int = 0) -> dict:
    key = jax.random.key(seed)
    ks = jax.random.split(key, 24)
    f32 = jnp.float32
    nrm = lambda k, shape, s: jax.random.normal(k, shape, f32) * s
    dt = jnp.exp(jax.random.uniform(ks[13], (DEPTH, B_HEADS), f32, np.log(1e-3), np.log(1e-1)))
    return {
        "x_prompt": nrm(ks[0], (BATCH, SEQ, D_MODEL), 1.0),
        "x_sample": nrm(ks[1], (DEC_BATCH, DEC_SEQ, D_MODEL), 1.0),
        "state_conv": nrm(ks[2], (DEPTH, DEC_BATCH, CONV_W - 1, 3 * D_B), 1.0),
        "state_delta": nrm(ks[3], (DEPTH, DEC_BATCH, B_HEADS, B_DK, B_DV), 0.1),
        "ln1": 1.0 + nrm(ks[4], (DEPTH, D_MODEL), 0.02),
        "w_in": nrm(ks[5], (DEPTH, D_MODEL, N_IN), D_MODEL ** -0.5),
        "a_ln_g": 1.0 + nrm(ks[6], (DEPTH, D_A), 0.02),
        "a_ln_b": nrm(ks[7], (DEPTH, D_A), 0.02),
        "w_s": nrm(ks[8], (DEPTH, A_GROUPS, GMLP_CHUNK, GMLP_CHUNK), 0.5 * GMLP_CHUNK ** -0.5),
        "b_s": 1.0 + nrm(ks[9], (DEPTH, A_GROUPS, GMLP_CHUNK), 0.02),
        "conv_w": nrm(ks[10], (DEPTH, 3 * D_B, CONV_W), 0.5),
        "a_log": jnp.log(jax.random.uniform(ks[11], (DEPTH, B_HEADS), f32, 1.0, 16.0)),
        "dt_bias": dt + jnp.log(-jnp.expm1(-dt)),
        "o_norm": 1.0 + nrm(ks[12], (DEPTH, B_DV), 0.02),
        "p_a": nrm(ks[14], (DEPTH, D_A, D_MODEL), D_A ** -0.5),
        "p_b": nrm(ks[15], (DEPTH, D_B, D_MODEL), D_B ** -0.5),
        "w_o": nrm(ks[16], (DEPTH, D_MODEL, D_MODEL), D_MODEL ** -0.5),
        "ln2": 1.0 + nrm(ks[17], (DEPTH, D_MODEL), 0.02),
        "w_up": nrm(ks[18], (DEPTH, D_MODEL, D_FF), D_MODEL ** -0.5),
        "w_down": nrm(ks[19], (DEPTH, D_FF, D_MODEL), 0.5 * D_FF ** -0.5),
        "final_norm": 1.0 + nrm(ks[20], (D_MODEL,), 0.02),
    }


def reference(x_prompt, x_sample, state_conv, state_delta, ln1, w_in, a_ln_g, a_ln_b, w_s, b_s,
              conv_w, a_log, dt_bias, o_norm, p_a, p_b, w_o, ln2, w_up, w_down, final_norm):
    yp, ys = x_prompt, x_sample
    conv_p, delta_p, conv_s, delta_s, gv_s = [], [], [], [], []
    zero_conv = jnp.zeros((x_prompt.shape[0], CONV_W - 1, 3 * D_B), x_prompt.dtype)
    zero_delta = jnp.zeros((x_prompt.shape[0], B_HEADS, B_DK, B_DV), jnp.float32)
    for l in range(DEPTH):
        params = (ln1[l], w_in[l], a_ln_g[l], a_ln_b[l], w_s[l], b_s[l], conv_w[l], a_log[l], dt_bias[l],
                  o_norm[l], p_a[l], p_b[l], w_o[l], ln2[l], w_up[l], w_down[l])
        yp, cp, dp, _ = trunk_layer(yp, zero_conv, zero_delta, *params)
        ys, cs, ds, vs = trunk_layer(ys, state_conv[l], state_delta[l], *params)
        conv_p.append(cp); delta_p.append(dp); conv_s.append(cs); delta_s.append(ds); gv_s.append(vs)
    y_prompt = rmsnorm(yp, final_norm)
    y_sample = rmsnorm(ys, final_norm)
    new_conv_prompt = jnp.stack(conv_p)
    new_delta_prompt = jnp.stack(delta_p)
    new_conv_sample = jnp.stack(conv_s)
    new_delta_sample = jnp.stack(delta_s)
    new_gmlp_v_sample = jnp.stack(gv_s)
    return (y_prompt, y_sample, new_conv_prompt, new_delta_prompt, new_conv_sample, new_delta_sample, new_gmlp_v_sample)
```

```python
import numpy as np
from contextlib import ExitStack
import concourse.bass as bass
import concourse.mybir as mybir
from concourse.bass_utils import run_bass_kernel_spmd

F32 = mybir.dt.float32
BF16 = mybir.dt.bfloat16
AF = mybir.ActivationFunctionType
ALU = mybir.AluOpType
AX = mybir.AxisListType

NCORES = 8
D = 1024
DEPTH = 2
NIN = 8208
DFF = 4096
EPS = 1e-6
NEG = -30000.0
SEQ_FULL = 4096
DEC_SEQ = 16
CW = 4


class Prog:
    def __init__(self, nc, es):
        self.nc = nc
        self.es = es
        self.eng = {"pe": nc.tensor, "act": nc.scalar, "dve": nc.vector, "pool": nc.gpsimd, "sp": nc.sync}
        self.sem = {k: es.enter_context(nc.semaphore("sem_" + k)) for k in self.eng}
        self.cnt = {k: 0 for k in self.eng}
        self.waited = {k: {} for k in self.eng}
        self.lastw = {}
        self.readers = {}
        self.dma_sems = {}
        self.dma_cnt = {}
        self.ninst = 0
        self.dead = False
        self.stop_at = None

    def stage(self, name):
        if self.stop_at is not None and name == self.stop_at:
            self.dead = True
            self.stopped = True
        elif getattr(self, "start_at", None) is not None and not getattr(self, "stopped", False):
            if name == "setup":
                self.dead = True
            elif name == self.start_at:
                self.dead = False

    def _need(self, e, semkey, val):
        if val <= 0:
            return
        w = self.waited[e]
        if w.get(semkey, 0) >= val:
            return
        w[semkey] = val
        s = self.sem[semkey] if semkey in self.sem else self.dma_sems[semkey]
        self.eng[e].wait_ge(s, val)
        self.ninst += 1

    def _deps(self, e, reads, writes):
        skip_self = (e == "pe")
        for k in reads:
            lw = self.lastw.get(k)
            if lw and not (skip_self and lw[0] == e):
                self._need(e, lw[0], lw[1])
        for k in writes:
            lw = self.lastw.get(k)
            if lw and not (skip_self and lw[0] == e):
                self._need(e, lw[0], lw[1])
            for sk, v in self.readers.get(k, {}).items():
                if not (skip_self and sk == e):
                    self._need(e, sk, v)

    def _commit(self, semkey, val, reads, writes):
        for k in writes:
            self.lastw[k] = (semkey, val)
            self.readers[k] = {}
        for k in reads:
            r = self.readers.setdefault(k, {})
            if r.get(semkey, 0) < val:
                r[semkey] = val

    def op(self, e, fn, *args, R=(), W=(), inc=True, **kw):
        if self.dead:
            return None
        W = list(W) + [k for k in R if k.startswith("pd") and k not in W]
        self._deps(e, R, W)
        ins = getattr(self.eng[e], fn)(*args, **kw)
        self.ninst += 1
        if inc:
            self.cnt[e] += 1
            ins.then_inc(self.sem[e], 1)
            self._commit(e, self.cnt[e], R, W)
        else:
            self._commit(e, self.cnt[e] + 1, R, W)
        return ins

    def dma(self, e, out, in_, R, W, semkey):
        if self.dead:
            return None
        if semkey not in self.dma_sems:
            self.dma_sems[semkey] = self.es.enter_context(self.nc.semaphore(semkey))
            self.dma_cnt[semkey] = 0
        self._deps(e, R, W)
        ins = self.eng[e].dma_start(out=out, in_=in_)
        self.dma_cnt[semkey] += 16
        ins.then_inc(self.dma_sems[semkey], 16)
        self._commit(semkey, self.dma_cnt[semkey], R, W)
        self.ninst += 1
        return ins

    def finish(self, e="sp"):
        for k, (sk, v) in list(self.lastw.items()):
            self._need(e, sk, v)


def build(SEQ, stop_at=None, lite=False, start_at=None):
    nc = bass.Bass("TRN2", target_bir_lowering=False)
    es = ExitStack()

    def din(name, shape, dt=F32):
        return nc.dram_tensor(name, list(shape), dt, kind="ExternalInput").ap()

    def dout(name, shape, dt=F32):
        return nc.dram_tensor(name, list(shape), dt, kind="ExternalOutput").ap()

    def dscr(name, shape, dt=BF16):
        return nc.dram_tensor(name, list(shape), dt, kind="Internal").ap()

    xT_p = din("xT_p", [2, 8, 128, SEQ])
    xT_s = din("xT_s", [1, 8, 128, DEC_SEQ])
    sconv = din("sconv", [DEPTH, 128, 24, 3])
    sdelta = din("sdelta", [DEPTH, 8, 128, 128])
    ln1_d = din("ln1c", [128, DEPTH, 8])
    ln2_d = din("ln2c", [128, DEPTH, 8])
    fn_d = din("fnc", [128, 8])
    cw_d = din("cwc", [128, DEPTH, 24, 4])
    on_d = din("onc", [128, DEPTH])
    alg_d = din("algb", [128, DEPTH, D])
    alb_d = din("albb", [128, DEPTH, D])
    wst_d = din("wstT", [128, DEPTH, 8, 128])
    bs_d = din("bsr", [1, DEPTH, D])
    alog_d = din("alogb", [128, DEPTH, 8])
    dtb_d = din("dtbb", [128, DEPTH, 8])
    wdecl = (lambda n, s_: dscr(n + "_lite", s_, F32)) if lite else din
    w_in_d = wdecl("w_in", [DEPTH, D, NIN])
    p_a_d = wdecl("p_a", [DEPTH, D, D])
    p_b_d = wdecl("p_b", [DEPTH, D, D])
    w_o_d = wdecl("w_o", [DEPTH, D, D])
    w_up_d = wdecl("w_up", [DEPTH, D, DFF])
    w_dn_d = wdecl("w_down", [DEPTH, DFF, D])

    yT_p = dout("yT_p", [2, 8, 128, SEQ])
    yT_s = dout("yT_s", [1, 8, 128, DEC_SEQ])
    ncv_p = dout("ncv_p", [DEPTH, 2, 128, 24, 3])
    ndl_p = dout("ndl_p", [DEPTH, 2, 8, 128, 128])
    ncv_s = dout("ncv_s", [DEPTH, 1, 128, 24, 3])
    ndl_s = dout("ndl_s", [DEPTH, 1, 8, 128, 128])
    ngv_s = dout("ngv_s", [DEPTH, 1, DEC_SEQ, D])

    w_in_b = dscr("w_in_b", [DEPTH, D, NIN])
    p_a_b = dscr("p_a_b", [DEPTH, D, D])
    p_b_b = dscr("p_b_b", [DEPTH, D, D])
    w_o_b = dscr("w_o_b", [DEPTH, D, D])
    w_up_b = dscr("w_up_b", [DEPTH, D, DFF])
    w_dn_b = dscr("w_dn_b", [DEPTH, DFF, D])

    with es:
        P = Prog(nc, es)
        P.stop_at = stop_at
        P.start_at = start_at

        def sb(name, shape, dt):
            return es.enter_context(nc.sbuf_tensor(name, list(shape), dt))

        TM = 128

        identf = sb("identf", [128, 128], F32)
        identb = sb("identb", [128, 128], BF16)
        trif = sb("trif", [128, 128], F32)
        ntrif = sb("ntrif", [128, 128], F32)
        trigt = sb("trigt", [128, 128], F32)
        onesf = sb("onesf", [128, 128], F32)
        offd = sb("offd", [128, 128], F32)
        nonesf = sb("nonesf", [128, 128], F32)
        onesb = sb("onesb", [128, 128], BF16)
        maskb = sb("maskb", [128, 8, 128], BF16)
        maskc = sb("maskc", [128, 8, 128], BF16)
        epsc = sb("epsc", [128, 1], F32)
        epsq = sb("epsq", [128, 1], F32)
        ln1c = sb("ln1c_s", [128, DEPTH, 8], F32)
        ln2c = sb("ln2c_s", [128, DEPTH, 8], F32)
        fnc = sb("fnc_s", [128, 8], F32)
        cwc = sb("cwc_s", [128, DEPTH, 24, 4], F32)
        onc = sb("onc_s", [128, DEPTH], F32)
        algb = sb("algb_s", [128, DEPTH, D], F32)
        albb = sb("albb_s", [128, DEPTH, D], F32)
        wstb = sb("wstb", [128, DEPTH, 8, 128], BF16)
        bsb = sb("bsb", [1, DEPTH, D], BF16)
        alogb = sb("alogb_s", [128, DEPTH, 8], F32)
        nexpA = sb("nexpA", [128, DEPTH, 8], F32)
        dtbb = sb("dtbb_s", [128, DEPTH, 8], F32)
        wba = sb("wba", [128, DEPTH, 8, 16], BF16)

        NA = [sb("NA%d" % i, [128, 8, 128], F32) for i in range(2)]
        NTA1 = sb("NTA1", [128, 8, 128], F32)
        xT = sb("xT", [128, 8, TM], F32)
        hT = sb("hT", [128, 8, TM], BF16)
        sqb = sb("sqb", [128, 16, TM], BF16)
        rstd = sb("rstd", [128, 4 * TM], F32)
        uT = sb("uT", [128, 8, TM], BF16)
        vf = sb("vf", [128, D], F32)
        vc = sb("vc", [128, D], F32)
        vnb = sb("vnb", [128, D], BF16)
        st4 = sb("st4", [128, 8], F32)
        qkvpre = sb("qkvpre", [128, 24, 3 + TM], BF16)
        halo = [sb("halo%d" % l, [128, 24, 3], BF16) for l in range(DEPTH)]
        cvf = sb("cvf", [128, 24, 3], F32)
        qkf = sb("qkf", [128, 16, TM], F32)
        qkT = sb("qkT", [128, 16, TM], BF16)
        vT = sb("vT", [128, 8, TM], BF16)
        zsT = sb("zsT", [128, 8, TM], BF16)
        sga = sb("sga", [128, 8, TM], BF16)
        sgb = sb("sgb", [128, 8, TM], BF16)
        oT = sb("oT", [128, 8, TM], F32)
        zbT = sb("zbT", [128, 8, TM], BF16)
        mixT = sb("mixT", [128, 8, TM], BF16)
        mtf = sb("mtf", [128, 4, TM], F32)
        hid = [sb("hid%d" % i, [128, 4, TM], BF16) for i in range(2)]
        rl = sb("rl", [128, 4, TM], BF16)
        S = [sb("S%d" % l, [128, 8, 128], F32) for l in range(DEPTH)]
        Sb = [sb("Sb%d" % l, [128, 8, 128], BF16) for l in range(DEPTH)]
        ba = sb("ba", [128, 16], F32)
        beta = sb("beta", [128, 8], F32)
        gx = sb("gx", [128, 8], F32)
        gm = sb("gm", [128, 8], F32)
        gn = sb("gn", [128, 8], F32)
        gg = sb("gg", [128, 8], F32)
        egc = sb("egc", [128, 16], F32)
        nbe = sb("nbe", [128, 8], F32)
        B1 = sb("B1", [128, 8, 128], F32)
        B2 = sb("B2", [128, 8, 128], F32)
        egam = sb("egam", [128, 8, 128], F32)
        B4 = sb("B4", [128, 8, 128], F32)
        PT = sb("PT", [128, 8, 128], BF16)
        Yb = sb("Yb", [128, 8, 128], BF16)
        bv = sb("bv", [128, 8, 128], BF16)
        kg = sb("kg", [128, 8, 128], BF16)
        qg = sb("qg", [128, 8, 128], BF16)
        rb = sb("rb", [128, 8, 128], BF16)
        vnw = sb("vnw", [128, 8, 128], BF16)
        NWB = 4
        wbuf = [sb("wbuf%d" % i, [128, 8, 512], BF16) for i in range(NWB)]

        def cdma(dst, src, key):
            P.dma("sp", dst, src, R=[], W=[key], semkey="dma_c_" + key)

        cdma(ln1c[:], ln1_d, "ln1c"); cdma(ln2c[:], ln2_d, "ln2c"); cdma(fnc[:], fn_d, "fnc")
        cdma(cwc[:], cw_d, "cwc"); cdma(onc[:], on_d, "onc"); cdma(algb[:], alg_d, "algb")
        cdma(albb[:], alb_d, "albb")
        cdma(alogb[:], alog_d, "alogb"); cdma(dtbb[:], dtb_d, "dtbb")

        def cast_w(dst, src, rows, key):
            for l in range(DEPTH):
                for r0 in range(0, rows, 128):
                    P.dma("pool", dst[l, r0:r0 + 128, :], src[l, r0:r0 + 128, :], R=[], W=[key], semkey="dma_" + key)

        cast_w(w_in_b, w_in_d, D, "w_in_b")
        cast_w(p_a_b, p_a_d, D, "p_a_b")
        cast_w(p_b_b, p_b_d, D, "p_b_b")
        cast_w(w_o_b, w_o_d, D, "w_o_b")
        cast_w(w_up_b, w_up_d, D, "w_up_b")
        cast_w(w_dn_b, w_dn_d, DFF, "w_dn_b")

        def gp(fn, *a, R=(), W=(), **kw):
            return P.op("pool", fn, *a, R=R, W=W, **kw)

        gp("memset", onesf[:], 1.0, W=["onesf"])
        gp("memset", nonesf[:], -1.0, W=["nonesf"])
        gp("memset", onesb[:], 1.0, W=["onesb"])
        gp("memset", epsc[:], EPS, W=["epsc"])
        gp("memset", epsq[:], EPS * 128.0, W=["epsq"])
        gp("affine_select", out=identf[:], in_=onesf[:], pattern=[[1, 128]], compare_op=ALU.is_equal, fill=0.0,
           base=0, channel_multiplier=-1, R=["onesf"], W=["identf"])
        gp("affine_select", out=trif[:], in_=onesf[:], pattern=[[1, 128]], compare_op=ALU.is_ge, fill=0.0,
           base=0, channel_multiplier=-1, R=["onesf"], W=["trif"])
        gp("affine_select", out=ntrif[:], in_=nonesf[:], pattern=[[1, 128]], compare_op=ALU.is_ge, fill=0.0,
           base=0, channel_multiplier=-1, R=["nonesf"], W=["ntrif"])
        gp("affine_select", out=trigt[:], in_=onesf[:], pattern=[[-1, 128]], compare_op=ALU.is_gt, fill=0.0,
           base=0, channel_multiplier=1, R=["onesf"], W=["trigt"])
        P.op("dve", "tensor_copy", out=identb[:], in_=identf[:], R=["identf"], W=["identb"])
        P.op("dve", "tensor_tensor", out=offd[:], in0=onesf[:], in1=identf[:], op=ALU.subtract, R=["onesf", "identf"], W=["offd"])
        mt0, mt1, mt2 = B1[:, 0, :], B2[:, 0, :], B4[:, 0, :]
        gp("memset", mt0, NEG, W=["B1"])
        gp("affine_select", out=mt1, in_=mt0, pattern=[[-1, 128]], compare_op=ALU.is_gt, fill=0.0,
           base=0, channel_multiplier=1, R=["B1"], W=["B2"])
        gp("affine_select", out=mt2, in_=mt0, pattern=[[1, 128]], compare_op=ALU.is_ge, fill=0.0,
           base=0, channel_multiplier=-1, R=["B1"], W=["B4"])
        for h in range(8):
            P.op("dve", "tensor_copy", out=maskb[:, h, :], in_=mt1, R=["B2"], W=["maskb"])
            P.op("dve", "tensor_copy", out=maskc[:, h, :], in_=mt2, R=["B4"], W=["maskc"])
        for l in range(DEPTH):
            P.dma("sp", egam[:], wst_d[:, l, :, :], R=[], W=["egam"], semkey="dma_c2")
            for g in range(8):
                P.op("dve", "tensor_tensor", out=wstb[:, l, g, :], in0=egam[:, g, :], in1=trif[:], op=ALU.mult,
                     R=["egam", "trif"], W=["wstb"])
            P.dma("sp", vf[0:1, :], bs_d[0:1, l, :], R=[], W=["vf"], semkey="dma_c3")
            P.op("dve", "tensor_copy", out=bsb[0:1, l, :], in_=vf[0:1, :], R=["vf"], W=["bsb"])
        P.op("act", "activation", out=nexpA[:], in_=alogb[:], func=AF.Exp, R=["alogb"], W=["nexpA"])
        P.op("dve", "tensor_scalar", out=nexpA[:], in0=nexpA[:], scalar1=-1.0, scalar2=None, op0=ALU.mult,
             R=["nexpA"], W=["nexpA"])
        for l in range(DEPTH):
            P.dma("sp", wba[:, l, :, :], w_in_b[l, :, 6144:6160].rearrange("(kc p) n -> p kc n", p=128),
                  R=["w_in_b"], W=["wba"], semkey="dma_c_wba")

        P.stage("setup")
        pd = [es.enter_context(nc.psum_tensor("pd%d" % i, [128, 1024], F32)) for i in range(4)]
        ps_state = {"i": 0}

        def ps_half():
            i = ps_state["i"] % 6
            ps_state["i"] += 1
            t = pd[i // 2]
            return t[:, (i % 2) * 512:(i % 2) * 512 + 512], "pd%d%s" % (i // 2, "ab"[i % 2])

        def ps_pair():
            if ps_state["i"] % 2:
                ps_state["i"] += 1
            i = ps_state["i"] % 6
            ps_state["i"] += 2
            return pd[i // 2][:], ["pd%da" % (i // 2), "pd%db" % (i // 2)]

        wplan = []
        wstate = {"issued": 0, "consumed": 0}

        def layer_plan(l):
            pl = []
            wi = w_in_b[l].rearrange("(kc p) n -> p kc n", p=128)
            for g in range(2):
                pl.append(("u%d" % g, wi[:, :, g * 512:(g + 1) * 512], "w_in_b"))
            for g in range(2):
                pl.append(("v%d" % g, wi[:, :, 1024 + g * 512:1024 + (g + 1) * 512], "w_in_b"))
            for g in range(6):
                pl.append(("qkv%d" % g, wi[:, :, 2048 + g * 512:2048 + (g + 1) * 512], "w_in_b"))
            for g in range(2):
                pl.append(("z%d" % g, wi[:, :, 5120 + g * 512:5120 + (g + 1) * 512], "w_in_b"))
            for g in range(2):
                pl.append(("ga%d" % g, wi[:, :, 6160 + g * 512:6160 + (g + 1) * 512], "w_in_b"))
                pl.append(("gb%d" % g, wi[:, :, 7184 + g * 512:7184 + (g + 1) * 512], "w_in_b"))
                pl.append(("pa%d" % g, p_a_b[l].rearrange("(kc p) n -> p kc n", p=128)[:, :, g * 512:(g + 1) * 512], "p_a_b"))
                pl.append(("pb%d" % g, p_b_b[l].rearrange("(kc p) n -> p kc n", p=128)[:, :, g * 512:(g + 1) * 512], "p_b_b"))
            for g in range(2):
                pl.append(("wo%d" % g, w_o_b[l].rearrange("(kc p) n -> p kc n", p=128)[:, :, g * 512:(g + 1) * 512], "w_o_b"))
            for g in range(8):
                pl.append(("up%d" % g, w_up_b[l].rearrange("(kc p) n -> p kc n", p=128)[:, :, g * 512:(g + 1) * 512], "w_up_b"))
                pl.append(("dn%d" % g, w_dn_b[l, g * 512:(g + 1) * 512, :].rearrange("(kc p) n -> p kc n", p=128), "w_dn_b"))
            return pl

        def w_issue_upto(n):
            while wstate["issued"] < min(n, len(wplan)):
                i = wstate["issued"]
                tag, view, srckey = wplan[i]
                slot = i % NWB
                if tag.startswith("dn"):
                    dst = wbuf[slot][:].rearrange("p a b -> p (a b)").rearrange("p (kc n) -> p kc n", kc=4)
                else:
                    dst = wbuf[slot][:]
                P.dma("sp", dst, view, R=[srckey], W=["wbuf%d" % slot], semkey="dma_wbuf%d" % slot)
                wstate["issued"] += 1

        def w_get(tag):
            i = wstate["consumed"]
            assert wplan[i][0] == tag, (wplan[i][0], tag)
            w_issue_upto(i + 1)
            wstate["consumed"] += 1
            slot = i % NWB
            return wbuf[slot], "wbuf%d" % slot, i

        def w_done(i):
            w_issue_upto(i + NWB)

        def mm(out, lhsT, rhs, start, stop, R, W, inc=None):
            if inc is None:
                inc = stop
            return P.op("pe", "matmul", out, lhsT=lhsT, rhs=rhs, start=start, stop=stop, R=R, W=W, inc=inc)

        def tr(out, in_, ident, R, W):
            return P.op("pe", "transpose", out=out, in_=in_, identity=ident, R=R, W=W)

        def act(out, in_, func, R, W, **kw):
            return P.op("act", "activation", out=out, in_=in_, func=func, R=R, W=W, **kw)

        def tt(e, out, in0, in1, op, R, W):
            return P.op(e, "tensor_tensor", out=out, in0=in0, in1=in1, op=op, R=R, W=W)

        def ts(e, out, in0, s1, op0, R, W, s2=None, op1=None):
            if op1 is None:
                return P.op(e, "tensor_scalar", out=out, in0=in0, scalar1=s1, scalar2=None, op0=op0, R=R, W=W)
            return P.op(e, "tensor_scalar", out=out, in0=in0, scalar1=s1, scalar2=s2, op0=op0, op1=op1, R=R, W=W)

        def stt(e, out, in0, scalar, in1, op0, op1, R, W):
            return P.op(e, "scalar_tensor_tensor", out=out, in0=in0, scalar=scalar, in1=in1, op0=op0, op1=op1,
                        R=R, W=W)

        def cp(e, out, in_, R, W):
            if e == "act":
                return act(out, in_, AF.Identity, R, W)
            return P.op(e, "tensor_copy", out=out, in_=in_, R=R, W=W)

        def rmsnorm_fm(T, gcol, gkey, dst, dstkey):
            tt("dve", sqb[:, 0:8, :T], xT[:, :, :T], xT[:, :, :T], ALU.mult, R=["xT"], W=["sqb"])
            ph, pk = ps_half()
            for kc in range(8):
                mm(ph[:, :T], onesb[:], sqb[:, kc, :T], kc == 0, kc == 7, R=["onesb", "sqb"], W=[pk])
            act(rstd[:, :T], ph[:, :T], AF.Ln, R=[pk, "epsc"], W=["rstd"], bias=epsc[:], scale=1.0 / D)
            act(rstd[:, :T], rstd[:, :T], AF.Exp, R=["rstd"], W=["rstd"], scale=-0.5)
            for kc in range(8):
                stt("dve", dst[:, kc, :T], xT[:, kc, :T], gcol[:, kc:kc + 1], rstd[:, :T], ALU.mult, ALU.mult,
                    R=["xT", "rstd", gkey], W=[dstkey])

        def fm_proj(T, wt, wkey, src, srckey, nblk, evac):
            bpb = 512 // T if T >= 128 else 4
            bpb = min(bpb, nblk)
            for m0 in range(0, nblk, bpb):
                ph, pk = ps_half()
                nb = min(bpb, nblk - m0)
                pv = ph[:, :nb * T].rearrange("p (b t) -> p b t", b=nb)
                for b in range(nb):
                    for kc in range(8):
                        mm(pv[:, b, :], wt[:, kc, (m0 + b) * 128:(m0 + b + 1) * 128], src[:, kc, :T], kc == 0, kc == 7,
                           R=[wkey, srckey], W=[pk])
                evac(m0, nb, pv, pk)

        def layer(l, T, C, seq, first_tile, last_tile):
            NCH = T // C
            rmsnorm_fm(T, ln1c[:, l, :], "ln1c", hT, "hT")
            prep_gen = delta_prep(l, T, C, 0) if NCH == 1 else None
            if prep_gen is not None:
                next(prep_gen)
            P.stage("norm1_%d" % l)
            for g in range(2):
                wt, wk, wi_ = w_get("u%d" % g)

                def ev(m0, nb, pv, pk, g=g):
                    act(uT[:, g * 4 + m0:g * 4 + m0 + nb, :T], pv, AF.Gelu, R=[pk], W=["uT"])
                fm_proj(T, wt, wk, hT, "hT", 4, ev)
                w_done(wi_)
            if prep_gen is not None:
                next(prep_gen)
            P.stage("u_%d" % l)
            wv = [w_get("v%d" % g) for g in range(2)]
            for c in range(NCH):
                c0 = c * C
                for g in range(2):
                    wt, wk, _ = wv[g]
                    ph, pk = ps_half()
                    for kc in range(8):
                        mm(ph[:C, :], hT[:, kc, c0:c0 + C], wt[:, kc, :], kc == 0, kc == 7, R=["hT", wk], W=[pk])
                    act(vf[:C, g * 512:(g + 1) * 512], ph[:C, :], AF.Gelu, R=[pk], W=["vf"])
                P.op("dve", "reduce_sum", out=st4[:C, 0:1], in_=vf[:C, :], axis=AX.X, R=["vf"], W=["st4"])
                ts("dve", st4[:C, 1:2], st4[:C, 0:1], -1.0 / D, ALU.mult, R=["st4"], W=["st4"])
                ts("dve", vc[:C, :], vf[:C, :], st4[:C, 1:2], ALU.add, R=["vf", "st4"], W=["vc"])
                tt("dve", vf[:C, :], vc[:C, :], vc[:C, :], ALU.mult, R=["vc"], W=["vf"])
                P.op("dve", "reduce_sum", out=st4[:C, 2:3], in_=vf[:C, :], axis=AX.X, R=["vf"], W=["st4"])
                act(st4[:C, 3:4], st4[:C, 2:3], AF.Ln, R=["st4", "epsc"], W=["st4"], bias=epsc[:C, :], scale=1.0 / D)
                act(st4[:C, 4:5], st4[:C, 3:4], AF.Exp, R=["st4"], W=["st4"], scale=-0.5)
                stt("dve", vc[:C, :], vc[:C, :], st4[:C, 4:5], algb[:C, l, :], ALU.mult, ALU.mult,
                    R=["vc", "st4", "algb"], W=["vc"])
                tt("dve", vc[:C, :], vc[:C, :], albb[:C, l, :], ALU.add, R=["vc", "albb"], W=["vc"])
                if seq["kind"] == "s":
                    P.dma("sp", ngv_s[l, 0, :, :], vc[:C, :], R=["vc"], W=[], semkey="dma_ngv")
                cp("act", vnb[:C, :], vc[:C, :], R=["vc"], W=["vnb"])
                pp, pks = ps_pair()
                ppv = pp.rearrange("p (g t) -> p g t", g=8)
                for g in range(8):
                    k = pks[g // 4]
                    mm(ppv[:, g, :C], vnb[:C, g * 128:(g + 1) * 128], wstb[:C, l, g, :C], True, False,
                       R=["vnb", "wstb"], W=[k])
                    mm(ppv[:, g, :C], onesb[0:1, :], bsb[0:1, l, g * 128:g * 128 + C], False, True,
                       R=["onesb", "bsb"], W=[k])
                tt("dve", uT[:, :, c0:c0 + C], ppv[:, :, :C], uT[:, :, c0:c0 + C], ALU.mult, R=pks + ["uT"], W=["uT"])
            for g in range(2):
                w_done(wv[g][2])
            if prep_gen is not None:
                for _ in prep_gen:
                    pass
            P.stage("gmlp_%d" % l)
            qp = qkvpre
            qk_ = "qkvpre"
            cp("pool", qp[:, :, 0:3], halo[l][:], R=["halo%d" % l], W=[qk_])
            for g in range(6):
                wt, wk, wi_ = w_get("qkv%d" % g)

                def ev(m0, nb, pv, pk, g=g):
                    cp("act", qp[:, g * 4 + m0:g * 4 + m0 + nb, 3:3 + T], pv, R=[pk], W=[qk_])
                    import os as _os
                    if last_tile and not _os.environ.get("K_NOCVF"):
                        cp("dve", cvf[:, g * 4 + m0:g * 4 + m0 + nb, :], pv[:, :, T - 3:T], R=[pk], W=["cvf"])
                fm_proj(T, wt, wk, hT, "hT", 4, ev)
                w_done(wi_)
            P.stage("qkv_%d" % l)
            if last_tile:
                dst = (ncv_p[l, seq["idx"]] if seq["kind"] == "p" else ncv_s[l, 0])
                P.dma("sp", dst, cvf[:], R=["cvf"], W=[], semkey="dma_cvf")
            P.stage("cvfdma_%d" % l)
            cp("pool", halo[l][:], qp[:, :, T:T + 3], R=[qk_], W=["halo%d" % l])
            P.stage("halo_%d" % l)
            tmpv = vf[:, :8 * T].rearrange("p (b t) -> p b t", b=8)
            vacc = vc[:, :8 * T].rearrange("p (b t) -> p b t", b=8)
            for blk in range(16):
                for j in range(CW):
                    wcol = cwc[:, l, blk, j:j + 1]
                    if j == 0:
                        ts("dve", qkf[:, blk, :T], qp[:, blk, 0:T], wcol, ALU.mult, R=[qk_, "cwc"], W=["qkf"])
                    else:
                        stt("dve", qkf[:, blk, :T], qp[:, blk, j:j + T], wcol, qkf[:, blk, :T], ALU.mult, ALU.add,
                            R=[qk_, "cwc", "qkf"], W=["qkf"])
            for j in range(CW):
                wj = cwc[:, l, 16:24, j:j + 1].to_broadcast([128, 8, T])
                src = qp[:, 16:24, j:j + T]
                if j == 0:
                    tt("pool", vacc, src, wj, ALU.mult, R=[qk_, "cwc"], W=["vc"])
                else:
                    tt("pool", tmpv, src, wj, ALU.mult, R=[qk_, "cwc"], W=["vf"])
                    tt("pool", vacc, vacc, tmpv, ALU.add, R=["vc", "vf"], W=["vc"])
            act(qkf[:, :, :T], qkf[:, :, :T], AF.Silu, R=["qkf"], W=["qkf"])
            act(vT[:, :, :T], vacc, AF.Silu, R=["vc"], W=["vT"])
            P.stage("silu_%d" % l)
            tt("dve", sqb[:, :, :T], qkf[:, :, :T], qkf[:, :, :T], ALU.mult, R=["qkf"], W=["sqb"])
            hpg = 4
            for hg in range(0, 16, hpg):
                ph, pk = ps_half()
                pv = ph[:, :hpg * T].rearrange("p (b t) -> p b t", b=hpg)
                for b in range(hpg):
                    mm(pv[:, b, :], onesb[:], sqb[:, hg + b, :T], True, True, R=["onesb", "sqb"], W=[pk])
                rv = rstd[:, :hpg * T].rearrange("p (b t) -> p b t", b=hpg)
                if hg < 8:
                    act(rv, pv, AF.Ln, R=[pk, "epsq"], W=["rstd"], bias=epsq[:], scale=128.0)
                else:
                    act(rv, pv, AF.Ln, R=[pk, "epsc"], W=["rstd"], bias=epsc[:], scale=1.0)
                act(rv, rv, AF.Exp, R=["rstd"], W=["rstd"], scale=-0.5)
                tt("dve", qkT[:, hg:hg + hpg, :T], qkf[:, hg:hg + hpg, :T], rv, ALU.mult, R=["qkf", "rstd"], W=["qkT"])
            P.stage("conv_%d" % l)
            def zfill(g):
                wt, wk, wi_ = w_get("z%d" % g)

                def ev(m0, nb, pv, pk, g=g):
                    act(zsT[:, g * 4 + m0:g * 4 + m0 + nb, :T], pv, AF.Silu, R=[pk], W=["zsT"])
                fm_proj(T, wt, wk, hT, "hT", 4, ev)
                w_done(wi_)
            for c in range(NCH):
                delta_chunk(l, T, C, c * C, hoisted=(NCH == 1), fillers=([zfill] if c == NCH - 1 else []))

            P.stage("delta_%d" % l)
            tt("dve", sqb[:, 0:8, :T], oT[:, :, :T], oT[:, :, :T], ALU.mult, R=["oT"], W=["sqb"])
            for hg in range(0, 8, hpg):
                ph, pk = ps_half()
                pv = ph[:, :hpg * T].rearrange("p (b t) -> p b t", b=hpg)
                for b in range(hpg):
                    mm(pv[:, b, :], onesb[:], sqb[:, hg + b, :T], True, True, R=["onesb", "sqb"], W=[pk])
                rv = rstd[:, :hpg * T].rearrange("p (b t) -> p b t", b=hpg)
                act(rv, pv, AF.Ln, R=[pk, "epsc"], W=["rstd"], bias=epsc[:], scale=1.0 / 128.0)
                act(rv, rv, AF.Exp, R=["rstd"], W=["rstd"], scale=-0.5)
                tt("dve", oT[:, hg:hg + hpg, :T], oT[:, hg:hg + hpg, :T], rv, ALU.mult, R=["oT", "rstd"], W=["oT"])
            stt("dve", zbT[:, :, :T], oT[:, :, :T], onc[:, l:l + 1], zsT[:, :, :T], ALU.mult, ALU.mult,
                R=["oT", "onc", "zsT"], W=["zbT"])
            P.stage("onorm_%d" % l)
            for g in range(2):
                wt, wk, wi_ = w_get("ga%d" % g)

                def ev(m0, nb, pv, pk, g=g):
                    act(sga[:, g * 4 + m0:g * 4 + m0 + nb, :T], pv, AF.Sigmoid, R=[pk], W=["sga"])
                fm_proj(T, wt, wk, hT, "hT", 4, ev)
                w_done(wi_)
                wt, wk, wi_ = w_get("gb%d" % g)

                def ev(m0, nb, pv, pk, g=g):
                    act(sgb[:, g * 4 + m0:g * 4 + m0 + nb, :T], pv, AF.Sigmoid, R=[pk], W=["sgb"])
                fm_proj(T, wt, wk, hT, "hT", 4, ev)
                w_done(wi_)
                wt, wk, wi_ = w_get("pa%d" % g)

                def ev(m0, nb, pv, pk, g=g):
                    tt("dve", mtf[:, m0:m0 + nb, :T], pv, sga[:, g * 4 + m0:g * 4 + m0 + nb, :T], ALU.mult,
                       R=[pk, "sga"], W=["mtf"])
                fm_proj(T, wt, wk, uT, "uT", 4, ev)
                w_done(wi_)
                wt, wk, wi_ = w_get("pb%d" % g)

                def ev(m0, nb, pv, pk, g=g):
                    tt("dve", mixT[:, g * 4 + m0:g * 4 + m0 + nb, :T], pv, sgb[:, g * 4 + m0:g * 4 + m0 + nb, :T], ALU.mult,
                       R=[pk, "sgb"], W=["mixT"])
                    tt("pool", mixT[:, g * 4 + m0:g * 4 + m0 + nb, :T], mixT[:, g * 4 + m0:g * 4 + m0 + nb, :T],
                       mtf[:, m0:m0 + nb, :T], ALU.add, R=["mixT", "mtf"], W=["mixT"])
                fm_proj(T, wt, wk, zbT, "zbT", 4, ev)
                w_done(wi_)
            for g in range(2):
                wt, wk, wi_ = w_get("wo%d" % g)

                def ev(m0, nb, pv, pk, g=g):
                    tt("dve", xT[:, g * 4 + m0:g * 4 + m0 + nb, :T], pv, xT[:, g * 4 + m0:g * 4 + m0 + nb, :T], ALU.add,
                       R=[pk, "xT"], W=["xT"])
                fm_proj(T, wt, wk, mixT, "mixT", 4, ev)
                w_done(wi_)
            P.stage("merge_%d" % l)
            rmsnorm_fm(T, ln2c[:, l, :], "ln2c", hT, "hT")
            acc = pd[3][:, :8 * T].rearrange("p (m t) -> p m t", m=8)
            for g in range(8):
                wt, wk, wi_ = w_get("up%d" % g)
                hb = hid[g % 2]
                hk = "hid%d" % (g % 2)

                def ev(m0, nb, pv, pk, hb=hb, hk=hk):
                    act(rl[:, m0:m0 + nb, :T], pv, AF.Relu, R=[pk], W=["rl"])
                    tt("pool", hb[:, m0:m0 + nb, :T], rl[:, m0:m0 + nb, :T], rl[:, m0:m0 + nb, :T], ALU.mult,
                       R=["rl"], W=[hk])
                fm_proj(T, wt, wk, hT, "hT", 4, ev)
                w_done(wi_)
                wt, wk, wi_ = w_get("dn%d" % g)
                wdv = wt[:].rearrange("p a b -> p (a b)").rearrange("p (kc n) -> p kc n", kc=4)
                for m in range(8):
                    for kc in range(4):
                        last = (m == 7 and kc == 3)
                        first_in_bank = (g == 0 and kc == 0 and (m * T) % 512 == 0)
                        P.op("pe", "matmul", acc[:, m, :], lhsT=wdv[:, kc, m * 128:(m + 1) * 128], rhs=hb[:, kc, :T],
                             start=first_in_bank, stop=(g == 7 and kc == 3), skip_group_check=True,
                             R=[wk, hk], W=["pd3a", "pd3b"], inc=(last or (g == 7 and kc == 3)))
                w_done(wi_)
            tt("dve", xT[:, :, :T], acc, xT[:, :, :T], ALU.add, R=["pd3a", "pd3b", "xT"], W=["xT"])
            P.stage("ffn_end_%d_%s" % (l, seq["kind"] + str(seq["idx"])))

        def delta_prep(l, T, C, c0):
            hp = min(8, 512 // C)
            nhalf = 8 // hp
            gtri, DTi = B1, B1
            ngb, Ds, Atmp = B2, B2, B2

            def w8(ps_ap, np_):
                return ps_ap[:np_, :8 * C].rearrange("p (h x) -> p h x", h=8)
            ph, pk = ps_half()
            for kc in range(8):
                mm(ph[:C, :16], hT[:, kc, c0:c0 + C], wba[:, l, kc, :], kc == 0, kc == 7, R=["hT", "wba"], W=[pk])
            cp("dve", ba[:C, :], ph[:C, :16], R=[pk], W=["ba"])
            act(beta[:C, :], ba[:C, 0:8], AF.Sigmoid, R=["ba"], W=["beta"])
            tt("dve", gx[:C, :], ba[:C, 8:16], dtbb[:C, l, :], ALU.add, R=["ba", "dtbb"], W=["gx"])
            ts("dve", gm[:C, :], gx[:C, :], 0.0, ALU.max, R=["gx"], W=["gm"])
            stt("dve", gn[:C, :], gm[:C, :], -2.0, gx[:C, :], ALU.mult, ALU.add, R=["gm", "gx"], W=["gn"])
            act(gn[:C, :], gn[:C, :], AF.Exp, R=["gn"], W=["gn"])
            act(gn[:C, :], gn[:C, :], AF.Ln, R=["gn"], W=["gn"], bias=1.0, scale=1.0)
            tt("dve", gn[:C, :], gn[:C, :], gm[:C, :], ALU.add, R=["gn", "gm"], W=["gn"])
            tt("dve", gg[:C, :], gn[:C, :], nexpA[:C, l, :], ALU.mult, R=["gn", "nexpA"], W=["gg"])
            yield
            P.stage("d_g_%d" % l)
            ph, pk = ps_half()
            mm(ph[:C, 0:8], trif[:C, :C], gg[:C, :], True, True, R=["trif", "gg"], W=[pk], inc=False)
            mm(ph[:C, 8:16], trigt[:C, :C], gg[:C, :], True, True, R=["trigt", "gg"], W=[pk])
            act(egc[:C, :], ph[:C, 0:16], AF.Exp, R=[pk], W=["egc"])
            stt("dve", nbe[:C, :], beta[:C, :], -1.0, egc[:C, 0:8], ALU.mult, ALU.mult, R=["beta", "egc"], W=["nbe"])
            P.stage("d_col_%d" % l)
            tt("pool", gtri[:C, :, :C], trif[:C, :C].unsqueeze(1).to_broadcast([C, 8, C]),
               gg[:C, :].unsqueeze(2).to_broadcast([C, 8, C]), ALU.mult, R=["trif", "gg"], W=["B1"])
            ts("dve", ngb[:C, :, :C], gg[:C, :].unsqueeze(2).to_broadcast([C, 8, C]), -1.0, ALU.mult,
               R=["gg"], W=["B2"])
            yield
            P.stage("d_rhs_%d" % l)
            pa, pak = ps_pair()
            pb, pbk = ps_pair()
            pc, pck = ps_pair()

            pav, pbv, pcv = w8(pa, 128), w8(pb, C), w8(pc, C)
            for hh in range(nhalf):
                hs = slice(hh * hp, (hh + 1) * hp)
                ka, kb_, kc_ = pak[hh if nhalf > 1 else 0], pbk[hh if nhalf > 1 else 0], pck[hh if nhalf > 1 else 0]
                mm(pav[:, hs, :], onesf[:C, :], gtri[:C, hs, :C], True, True, R=["onesf", "B1"], W=[ka])
                mm(pbv[:, hs, :], onesf[:C, :C], gtri[:C, hs, :C], True, False, R=["onesf", "B1"], W=[kb_])
                mm(pbv[:, hs, :], trif[:C, :C], ngb[:C, hs, :C], False, False, R=["trif", "B2"], W=[kb_])
                mm(pbv[:, hs, :], identb[:C, :C], maskb[:C, hs, :C], False, True, R=["identb", "maskb"], W=[kb_])
                mm(pcv[:, hs, :], nonesf[:C, :C], gtri[:C, hs, :C], True, False, R=["nonesf", "B1"], W=[kc_])
                mm(pcv[:, hs, :], ntrif[:C, :C], ngb[:C, hs, :C], False, False, R=["ntrif", "B2"], W=[kc_])
                mm(pcv[:, hs, :], identb[:C, :C], maskc[:C, hs, :C], False, True, R=["identb", "maskc"], W=[kc_])
            act(egam[:, :, :C], pav, AF.Exp, R=pak, W=["egam"])
            act(DTi[:C, :, :C], pbv, AF.Exp, R=pbk, W=["B1"])
            act(Ds[:C, :, :C], pcv, AF.Exp, R=pck, W=["B2"])
            tt("pool", NA[1][:C, :, :C], identf[:C, :C].unsqueeze(1).to_broadcast([C, 8, C]),
               beta[:C, :].unsqueeze(2).to_broadcast([C, 8, C]), ALU.mult, R=["identf", "beta"], W=["NA1"])
            pbb, pbbk = ps_pair()
            pbbv = w8(pbb, C)
            for hh in range(nhalf):
                hs = slice(hh * hp, (hh + 1) * hp)
                mm(pbbv[:, hs, :], onesf[:C, :C], NA[1][:C, hs, :C], True, True, R=["onesf", "NA1"],
                   W=[pbbk[hh if nhalf > 1 else 0]])
            cp("act", NA[1][:C, :, :C], pbbv, R=pbbk, W=["NA1"])
            yield

        def delta_chunk(l, T, C, c0, hoisted=False, fillers=()):
            hp = min(8, 512 // C)
            nhalf = 8 // hp

            def wide(ps, C2):
                return ps[:, :].rearrange("p (h x) -> p h x", h=8) if False else None

            qv = qkT[:, 0:8, c0:c0 + C]
            kv = qkT[:, 8:16, c0:c0 + C]
            gtri, DTi = B1, B1
            ngb, Ds, Atmp = B2, B2, B2
            Yf, rtmp = B4, B4

            def w8(ps_ap, np_):
                return ps_ap[:np_, :8 * C].rearrange("p (h x) -> p h x", h=8)
            if not hoisted:
                for _ in delta_prep(l, T, C, c0):
                    pass
            P.stage("d_exp_%d" % l)
            pkk, pkkk = ps_pair()
            pqk, pqkk = ps_pair()
            pkkv, pqkv = w8(pkk, C), w8(pqk, C)
            for h in range(8):
                k_ = pkkk[(h // hp) if nhalf > 1 else 0]
                mm(pkkv[:, h, :], kv[:, h, :], kv[:, h, :], True, True, R=["qkT"], W=[k_], inc=(h % hp == hp - 1))
            for h in range(8):
                k_ = pqkk[(h // hp) if nhalf > 1 else 0]
                mm(pqkv[:, h, :], kv[:, h, :], qv[:, h, :], True, True, R=["qkT"], W=[k_], inc=(h % hp == hp - 1))
            tt("dve", PT[:C, :, :C], pqkv, DTi[:C, :, :C], ALU.mult, R=pqkk + ["B1"], W=["PT"])
            tt("dve", Atmp[:C, :, :C], pkkv, Ds[:C, :, :C], ALU.mult, R=pkkk + ["B2"], W=["B2"])
            P.stage("d_kk_%d" % l)
            NTA = [B2, NTA1]
            NTk = ["B2", "NTA1"]
            NAk = ["NA0", "NA1"]
            tt("dve", B2[:C, :, :C], B2[:C, :, :C], beta[:C, :].unsqueeze(2).to_broadcast([C, 8, C]), ALU.mult,
               R=["B2", "beta"], W=["B2"])
            P.stage("d_A_%d" % l)
            tt("dve", NA[0][:C, :, :C], pkkv, DTi[:C, :, :C], ALU.mult, R=pkkk + ["B1"], W=["NA0"])
            tt("dve", NA[0][:C, :, :C], NA[0][:C, :, :C], NA[1][:C, :, :C], ALU.mult, R=["NA0", "NA1"], W=["NA0"])
            tt("dve", NA[0][:C, :, :C], NA[0][:C, :, :C], offd[:C, :C].unsqueeze(1).to_broadcast([C, 8, C]), ALU.mult,
               R=["NA0", "offd"], W=["NA0"])
            P.stage("d_tr_%d" % l)
            tt("dve", Yf[:C, :, :C], identf[:C, :C].unsqueeze(1).to_broadcast([C, 8, C]), NA[0][:C, :, :C], ALU.subtract,
               R=["identf", "NA0"], W=["B4"])
            nlev = max(1, int(np.ceil(np.log2(C))) - 1)
            cur = 0
            for lev in range(nlev):
                nxt = 1 - cur
                lastlev = (lev == nlev - 1)
                Nc, NTc = NA[cur], NTA[cur]
                Nck, NTck = NAk[cur], NTk[cur]
                p2t, p2tk = ps_pair()
                p2tv = w8(p2t, C)
                for h in range(8):
                    k_ = p2tk[(h // hp) if nhalf > 1 else 0]
                    mm(p2tv[:, h, :], Nc[:C, h, :C], NTc[:C, h, :C], True, True, R=[Nck, NTck], W=[k_],
                       inc=(h % hp == hp - 1))
                if not lastlev:
                    p2, p2k = ps_pair()
                    p2v = w8(p2, C)
                    for h in range(8):
                        k_ = p2k[(h // hp) if nhalf > 1 else 0]
                        mm(p2v[:, h, :], NTc[:C, h, :C], Nc[:C, h, :C], True, True, R=[Nck, NTck], W=[k_],
                           inc=(h % hp == hp - 1))
                cp("act", NTA[nxt][:C, :, :C], p2tv, R=p2tk, W=[NTk[nxt]])
                if not lastlev:
                    cp("dve", NA[nxt][:C, :, :C], p2v, R=p2k, W=[NAk[nxt]])
                py, pyk = ps_pair()
                pyv = w8(py, C)
                for h in range(8):
                    k_ = pyk[(h // hp) if nhalf > 1 else 0]
                    mm(pyv[:, h, :], NTA[nxt][:C, h, :C], Yf[:C, h, :C], True, True, R=[NTk[nxt], "B4"], W=[k_],
                       inc=(h % hp == hp - 1))
                tt("dve", Yf[:C, :, :C], Yf[:C, :, :C], pyv, ALU.add, R=["B4"] + pyk, W=["B4"])
                cur = nxt
            cp("pool", Yb[:C, :, :C], Yf[:C, :, :C], R=["B4"], W=["Yb"])
            P.stage("d_neu_%d" % l)
            pt_, ptk = ps_half()
            ptv = pt_.bitcast(BF16)[:C, :1024].rearrange("p (h x) -> p h x", h=8)
            for h in range(8):
                tr(ptv[:, h, :], vT[:, h, c0:c0 + C], identb[:], R=["vT", "identb"], W=[ptk])
            tt("dve", bv[:C, :, :], ptv, beta[:C, :].unsqueeze(2).to_broadcast([C, 8, 128]), ALU.mult,
               R=[ptk, "beta"], W=["bv"])
            pt2, pt2k = ps_half()
            pt2v = pt2.bitcast(BF16)[:C, :1024].rearrange("p (h x) -> p h x", h=8)
            for h in range(8):
                tr(pt2v[:, h, :], kv[:, h, :], identb[:], R=["qkT", "identb"], W=[pt2k])
            tt("dve", kg[:C, :, :], pt2v, egc[:C, 8:16].unsqueeze(2).to_broadcast([C, 8, 128]), ALU.mult,
               R=[pt2k, "egc"], W=["kg"])
            tt("pool", qg[:, :, :C], qv, egam[:, :, :C], ALU.mult, R=["qkT", "egam"], W=["qg"])
            P.stage("d_tok_%d" % l)
            Sl, Sbl = S[l], Sb[l]
            Sk, Sbk = "S%d" % l, "Sb%d" % l
            pks_, pksk = ps_pair()
            pksv = pks_[:C, :].rearrange("p (h x) -> p h x", h=8)
            for h in range(8):
                mm(pksv[:, h, :], kv[:, h, :], Sbl[:, h, :], True, True, R=["qkT", Sbk], W=[pksk[h // 4]],
                   inc=(h % 4 == 3))
            for f_ in fillers:
                f_(0)
            tt("dve", rtmp[:C, :, :], pksv, nbe[:C, :].unsqueeze(2).to_broadcast([C, 8, 128]), ALU.mult,
               R=pksk + ["nbe"], W=["B4"])
            tt("dve", rb[:C, :, :], rtmp[:C, :, :], bv[:C, :, :], ALU.add, R=["B4", "bv"], W=["rb"])
            pvn, pvnk = ps_pair()
            pvnv = pvn[:C, :].rearrange("p (h x) -> p h x", h=8)
            for h in range(8):
                mm(pvnv[:, h, :], Yb[:C, h, :C], rb[:C, h, :], True, True, R=["Yb", "rb"], W=[pvnk[h // 4]],
                   inc=(h % 4 == 3))
            for f_ in fillers:
                f_(1)
            cp("act", vnw[:C, :, :], pvnv, R=pvnk, W=["vnw"])
            po, pok = ps_pair()
            pov = w8(po, 128)
            for h in range(8):
                k_ = pok[(h // hp) if nhalf > 1 else 0]
                mm(pov[:, h, :], Sbl[:, h, :], qg[:, h, :C], True, False, R=[Sbk, "qg"], W=[k_])
                mm(pov[:, h, :], vnw[:C, h, :], PT[:C, h, :C], False, True, R=["vnw", "PT"], W=[k_],
                   inc=(h % hp == hp - 1))
            cp("act", oT[:, :, c0:c0 + C], pov, R=pok, W=["oT"])
            psn, psnk = ps_pair()
            psnv = psn[:, :].rearrange("p (h x) -> p h x", h=8)
            for h in range(8):
                mm(psnv[:, h, :], kg[:C, h, :], vnw[:C, h, :], True, True, R=["kg", "vnw"], W=[psnk[h // 4]],
                   inc=(h % 4 == 3))
            tt("dve", Sl[:], Sl[:], egam[:, :, C - 1:C].to_broadcast([128, 8, 128]), ALU.mult, R=[Sk, "egam"], W=[Sk])
            tt("dve", Sl[:], Sl[:], psnv, ALU.add, R=[Sk] + psnk, W=[Sk])
            cp("act", Sbl[:], Sl[:], R=[Sk], W=[Sbk])

        seqs = [
            {"kind": "p", "idx": 0, "L": SEQ, "T": 128, "C": 128},
            {"kind": "p", "idx": 1, "L": SEQ, "T": 128, "C": 128},
            {"kind": "s", "idx": 0, "L": DEC_SEQ, "T": DEC_SEQ, "C": DEC_SEQ},
        ]
        for sq in seqs:
            ntile = sq["L"] // sq["T"]
            for _ in range(ntile):
                for l in range(DEPTH):
                    wplan.extend(layer_plan(l))

        for sq in seqs:
            T, C = sq["T"], sq["C"]
            ntile = sq["L"] // T
            for l in range(DEPTH):
                if sq["kind"] == "p":
                    P.op("pool", "memset", S[l][:], 0.0, W=["S%d" % l])
                    P.op("pool", "memset", Sb[l][:], 0.0, W=["Sb%d" % l])
                    P.op("pool", "memset", halo[l][:], 0.0, W=["halo%d" % l])
                else:
                    P.dma("sp", S[l][:], sdelta[l].rearrange("h d v -> d h v"), R=[], W=["S%d" % l], semkey="dma_Sin%d" % l)
                    cp("act", Sb[l][:], S[l][:], R=["S%d" % l], W=["Sb%d" % l])
                    P.dma("sp", cvf[:], sconv[l], R=[], W=["cvf"], semkey="dma_cvfin")
                    cp("dve", halo[l][:], cvf[:], R=["cvf"], W=["halo%d" % l])
            for ti in range(ntile):
                t0 = ti * T
                src = (xT_p[sq["idx"], :, :, t0:t0 + T] if sq["kind"] == "p" else xT_s[0, :, :, :])
                P.dma("sp", xT[:, :, :T], src.rearrange("kc p t -> p kc t"), R=[], W=["xT"], semkey="dma_xT")
                for l in range(DEPTH):
                    layer(l, T, C, sq, ti == 0, ti == ntile - 1)
                tt("dve", sqb[:, 0:8, :T], xT[:, :, :T], xT[:, :, :T], ALU.mult, R=["xT"], W=["sqb"])
                ph, pk = ps_half()
                for kc in range(8):
                    mm(ph[:, :T], onesb[:], sqb[:, kc, :T], kc == 0, kc == 7, R=["onesb", "sqb"], W=[pk])
                act(rstd[:, :T], ph[:, :T], AF.Ln, R=[pk, "epsc"], W=["rstd"], bias=epsc[:], scale=1.0 / D)
                act(rstd[:, :T], rstd[:, :T], AF.Exp, R=["rstd"], W=["rstd"], scale=-0.5)
                for kc in range(8):
                    stt("dve", oT[:, kc, :T], xT[:, kc, :T], fnc[:, kc:kc + 1], rstd[:, :T], ALU.mult, ALU.mult,
                        R=["xT", "rstd", "fnc"], W=["oT"])
                dst = (yT_p[sq["idx"], :, :, t0:t0 + T] if sq["kind"] == "p" else yT_s[0, :, :, :])
                P.dma("sp", dst.rearrange("kc p t -> p kc t"), oT[:, :, :T], R=["oT"], W=[], semkey="dma_y")
                P.stage("tile_end_%s_%d" % (sq["kind"] + str(sq["idx"]), ti))
            for l in range(DEPTH):
                dst = (ndl_p[l, sq["idx"]] if sq["kind"] == "p" else ndl_s[l, 0])
                P.dma("sp", dst.rearrange("h d v -> d h v"), S[l][:], R=["S%d" % l], W=[], semkey="dma_sout%d" % l)
        import os as _os
        if _os.environ.get("K_DELAY"):
            P.dead = False
            for _i in range(int(_os.environ["K_DELAY"])):
                P.op("pe", "matmul", pd[0][:, 0:512], lhsT=onesb[:], rhs=wbuf[0][:, 0, :], start=True, stop=True,
                     R=["onesb", "wbuf0"], W=["pd0a"], inc=(_i % 64 == 63))
            P.dead = True
        assert P.dead or wstate["consumed"] == len(wplan), (wstate, len(wplan))
        for sk, v in P.dma_cnt.items():
            P._need("sp", sk, v)
        for e_ in ("pe", "act", "dve", "pool"):
            P._need("sp", e_, P.cnt[e_])
        build.ninst = P.ninst
    return nc


def _prep_inputs(inputs, SEQ):
    f = lambda a: np.ascontiguousarray(np.asarray(a, dtype=np.float32))
    xp = f(inputs["x_prompt"])[:, :SEQ]
    xs = f(inputs["x_sample"])
    xpT = np.ascontiguousarray(xp.reshape(16, SEQ, 8, 128).transpose(0, 2, 3, 1))
    xsT = np.ascontiguousarray(xs.reshape(8, DEC_SEQ, 8, 128).transpose(0, 2, 3, 1))
    sc = f(inputs["state_conv"])
    scT = np.ascontiguousarray(sc.reshape(DEPTH, 8, 3, 24, 128).transpose(1, 0, 4, 3, 2))
    sd = f(inputs["state_delta"])
    rep = lambda a: np.ascontiguousarray(np.broadcast_to(a[None], (128,) + a.shape))
    common = {
        "ln1c": np.ascontiguousarray(f(inputs["ln1"]).reshape(DEPTH, 8, 128).transpose(2, 0, 1)),
        "ln2c": np.ascontiguousarray(f(inputs["ln2"]).reshape(DEPTH, 8, 128).transpose(2, 0, 1)),
        "fnc": np.ascontiguousarray(f(inputs["final_norm"]).reshape(8, 128).transpose(1, 0)),
        "cwc": np.ascontiguousarray(f(inputs["conv_w"]).reshape(DEPTH, 24, 128, CW).transpose(2, 0, 1, 3)),
        "onc": np.ascontiguousarray(f(inputs["o_norm"]).transpose(1, 0)),
        "algb": rep(f(inputs["a_ln_g"])),
        "albb": rep(f(inputs["a_ln_b"])),
        "wstT": np.ascontiguousarray(f(inputs["w_s"]).transpose(3, 0, 1, 2)),
        "bsr": np.ascontiguousarray(f(inputs["b_s"]).reshape(1, DEPTH, D)),
        "alogb": rep(f(inputs["a_log"])),
        "dtbb": rep(f(inputs["dt_bias"])),
        "w_in": f(inputs["w_in"]), "p_a": f(inputs["p_a"]), "p_b": f(inputs["p_b"]), "w_o": f(inputs["w_o"]),
        "w_up": f(inputs["w_up"]), "w_down": f(inputs["w_down"]),
    }
    maps = []
    for c in range(NCORES):
        m = dict(common)
        m["xT_p"] = np.ascontiguousarray(xpT[2 * c:2 * c + 2])
        m["xT_s"] = np.ascontiguousarray(xsT[c:c + 1])
        m["sconv"] = np.ascontiguousarray(scT[c])
        m["sdelta"] = np.ascontiguousarray(sd[:, c])
        maps.append(m)
    return maps


def _assemble(results, SEQ):
    yp = np.concatenate([r["yT_p"] for r in results], axis=0)
    y_prompt = np.ascontiguousarray(yp.transpose(0, 3, 1, 2).reshape(16, SEQ, D))
    ys = np.concatenate([r["yT_s"] for r in results], axis=0)
    y_sample = np.ascontiguousarray(ys.transpose(0, 3, 1, 2).reshape(8, DEC_SEQ, D))
    cvp = np.concatenate([r["ncv_p"] for r in results], axis=1)
    new_conv_prompt = np.ascontiguousarray(cvp.transpose(0, 1, 4, 3, 2).reshape(DEPTH, 16, 3, 3072))
    new_delta_prompt = np.ascontiguousarray(np.concatenate([r["ndl_p"] for r in results], axis=1))
    cvs = np.concatenate([r["ncv_s"] for r in results], axis=1)
    new_conv_sample = np.ascontiguousarray(cvs.transpose(0, 1, 4, 3, 2).reshape(DEPTH, 8, 3, 3072))
    new_delta_sample = np.ascontiguousarray(np.concatenate([r["ndl_s"] for r in results], axis=1))
    new_gv = np.ascontiguousarray(np.concatenate([r["ngv_s"] for r in results], axis=1))
    outs = (y_prompt, y_sample, new_conv_prompt, new_delta_prompt, new_conv_sample, new_delta_sample, new_gv)
    return tuple(np.asarray(o, dtype=np.float32) for o in outs)


def run(inputs, SEQ, stop_at=None, lite=False, start_at=None):
    nc = build(SEQ, stop_at, lite, start_at)
    maps = _prep_inputs(inputs, SEQ)
    if lite:
        for m in maps:
            for k in ("w_in", "p_a", "p_b", "w_o", "w_up", "w_down"):
                m.pop(k)
    res = run_bass_kernel_spmd(nc, maps, core_ids=list(range(NCORES)))
    return _assemble(res.results, SEQ)


def kernel(**inputs):
    return run(inputs, SEQ_FULL)
```

```python
import numpy as np
from contextlib import ExitStack
import concourse.bass as bass
import concourse.mybir as mybir
from concourse.bass_utils import run_bass_kernel_spmd

F32 = mybir.dt.float32
BF16 = mybir.dt.bfloat16
AF = mybir.ActivationFunctionType
ALU = mybir.AluOpType
AX = mybir.AxisListType

NCORES = 8
D = 1024
DEPTH = 2
NIN = 8208
DFF = 4096
EPS = 1e-6
NEG = -30000.0
import os as _os0
NOSELF = set((_os0.environ.get("K_NOSELF") or "").split(",")) - {""}
SEQ_FULL = 4096
DEC_SEQ = 16
CW = 4


class Prog:
    def __init__(self, nc, es):
        self.nc = nc
        self.es = es
        self.eng = {"pe": nc.tensor, "act": nc.scalar, "dve": nc.vector, "pool": nc.gpsimd, "sp": nc.sync}
        self.sem = {k: es.enter_context(nc.semaphore("sem_" + k)) for k in self.eng}
        self.cnt = {k: 0 for k in self.eng}
        self.waited = {k: {} for k in self.eng}
        self.lastw = {}
        self.readers = {}
        self.dma_sems = {}
        self.dma_cnt = {}
        self.ninst = 0
        self.dead = False
        self.stop_at = None

    def stage(self, name):
        if self.stop_at is not None and name == self.stop_at:
            self.dead = True
            self.stopped = True
        elif getattr(self, "start_at", None) is not None and not getattr(self, "stopped", False):
            if name == "setup":
                self.dead = True
            elif name == self.start_at:
                self.dead = False

    def _need(self, e, semkey, val):
        if val <= 0:
            return
        w = self.waited[e]
        if w.get(semkey, 0) >= val:
            return
        w[semkey] = val
        s = self.sem[semkey] if semkey in self.sem else self.dma_sems[semkey]
        self.eng[e].wait_ge(s, val)
        self.ninst += 1

    def _deps(self, e, reads, writes):
        skip_self = (e == "pe") or (e in NOSELF)
        for k in reads:
            lw = self.lastw.get(k)
            if lw and not (skip_self and lw[0] == e):
                self._need(e, lw[0], lw[1])
        for k in writes:
            lw = self.lastw.get(k)
            if lw and not (skip_self and lw[0] == e):
                self._need(e, lw[0], lw[1])
            for sk, v in self.readers.get(k, {}).items():
                if not (skip_self and sk == e):
                    self._need(e, sk, v)

    def _commit(self, semkey, val, reads, writes):
        for k in writes:
            self.lastw[k] = (semkey, val)
            self.readers[k] = {}
        for k in reads:
            r = self.readers.setdefault(k, {})
            if r.get(semkey, 0) < val:
                r[semkey] = val

    def op(self, e, fn, *args, R=(), W=(), inc=True, **kw):
        if self.dead:
            return None
        W = list(W) + [k for k in R if k.startswith("pd") and k not in W]
        self._deps(e, R, W)
        ins = getattr(self.eng[e], fn)(*args, **kw)
        self.ninst += 1
        if inc:
            self.cnt[e] += 1
            ins.then_inc(self.sem[e], 1)
            self._commit(e, self.cnt[e], R, W)
        else:
            self._commit(e, self.cnt[e] + 1, R, W)
        return ins

    def dma(self, e, out, in_, R, W, semkey):
        if self.dead:
            return None
        if semkey not in self.dma_sems:
            self.dma_sems[semkey] = self.es.enter_context(self.nc.semaphore(semkey))
            self.dma_cnt[semkey] = 0
        self._deps(e, R, W)
        ins = self.eng[e].dma_start(out=out, in_=in_)
        self.dma_cnt[semkey] += 16
        ins.then_inc(self.dma_sems[semkey], 16)
        self._commit(semkey, self.dma_cnt[semkey], R, W)
        self.ninst += 1
        return ins

    def finish(self, e="sp"):
        for k, (sk, v) in list(self.lastw.items()):
            self._need(e, sk, v)


def build(SEQ, stop_at=None, lite=False, start_at=None):
    nc = bass.Bass("TRN2", target_bir_lowering=False)
    es = ExitStack()

    def din(name, shape, dt=F32):
        return nc.dram_tensor(name, list(shape), dt, kind="ExternalInput").ap()

    def dout(name, shape, dt=F32):
        return nc.dram_tensor(name, list(shape), dt, kind="ExternalOutput").ap()

    def dscr(name, shape, dt=BF16):
        return nc.dram_tensor(name, list(shape), dt, kind="Internal").ap()

    xT_p = din("xT_p", [2, 8, 128, SEQ])
    xT_s = din("xT_s", [1, 8, 128, DEC_SEQ])
    sconv = din("sconv", [DEPTH, 128, 24, 3])
    sdelta = din("sdelta", [DEPTH, 8, 128, 128])
    ln1_d = din("ln1c", [128, DEPTH, 8])
    ln2_d = din("ln2c", [128, DEPTH, 8])
    fn_d = din("fnc", [128, 8])
    cw_d = din("cwc", [128, DEPTH, 24, 4])
    on_d = din("onc", [128, DEPTH])
    alg_d = din("algb", [128, DEPTH, D])
    alb_d = din("albb", [128, DEPTH, D])
    wst_d = din("wstT", [128, DEPTH, 8, 128])
    bs_d = din("bsr", [1, DEPTH, D])
    alog_d = din("alogb", [128, DEPTH, 8])
    dtb_d = din("dtbb", [128, DEPTH, 8])
    wdecl = (lambda n, s_: dscr(n + "_lite", s_, F32)) if lite else din
    w_in_d = wdecl("w_in", [DEPTH, D, NIN])
    p_a_d = wdecl("p_a", [DEPTH, D, D])
    p_b_d = wdecl("p_b", [DEPTH, D, D])
    w_o_d = wdecl("w_o", [DEPTH, D, D])
    w_up_d = wdecl("w_up", [DEPTH, D, DFF])
    w_dn_d = wdecl("w_down", [DEPTH, DFF, D])

    yT_p = dout("yT_p", [2, 8, 128, SEQ])
    yT_s = dout("yT_s", [1, 8, 128, DEC_SEQ])
    ncv_p = dout("ncv_p", [DEPTH, 2, 128, 24, 3])
    ndl_p = dout("ndl_p", [DEPTH, 2, 8, 128, 128])
    ncv_s = dout("ncv_s", [DEPTH, 1, 128, 24, 3])
    ndl_s = dout("ndl_s", [DEPTH, 1, 8, 128, 128])
    ngv_s = dout("ngv_s", [DEPTH, 1, DEC_SEQ, D])

    w_in_b = dscr("w_in_b", [DEPTH, D, NIN])
    p_a_b = dscr("p_a_b", [DEPTH, D, D])
    p_b_b = dscr("p_b_b", [DEPTH, D, D])
    w_o_b = dscr("w_o_b", [DEPTH, D, D])
    w_up_b = dscr("w_up_b", [DEPTH, D, DFF])
    w_dn_b = dscr("w_dn_b", [DEPTH, DFF, D])

    with es:
        P = Prog(nc, es)
        P.stop_at = stop_at
        P.start_at = start_at

        def sb(name, shape, dt):
            return es.enter_context(nc.sbuf_tensor(name, list(shape), dt))

        TM = 128

        identf = sb("identf", [128, 128], F32)
        identb = sb("identb", [128, 128], BF16)
        trif = sb("trif", [128, 128], F32)
        ntrif = sb("ntrif", [128, 128], F32)
        trigt = sb("trigt", [128, 128], F32)
        onesf = sb("onesf", [128, 128], F32)
        offd = sb("offd", [128, 128], F32)
        nonesf = sb("nonesf", [128, 128], F32)
        onesb = sb("onesb", [128, 128], BF16)
        maskb = sb("maskb", [128, 8, 128], BF16)
        maskc = sb("maskc", [128, 8, 128], BF16)
        epsc = sb("epsc", [128, 1], F32)
        epsq = sb("epsq", [128, 1], F32)
        ln1c = sb("ln1c_s", [128, DEPTH, 8], F32)
        ln2c = sb("ln2c_s", [128, DEPTH, 8], F32)
        fnc = sb("fnc_s", [128, 8], F32)
        cwc = sb("cwc_s", [128, DEPTH, 24, 4], F32)
        onc = sb("onc_s", [128, DEPTH], F32)
        algb = sb("algb_s", [128, DEPTH, D], F32)
        albb = sb("albb_s", [128, DEPTH, D], F32)
        wstb = sb("wstb", [128, DEPTH, 8, 128], BF16)
        bsb = sb("bsb", [1, DEPTH, D], BF16)
        alogb = sb("alogb_s", [128, DEPTH, 8], F32)
        nexpA = sb("nexpA", [128, DEPTH, 8], F32)
        dtbb = sb("dtbb_s", [128, DEPTH, 8], F32)
        wba = sb("wba", [128, DEPTH, 8, 16], BF16)

        NA = [sb("NA%d" % i, [128, 8, 128], F32) for i in range(2)]
        NTA1 = sb("NTA1", [128, 8, 128], F32)
        xT = sb("xT", [128, 8, TM], F32)
        hT = sb("hT", [128, 8, TM], BF16)
        sqb = sb("sqb", [128, 16, TM], BF16)
        rstd = sb("rstd", [128, 4 * TM], F32)
        uT = sb("uT", [128, 8, TM], BF16)
        vf = sb("vf", [128, D], F32)
        vc = sb("vc", [128, D], F32)
        vnb = sb("vnb", [128, D], BF16)
        st4 = sb("st4", [128, 8], F32)
        qkvpre = sb("qkvpre", [128, 24, 3 + TM], BF16)
        halo = [sb("halo%d" % l, [128, 24, 3], BF16) for l in range(DEPTH)]
        cvf = sb("cvf", [128, 24, 3], F32)
        qkf = sb("qkf", [128, 16, TM], F32)
        qkT = sb("qkT", [128, 16, TM], BF16)
        vT = sb("vT", [128, 8, TM], BF16)
        zsT = sb("zsT", [128, 8, TM], BF16)
        sga = sb("sga", [128, 8, TM], BF16)
        sgb = sb("sgb", [128, 8, TM], BF16)
        oT = sb("oT", [128, 8, TM], F32)
        zbT = sb("zbT", [128, 8, TM], BF16)
        mixT = sb("mixT", [128, 8, TM], BF16)
        mtf = sb("mtf", [128, 4, TM], F32)
        hid = [sb("hid%d" % i, [128, 4, TM], BF16) for i in range(2)]
        rl = sb("rl", [128, 4, TM], BF16)
        rl2 = sb("rl2", [128, 4, TM], BF16)
        S = [sb("S%d" % l, [128, 8, 128], F32) for l in range(DEPTH)]
        Sb = [sb("Sb%d" % l, [128, 8, 128], BF16) for l in range(DEPTH)]
        ba = sb("ba", [128, 16], F32)
        beta = sb("beta", [128, 8], F32)
        gx = sb("gx", [128, 8], F32)
        gm = sb("gm", [128, 8], F32)
        gn = sb("gn", [128, 8], F32)
        gg = sb("gg", [128, 8], F32)
        egc = sb("egc", [128, 16], F32)
        nbe = sb("nbe", [128, 8], F32)
        B1 = sb("B1", [128, 8, 128], F32)
        B2 = sb("B2", [128, 8, 128], F32)
        egam = sb("egam", [128, 8, 128], F32)
        B4 = sb("B4", [128, 8, 128], F32)
        PT = sb("PT", [128, 8, 128], BF16)
        Yb = sb("Yb", [128, 8, 128], BF16)
        bv = sb("bv", [128, 8, 128], BF16)
        kg = sb("kg", [128, 8, 128], BF16)
        qg = sb("qg", [128, 8, 128], BF16)
        rb = sb("rb", [128, 8, 128], BF16)
        vnw = sb("vnw", [128, 8, 128], BF16)
        NWB = 4
        wbuf = [sb("wbuf%d" % i, [128, 8, 512], BF16) for i in range(NWB)]

        def cdma(dst, src, key):
            P.dma("sp", dst, src, R=[], W=[key], semkey="dma_c_" + key)

        cdma(ln1c[:], ln1_d, "ln1c"); cdma(ln2c[:], ln2_d, "ln2c"); cdma(fnc[:], fn_d, "fnc")
        cdma(cwc[:], cw_d, "cwc"); cdma(onc[:], on_d, "onc"); cdma(algb[:], alg_d, "algb")
        cdma(albb[:], alb_d, "albb")
        cdma(alogb[:], alog_d, "alogb"); cdma(dtbb[:], dtb_d, "dtbb")

        def cast_w(dst, src, rows, key):
            for l in range(DEPTH):
                for r0 in range(0, rows, 128):
                    P.dma("pool", dst[l, r0:r0 + 128, :], src[l, r0:r0 + 128, :], R=[], W=[key], semkey="dma_" + key)

        cast_w(w_in_b, w_in_d, D, "w_in_b")
        cast_w(p_a_b, p_a_d, D, "p_a_b")
        cast_w(p_b_b, p_b_d, D, "p_b_b")
        cast_w(w_o_b, w_o_d, D, "w_o_b")
        cast_w(w_up_b, w_up_d, D, "w_up_b")
        cast_w(w_dn_b, w_dn_d, DFF, "w_dn_b")

        def gp(fn, *a, R=(), W=(), **kw):
            return P.op("pool", fn, *a, R=R, W=W, **kw)

        gp("memset", onesf[:], 1.0, W=["onesf"])
        gp("memset", nonesf[:], -1.0, W=["nonesf"])
        gp("memset", onesb[:], 1.0, W=["onesb"])
        gp("memset", epsc[:], EPS, W=["epsc"])
        gp("memset", epsq[:], EPS * 128.0, W=["epsq"])
        gp("affine_select", out=identf[:], in_=onesf[:], pattern=[[1, 128]], compare_op=ALU.is_equal, fill=0.0,
           base=0, channel_multiplier=-1, R=["onesf"], W=["identf"])
        gp("affine_select", out=trif[:], in_=onesf[:], pattern=[[1, 128]], compare_op=ALU.is_ge, fill=0.0,
           base=0, channel_multiplier=-1, R=["onesf"], W=["trif"])
        gp("affine_select", out=ntrif[:], in_=nonesf[:], pattern=[[1, 128]], compare_op=ALU.is_ge, fill=0.0,
           base=0, channel_multiplier=-1, R=["nonesf"], W=["ntrif"])
        gp("affine_select", out=trigt[:], in_=onesf[:], pattern=[[-1, 128]], compare_op=ALU.is_gt, fill=0.0,
           base=0, channel_multiplier=1, R=["onesf"], W=["trigt"])
        P.op("dve", "tensor_copy", out=identb[:], in_=identf[:], R=["identf"], W=["identb"])
        P.op("dve", "tensor_tensor", out=offd[:], in0=onesf[:], in1=identf[:], op=ALU.subtract, R=["onesf", "identf"], W=["offd"])
        mt0, mt1, mt2 = B1[:, 0, :], B2[:, 0, :], B4[:, 0, :]
        gp("memset", mt0, NEG, W=["B1"])
        gp("affine_select", out=mt1, in_=mt0, pattern=[[-1, 128]], compare_op=ALU.is_gt, fill=0.0,
           base=0, channel_multiplier=1, R=["B1"], W=["B2"])
        gp("affine_select", out=mt2, in_=mt0, pattern=[[1, 128]], compare_op=ALU.is_ge, fill=0.0,
           base=0, channel_multiplier=-1, R=["B1"], W=["B4"])
        for h in range(8):
            P.op("dve", "tensor_copy", out=maskb[:, h, :], in_=mt1, R=["B2"], W=["maskb"])
            P.op("dve", "tensor_copy", out=maskc[:, h, :], in_=mt2, R=["B4"], W=["maskc"])
        for l in range(DEPTH):
            P.dma("sp", egam[:], wst_d[:, l, :, :], R=[], W=["egam"], semkey="dma_c2")
            for g in range(8):
                P.op("dve", "tensor_tensor", out=wstb[:, l, g, :], in0=egam[:, g, :], in1=trif[:], op=ALU.mult,
                     R=["egam", "trif"], W=["wstb"])
            P.dma("sp", vf[0:1, :], bs_d[0:1, l, :], R=[], W=["vf"], semkey="dma_c3")
            P.op("dve", "tensor_copy", out=bsb[0:1, l, :], in_=vf[0:1, :], R=["vf"], W=["bsb"])
        P.op("act", "activation", out=nexpA[:], in_=alogb[:], func=AF.Exp, R=["alogb"], W=["nexpA"])
        P.op("dve", "tensor_scalar", out=nexpA[:], in0=nexpA[:], scalar1=-1.0, scalar2=None, op0=ALU.mult,
             R=["nexpA"], W=["nexpA"])
        for l in range(DEPTH):
            P.dma("sp", wba[:, l, :, :], w_in_b[l, :, 6144:6160].rearrange("(kc p) n -> p kc n", p=128),
                  R=["w_in_b"], W=["wba"], semkey="dma_c_wba")

        P.stage("setup")
        pd = [es.enter_context(nc.psum_tensor("pd%d" % i, [128, 1024], F32)) for i in range(4)]
        ps_state = {"i": 0}

        def ps_half():
            i = ps_state["i"] % 6
            ps_state["i"] += 1
            t = pd[i // 2]
            return t[:, (i % 2) * 512:(i % 2) * 512 + 512], "pd%d%s" % (i // 2, "ab"[i % 2])

        def ps_pair():
            if ps_state["i"] % 2:
                ps_state["i"] += 1
            i = ps_state["i"] % 6
            ps_state["i"] += 2
            return pd[i // 2][:], ["pd%da" % (i // 2), "pd%db" % (i // 2)]

        wplan = []
        wstate = {"issued": 0, "consumed": 0}

        def layer_plan(l):
            pl = []
            wi = w_in_b[l].rearrange("(kc p) n -> p kc n", p=128)
            for g in range(2):
                pl.append(("u%d" % g, wi[:, :, g * 512:(g + 1) * 512], "w_in_b"))
            for g in range(2):
                pl.append(("v%d" % g, wi[:, :, 1024 + g * 512:1024 + (g + 1) * 512], "w_in_b"))
            for g in range(6):
                pl.append(("qkv%d" % g, wi[:, :, 2048 + g * 512:2048 + (g + 1) * 512], "w_in_b"))
            for g in range(2):
                pl.append(("z%d" % g, wi[:, :, 5120 + g * 512:5120 + (g + 1) * 512], "w_in_b"))
            for g in range(2):
                pl.append(("ga%d" % g, wi[:, :, 6160 + g * 512:6160 + (g + 1) * 512], "w_in_b"))
                pl.append(("gb%d" % g, wi[:, :, 7184 + g * 512:7184 + (g + 1) * 512], "w_in_b"))
                pl.append(("pa%d" % g, p_a_b[l].rearrange("(kc p) n -> p kc n", p=128)[:, :, g * 512:(g + 1) * 512], "p_a_b"))
                pl.append(("pb%d" % g, p_b_b[l].rearrange("(kc p) n -> p kc n", p=128)[:, :, g * 512:(g + 1) * 512], "p_b_b"))
            for g in range(2):
                pl.append(("wo%d" % g, w_o_b[l].rearrange("(kc p) n -> p kc n", p=128)[:, :, g * 512:(g + 1) * 512], "w_o_b"))
            def up_e(g):
                return ("up%d" % g, w_up_b[l].rearrange("(kc p) n -> p kc n", p=128)[:, :, g * 512:(g + 1) * 512], "w_up_b")

            def dn_e(g):
                return ("dn%d" % g, w_dn_b[l, g * 512:(g + 1) * 512, :].rearrange("(kc p) n -> p kc n", p=128), "w_dn_b")
            pl.append(up_e(0))
            for g in range(8):
                if g + 1 < 8:
                    pl.append(up_e(g + 1))
                pl.append(dn_e(g))
            return pl

        def w_issue_upto(n):
            while wstate["issued"] < min(n, len(wplan)):
                i = wstate["issued"]
                tag, view, srckey = wplan[i]
                slot = i % NWB
                if tag.startswith("dn"):
                    dst = wbuf[slot][:].rearrange("p a b -> p (a b)").rearrange("p (kc n) -> p kc n", kc=4)
                else:
                    dst = wbuf[slot][:]
                P.dma("sp", dst, view, R=[srckey], W=["wbuf%d" % slot], semkey="dma_wbuf%d" % slot)
                wstate["issued"] += 1

        def w_get(tag):
            i = wstate["consumed"]
            assert wplan[i][0] == tag, (wplan[i][0], tag)
            w_issue_upto(i + 1)
            wstate["consumed"] += 1
            slot = i % NWB
            return wbuf[slot], "wbuf%d" % slot, i

        def w_done(i):
            w_issue_upto(i + NWB)

        def mm(out, lhsT, rhs, start, stop, R, W, inc=None):
            if inc is None:
                inc = stop
            return P.op("pe", "matmul", out, lhsT=lhsT, rhs=rhs, start=start, stop=stop, R=R, W=W, inc=inc)

        def tr(out, in_, ident, R, W):
            return P.op("pe", "transpose", out=out, in_=in_, identity=ident, R=R, W=W)

        def act(out, in_, func, R, W, **kw):
            return P.op("act", "activation", out=out, in_=in_, func=func, R=R, W=W, **kw)

        def tt(e, out, in0, in1, op, R, W):
            return P.op(e, "tensor_tensor", out=out, in0=in0, in1=in1, op=op, R=R, W=W)

        def ts(e, out, in0, s1, op0, R, W, s2=None, op1=None):
            if op1 is None:
                return P.op(e, "tensor_scalar", out=out, in0=in0, scalar1=s1, scalar2=None, op0=op0, R=R, W=W)
            return P.op(e, "tensor_scalar", out=out, in0=in0, scalar1=s1, scalar2=s2, op0=op0, op1=op1, R=R, W=W)

        def stt(e, out, in0, scalar, in1, op0, op1, R, W):
            return P.op(e, "scalar_tensor_tensor", out=out, in0=in0, scalar=scalar, in1=in1, op0=op0, op1=op1,
                        R=R, W=W)

        def cp(e, out, in_, R, W):
            if e == "act":
                return act(out, in_, AF.Identity, R, W)
            return P.op(e, "tensor_copy", out=out, in_=in_, R=R, W=W)

        def rmsnorm_fm(T, gcol, gkey, dst, dstkey):
            tt("dve", sqb[:, 0:8, :T], xT[:, :, :T], xT[:, :, :T], ALU.mult, R=["xT"], W=["sqb"])
            ph, pk = ps_half()
            for kc in range(8):
                mm(ph[:, :T], onesb[:], sqb[:, kc, :T], kc == 0, kc == 7, R=["onesb", "sqb"], W=[pk])
            act(rstd[:, :T], ph[:, :T], AF.Ln, R=[pk, "epsc"], W=["rstd"], bias=epsc[:], scale=1.0 / D)
            act(rstd[:, :T], rstd[:, :T], AF.Exp, R=["rstd"], W=["rstd"], scale=-0.5)
            for kc in range(8):
                stt("dve", dst[:, kc, :T], xT[:, kc, :T], gcol[:, kc:kc + 1], rstd[:, :T], ALU.mult, ALU.mult,
                    R=["xT", "rstd", gkey], W=[dstkey])

        def fm_proj(T, wt, wkey, src, srckey, nblk, evac):
            bpb = 512 // T if T >= 128 else 4
            bpb = min(bpb, nblk)
            for m0 in range(0, nblk, bpb):
                ph, pk = ps_half()
                nb = min(bpb, nblk - m0)
                pv = ph[:, :nb * T].rearrange("p (b t) -> p b t", b=nb)
                for b in range(nb):
                    for kc in range(8):
                        mm(pv[:, b, :], wt[:, kc, (m0 + b) * 128:(m0 + b + 1) * 128], src[:, kc, :T], kc == 0, kc == 7,
                           R=[wkey, srckey], W=[pk])
                evac(m0, nb, pv, pk)

        def layer(l, T, C, seq, first_tile, last_tile):
            NCH = T // C
            rmsnorm_fm(T, ln1c[:, l, :], "ln1c", hT, "hT")
            prep_gen = delta_prep(l, T, C, 0) if NCH == 1 else None
            if prep_gen is not None:
                next(prep_gen)
            P.stage("norm1_%d" % l)
            for g in range(2):
                wt, wk, wi_ = w_get("u%d" % g)

                def ev(m0, nb, pv, pk, g=g):
                    act(uT[:, g * 4 + m0:g * 4 + m0 + nb, :T], pv, AF.Gelu, R=[pk], W=["uT"])
                fm_proj(T, wt, wk, hT, "hT", 4, ev)
                w_done(wi_)
            if prep_gen is not None:
                next(prep_gen)
            P.stage("u_%d" % l)
            wv = [w_get("v%d" % g) for g in range(2)]
            for c in range(NCH):
                c0 = c * C
                for g in range(2):
                    wt, wk, _ = wv[g]
                    ph, pk = ps_half()
                    for kc in range(8):
                        mm(ph[:C, :], hT[:, kc, c0:c0 + C], wt[:, kc, :], kc == 0, kc == 7, R=["hT", wk], W=[pk])
                    act(vf[:C, g * 512:(g + 1) * 512], ph[:C, :], AF.Gelu, R=[pk], W=["vf"])
                P.op("dve", "reduce_sum", out=st4[:C, 0:1], in_=vf[:C, :], axis=AX.X, R=["vf"], W=["st4"])
                ts("dve", st4[:C, 1:2], st4[:C, 0:1], -1.0 / D, ALU.mult, R=["st4"], W=["st4"])
                ts("dve", vc[:C, :], vf[:C, :], st4[:C, 1:2], ALU.add, R=["vf", "st4"], W=["vc"])
                tt("dve", vf[:C, :], vc[:C, :], vc[:C, :], ALU.mult, R=["vc"], W=["vf"])
                P.op("dve", "reduce_sum", out=st4[:C, 2:3], in_=vf[:C, :], axis=AX.X, R=["vf"], W=["st4"])
                act(st4[:C, 3:4], st4[:C, 2:3], AF.Ln, R=["st4", "epsc"], W=["st4"], bias=epsc[:C, :], scale=1.0 / D)
                act(st4[:C, 4:5], st4[:C, 3:4], AF.Exp, R=["st4"], W=["st4"], scale=-0.5)
                stt("dve", vc[:C, :], vc[:C, :], st4[:C, 4:5], algb[:C, l, :], ALU.mult, ALU.mult,
                    R=["vc", "st4", "algb"], W=["vc"])
                tt("dve", vc[:C, :], vc[:C, :], albb[:C, l, :], ALU.add, R=["vc", "albb"], W=["vc"])
                if seq["kind"] == "s":
                    P.dma("sp", ngv_s[l, 0, :, :], vc[:C, :], R=["vc"], W=[], semkey="dma_ngv")
                cp("act", vnb[:C, :], vc[:C, :], R=["vc"], W=["vnb"])
                pp, pks = ps_pair()
                ppv = pp.rearrange("p (g t) -> p g t", g=8)
                for g in range(8):
                    k = pks[g // 4]
                    mm(ppv[:, g, :C], vnb[:C, g * 128:(g + 1) * 128], wstb[:C, l, g, :C], True, False,
                       R=["vnb", "wstb"], W=[k])
                    mm(ppv[:, g, :C], onesb[0:1, :], bsb[0:1, l, g * 128:g * 128 + C], False, True,
                       R=["onesb", "bsb"], W=[k])
                tt("dve", uT[:, :, c0:c0 + C], ppv[:, :, :C], uT[:, :, c0:c0 + C], ALU.mult, R=pks + ["uT"], W=["uT"])
            for g in range(2):
                w_done(wv[g][2])
            if prep_gen is not None:
                for _ in prep_gen:
                    pass
            P.stage("gmlp_%d" % l)
            qp = qkvpre
            qk_ = "qkvpre"
            cp("pool", qp[:, :, 0:3], halo[l][:], R=["halo%d" % l], W=[qk_])
            for g in range(6):
                wt, wk, wi_ = w_get("qkv%d" % g)

                def ev(m0, nb, pv, pk, g=g):
                    cp("act", qp[:, g * 4 + m0:g * 4 + m0 + nb, 3:3 + T], pv, R=[pk], W=[qk_])
                    import os as _os
                    if last_tile and not _os.environ.get("K_NOCVF"):
                        cp("dve", cvf[:, g * 4 + m0:g * 4 + m0 + nb, :], pv[:, :, T - 3:T], R=[pk], W=["cvf"])
                fm_proj(T, wt, wk, hT, "hT", 4, ev)
                w_done(wi_)
            P.stage("qkv_%d" % l)
            if last_tile:
                dst = (ncv_p[l, seq["idx"]] if seq["kind"] == "p" else ncv_s[l, 0])
                P.dma("sp", dst, cvf[:], R=["cvf"], W=[], semkey="dma_cvf")
            P.stage("cvfdma_%d" % l)
            cp("pool", halo[l][:], qp[:, :, T:T + 3], R=[qk_], W=["halo%d" % l])
            P.stage("halo_%d" % l)
            tmpv = vf[:, :8 * T].rearrange("p (b t) -> p b t", b=8)
            vacc = vc[:, :8 * T].rearrange("p (b t) -> p b t", b=8)
            for gi in range(3):
                e_ = "dve" if gi < 2 else "pool"
                accv = qkf[:, gi * 8:(gi + 1) * 8, :T] if gi < 2 else vacc
                acck = "qkf" if gi < 2 else "vc"
                for j in range(CW):
                    wj = cwc[:, l, gi * 8:(gi + 1) * 8, j:j + 1].to_broadcast([128, 8, T])
                    src = qp[:, gi * 8:(gi + 1) * 8, j:j + T]
                    if j == 0:
                        tt(e_, accv, src, wj, ALU.mult, R=[qk_, "cwc"], W=[acck])
                    else:
                        tt(e_, tmpv, src, wj, ALU.mult, R=[qk_, "cwc"], W=["vf"])
                        tt(e_, accv, accv, tmpv, ALU.add, R=[acck, "vf"], W=[acck])
            P.stage("taps_%d" % l)
            act(qkf[:, :, :T], qkf[:, :, :T], AF.Silu, R=["qkf"], W=["qkf"])
            act(vT[:, :, :T], vacc, AF.Silu, R=["vc"], W=["vT"])
            P.stage("silu_%d" % l)
            tt("dve", sqb[:, :, :T], qkf[:, :, :T], qkf[:, :, :T], ALU.mult, R=["qkf"], W=["sqb"])
            hpg = 4
            for hg in range(0, 16, hpg):
                ph, pk = ps_half()
                pv = ph[:, :hpg * T].rearrange("p (b t) -> p b t", b=hpg)
                for b in range(hpg):
                    mm(pv[:, b, :], onesb[:], sqb[:, hg + b, :T], True, True, R=["onesb", "sqb"], W=[pk])
                rv = rstd[:, :hpg * T].rearrange("p (b t) -> p b t", b=hpg)
                if hg < 8:
                    act(rv, pv, AF.Ln, R=[pk, "epsq"], W=["rstd"], bias=epsq[:], scale=128.0)
                else:
                    act(rv, pv, AF.Ln, R=[pk, "epsc"], W=["rstd"], bias=epsc[:], scale=1.0)
                act(rv, rv, AF.Exp, R=["rstd"], W=["rstd"], scale=-0.5)
                tt("dve", qkT[:, hg:hg + hpg, :T], qkf[:, hg:hg + hpg, :T], rv, ALU.mult, R=["qkf", "rstd"], W=["qkT"])
            P.stage("conv_%d" % l)
            def zfill(g):
                wt, wk, wi_ = w_get("z%d" % g)

                def ev(m0, nb, pv, pk, g=g):
                    act(zsT[:, g * 4 + m0:g * 4 + m0 + nb, :T], pv, AF.Silu, R=[pk], W=["zsT"])
                fm_proj(T, wt, wk, hT, "hT", 4, ev)
                w_done(wi_)
            for c in range(NCH):
                delta_chunk(l, T, C, c * C, hoisted=(NCH == 1), fillers=([zfill] if c == NCH - 1 else []))

            P.stage("delta_%d" % l)
            tt("dve", sqb[:, 0:8, :T], oT[:, :, :T], oT[:, :, :T], ALU.mult, R=["oT"], W=["sqb"])
            for hg in range(0, 8, hpg):
                ph, pk = ps_half()
                pv = ph[:, :hpg * T].rearrange("p (b t) -> p b t", b=hpg)
                for b in range(hpg):
                    mm(pv[:, b, :], onesb[:], sqb[:, hg + b, :T], True, True, R=["onesb", "sqb"], W=[pk])
                rv = rstd[:, :hpg * T].rearrange("p (b t) -> p b t", b=hpg)
                act(rv, pv, AF.Ln, R=[pk, "epsc"], W=["rstd"], bias=epsc[:], scale=1.0 / 128.0)
                act(rv, rv, AF.Exp, R=["rstd"], W=["rstd"], scale=-0.5)
                tt("dve", oT[:, hg:hg + hpg, :T], oT[:, hg:hg + hpg, :T], rv, ALU.mult, R=["oT", "rstd"], W=["oT"])
            stt("dve", zbT[:, :, :T], oT[:, :, :T], onc[:, l:l + 1], zsT[:, :, :T], ALU.mult, ALU.mult,
                R=["oT", "onc", "zsT"], W=["zbT"])
            P.stage("onorm_%d" % l)
            for g in range(2):
                wt, wk, wi_ = w_get("ga%d" % g)

                def ev(m0, nb, pv, pk, g=g):
                    act(sga[:, g * 4 + m0:g * 4 + m0 + nb, :T], pv, AF.Sigmoid, R=[pk], W=["sga"])
                fm_proj(T, wt, wk, hT, "hT", 4, ev)
                w_done(wi_)
                wt, wk, wi_ = w_get("gb%d" % g)

                def ev(m0, nb, pv, pk, g=g):
                    act(sgb[:, g * 4 + m0:g * 4 + m0 + nb, :T], pv, AF.Sigmoid, R=[pk], W=["sgb"])
                fm_proj(T, wt, wk, hT, "hT", 4, ev)
                w_done(wi_)
                wt, wk, wi_ = w_get("pa%d" % g)

                def ev(m0, nb, pv, pk, g=g):
                    tt("dve", mtf[:, m0:m0 + nb, :T], pv, sga[:, g * 4 + m0:g * 4 + m0 + nb, :T], ALU.mult,
                       R=[pk, "sga"], W=["mtf"])
                fm_proj(T, wt, wk, uT, "uT", 4, ev)
                w_done(wi_)
                wt, wk, wi_ = w_get("pb%d" % g)

                def ev(m0, nb, pv, pk, g=g):
                    tt("dve", mixT[:, g * 4 + m0:g * 4 + m0 + nb, :T], pv, sgb[:, g * 4 + m0:g * 4 + m0 + nb, :T], ALU.mult,
                       R=[pk, "sgb"], W=["mixT"])
                    tt("pool", mixT[:, g * 4 + m0:g * 4 + m0 + nb, :T], mixT[:, g * 4 + m0:g * 4 + m0 + nb, :T],
                       mtf[:, m0:m0 + nb, :T], ALU.add, R=["mixT", "mtf"], W=["mixT"])
                fm_proj(T, wt, wk, zbT, "zbT", 4, ev)
                w_done(wi_)
            for g in range(2):
                wt, wk, wi_ = w_get("wo%d" % g)

                def ev(m0, nb, pv, pk, g=g):
                    tt("dve", xT[:, g * 4 + m0:g * 4 + m0 + nb, :T], pv, xT[:, g * 4 + m0:g * 4 + m0 + nb, :T], ALU.add,
                       R=[pk, "xT"], W=["xT"])
                fm_proj(T, wt, wk, mixT, "mixT", 4, ev)
                w_done(wi_)
            P.stage("merge_%d" % l)
            rmsnorm_fm(T, ln2c[:, l, :], "ln2c", hT, "hT")
            acc = pd[3][:, :8 * T].rearrange("p (m t) -> p m t", m=8)
            rlb = [rl, rl2]

            def ffn_up(g):
                wt, wk, wi_ = w_get("up%d" % g)
                hb = hid[g % 2]
                hk = "hid%d" % (g % 2)
                rb_ = rlb[g % 2]
                rk_ = "rl%d" % (g % 2)

                def ev(m0, nb, pv, pk, hb=hb, hk=hk):
                    act(rb_[:, m0:m0 + nb, :T], pv, AF.Relu, R=[pk], W=[rk_])
                    tt("pool", hb[:, m0:m0 + nb, :T], rb_[:, m0:m0 + nb, :T], rb_[:, m0:m0 + nb, :T], ALU.mult,
                       R=[rk_], W=[hk])
                fm_proj(T, wt, wk, hT, "hT", 4, ev)
                w_done(wi_)

            def ffn_down(g):
                hb = hid[g % 2]
                hk = "hid%d" % (g % 2)
                wt, wk, wi_ = w_get("dn%d" % g)
                wdv = wt[:].rearrange("p a b -> p (a b)").rearrange("p (kc n) -> p kc n", kc=4)
                for m in range(8):
                    for kc in range(4):
                        last = (m == 7 and kc == 3)
                        first_in_bank = (g == 0 and kc == 0 and (m * T) % 512 == 0)
                        P.op("pe", "matmul", acc[:, m, :], lhsT=wdv[:, kc, m * 128:(m + 1) * 128], rhs=hb[:, kc, :T],
                             start=first_in_bank, stop=(g == 7 and kc == 3), skip_group_check=True,
                             R=[wk, hk], W=["pd3a", "pd3b"], inc=(last or (g == 7 and kc == 3)))
                w_done(wi_)

            ffn_up(0)
            for g in range(8):
                if g + 1 < 8:
                    ffn_up(g + 1)
                ffn_down(g)
            tt("dve", xT[:, :, :T], acc, xT[:, :, :T], ALU.add, R=["pd3a", "pd3b", "xT"], W=["xT"])
            P.stage("ffn_end_%d_%s" % (l, seq["kind"] + str(seq["idx"])))

        def delta_prep(l, T, C, c0):
            hp = min(8, 512 // C)
            nhalf = 8 // hp
            gtri, DTi = B1, B1
            ngb, Ds, Atmp = B2, B2, B2

            def w8(ps_ap, np_):
                return ps_ap[:np_, :8 * C].rearrange("p (h x) -> p h x", h=8)
            ph, pk = ps_half()
            for kc in range(8):
                mm(ph[:C, :16], hT[:, kc, c0:c0 + C], wba[:, l, kc, :], kc == 0, kc == 7, R=["hT", "wba"], W=[pk])
            cp("dve", ba[:C, :], ph[:C, :16], R=[pk], W=["ba"])
            act(beta[:C, :], ba[:C, 0:8], AF.Sigmoid, R=["ba"], W=["beta"])
            tt("dve", gx[:C, :], ba[:C, 8:16], dtbb[:C, l, :], ALU.add, R=["ba", "dtbb"], W=["gx"])
            ts("dve", gm[:C, :], gx[:C, :], 0.0, ALU.max, R=["gx"], W=["gm"])
            stt("dve", gn[:C, :], gm[:C, :], -2.0, gx[:C, :], ALU.mult, ALU.add, R=["gm", "gx"], W=["gn"])
            act(gn[:C, :], gn[:C, :], AF.Exp, R=["gn"], W=["gn"])
            act(gn[:C, :], gn[:C, :], AF.Ln, R=["gn"], W=["gn"], bias=1.0, scale=1.0)
            tt("dve", gn[:C, :], gn[:C, :], gm[:C, :], ALU.add, R=["gn", "gm"], W=["gn"])
            tt("dve", gg[:C, :], gn[:C, :], nexpA[:C, l, :], ALU.mult, R=["gn", "nexpA"], W=["gg"])
            yield
            P.stage("d_g_%d" % l)
            ph, pk = ps_half()
            mm(ph[:C, 0:8], trif[:C, :C], gg[:C, :], True, True, R=["trif", "gg"], W=[pk], inc=False)
            mm(ph[:C, 8:16], trigt[:C, :C], gg[:C, :], True, True, R=["trigt", "gg"], W=[pk])
            act(egc[:C, :], ph[:C, 0:16], AF.Exp, R=[pk], W=["egc"])
            stt("dve", nbe[:C, :], beta[:C, :], -1.0, egc[:C, 0:8], ALU.mult, ALU.mult, R=["beta", "egc"], W=["nbe"])
            P.stage("d_col_%d" % l)
            tt("pool", gtri[:C, :, :C], trif[:C, :C].unsqueeze(1).to_broadcast([C, 8, C]),
               gg[:C, :].unsqueeze(2).to_broadcast([C, 8, C]), ALU.mult, R=["trif", "gg"], W=["B1"])
            ts("dve", ngb[:C, :, :C], gg[:C, :].unsqueeze(2).to_broadcast([C, 8, C]), -1.0, ALU.mult,
               R=["gg"], W=["B2"])
            yield
            P.stage("d_rhs_%d" % l)
            pa, pak = ps_pair()
            pb, pbk = ps_pair()
            pc, pck = ps_pair()

            pav, pbv, pcv = w8(pa, 128), w8(pb, C), w8(pc, C)
            for hh in range(nhalf):
                hs = slice(hh * hp, (hh + 1) * hp)
                ka, kb_, kc_ = pak[hh if nhalf > 1 else 0], pbk[hh if nhalf > 1 else 0], pck[hh if nhalf > 1 else 0]
                mm(pav[:, hs, :], onesf[:C, :], gtri[:C, hs, :C], True, True, R=["onesf", "B1"], W=[ka])
                mm(pbv[:, hs, :], onesf[:C, :C], gtri[:C, hs, :C], True, False, R=["onesf", "B1"], W=[kb_])
                mm(pbv[:, hs, :], trif[:C, :C], ngb[:C, hs, :C], False, False, R=["trif", "B2"], W=[kb_])
                mm(pbv[:, hs, :], identb[:C, :C], maskb[:C, hs, :C], False, True, R=["identb", "maskb"], W=[kb_])
                mm(pcv[:, hs, :], nonesf[:C, :C], gtri[:C, hs, :C], True, False, R=["nonesf", "B1"], W=[kc_])
                mm(pcv[:, hs, :], ntrif[:C, :C], ngb[:C, hs, :C], False, False, R=["ntrif", "B2"], W=[kc_])
                mm(pcv[:, hs, :], identb[:C, :C], maskc[:C, hs, :C], False, True, R=["identb", "maskc"], W=[kc_])
            act(egam[:, :, :C], pav, AF.Exp, R=pak, W=["egam"])
            act(DTi[:C, :, :C], pbv, AF.Exp, R=pbk, W=["B1"])
            act(Ds[:C, :, :C], pcv, AF.Exp, R=pck, W=["B2"])
            tt("pool", NA[1][:C, :, :C], identf[:C, :C].unsqueeze(1).to_broadcast([C, 8, C]),
               beta[:C, :].unsqueeze(2).to_broadcast([C, 8, C]), ALU.mult, R=["identf", "beta"], W=["NA1"])
            pbb, pbbk = ps_pair()
            pbbv = w8(pbb, C)
            for hh in range(nhalf):
                hs = slice(hh * hp, (hh + 1) * hp)
                mm(pbbv[:, hs, :], onesf[:C, :C], NA[1][:C, hs, :C], True, True, R=["onesf", "NA1"],
                   W=[pbbk[hh if nhalf > 1 else 0]])
            cp("act", NA[1][:C, :, :C], pbbv, R=pbbk, W=["NA1"])
            yield

        def delta_chunk(l, T, C, c0, hoisted=False, fillers=()):
            hp = min(8, 512 // C)
            nhalf = 8 // hp

            def wide(ps, C2):
                return ps[:, :].rearrange("p (h x) -> p h x", h=8) if False else None

            qv = qkT[:, 0:8, c0:c0 + C]
            kv = qkT[:, 8:16, c0:c0 + C]
            gtri, DTi = B1, B1
            ngb, Ds, Atmp = B2, B2, B2
            Yf, rtmp = B4, B4

            def w8(ps_ap, np_):
                return ps_ap[:np_, :8 * C].rearrange("p (h x) -> p h x", h=8)
            if not hoisted:
                for _ in delta_prep(l, T, C, c0):
                    pass
            P.stage("d_exp_%d" % l)
            pkk, pkkk = ps_pair()
            pqk, pqkk = ps_pair()
            pkkv, pqkv = w8(pkk, C), w8(pqk, C)
            for h in range(8):
                k_ = pkkk[(h // hp) if nhalf > 1 else 0]
                mm(pkkv[:, h, :], kv[:, h, :], kv[:, h, :], True, True, R=["qkT"], W=[k_], inc=(h % hp == hp - 1))
            for h in range(8):
                k_ = pqkk[(h // hp) if nhalf > 1 else 0]
                mm(pqkv[:, h, :], kv[:, h, :], qv[:, h, :], True, True, R=["qkT"], W=[k_], inc=(h % hp == hp - 1))
            tt("dve", PT[:C, :, :C], pqkv, DTi[:C, :, :C], ALU.mult, R=pqkk + ["B1"], W=["PT"])
            tt("dve", Atmp[:C, :, :C], pkkv, Ds[:C, :, :C], ALU.mult, R=pkkk + ["B2"], W=["B2"])
            P.stage("d_kk_%d" % l)
            NTA = [B2, NTA1]
            NTk = ["B2", "NTA1"]
            NAk = ["NA0", "NA1"]
            tt("dve", B2[:C, :, :C], B2[:C, :, :C], beta[:C, :].unsqueeze(2).to_broadcast([C, 8, C]), ALU.mult,
               R=["B2", "beta"], W=["B2"])
            P.stage("d_A_%d" % l)
            tt("dve", NA[0][:C, :, :C], pkkv, DTi[:C, :, :C], ALU.mult, R=pkkk + ["B1"], W=["NA0"])
            tt("dve", NA[0][:C, :, :C], NA[0][:C, :, :C], NA[1][:C, :, :C], ALU.mult, R=["NA0", "NA1"], W=["NA0"])
            tt("dve", NA[0][:C, :, :C], NA[0][:C, :, :C], offd[:C, :C].unsqueeze(1).to_broadcast([C, 8, C]), ALU.mult,
               R=["NA0", "offd"], W=["NA0"])
            P.stage("d_tr_%d" % l)
            tt("dve", Yf[:C, :, :C], identf[:C, :C].unsqueeze(1).to_broadcast([C, 8, C]), NA[0][:C, :, :C], ALU.subtract,
               R=["identf", "NA0"], W=["B4"])
            nlev = max(1, int(np.ceil(np.log2(C))) - 1)
            cur = 0
            for lev in range(nlev):
                nxt = 1 - cur
                lastlev = (lev == nlev - 1)
                Nc, NTc = NA[cur], NTA[cur]
                Nck, NTck = NAk[cur], NTk[cur]
                p2t, p2tk = ps_pair()
                p2tv = w8(p2t, C)
                for h in range(8):
                    k_ = p2tk[(h // hp) if nhalf > 1 else 0]
                    mm(p2tv[:, h, :], Nc[:C, h, :C], NTc[:C, h, :C], True, True, R=[Nck, NTck], W=[k_],
                       inc=(h % hp == hp - 1))
                if not lastlev:
                    p2, p2k = ps_pair()
                    p2v = w8(p2, C)
                    for h in range(8):
                        k_ = p2k[(h // hp) if nhalf > 1 else 0]
                        mm(p2v[:, h, :], NTc[:C, h, :C], Nc[:C, h, :C], True, True, R=[Nck, NTck], W=[k_],
                           inc=(h % hp == hp - 1))
                cp("act", NTA[nxt][:C, :, :C], p2tv, R=p2tk, W=[NTk[nxt]])
                if not lastlev:
                    cp("dve", NA[nxt][:C, :, :C], p2v, R=p2k, W=[NAk[nxt]])
                py, pyk = ps_pair()
                pyv = w8(py, C)
                for h in range(8):
                    k_ = pyk[(h // hp) if nhalf > 1 else 0]
                    mm(pyv[:, h, :], NTA[nxt][:C, h, :C], Yf[:C, h, :C], True, True, R=[NTk[nxt], "B4"], W=[k_],
                       inc=(h % hp == hp - 1))
                tt("dve", Yf[:C, :, :C], Yf[:C, :, :C], pyv, ALU.add, R=["B4"] + pyk, W=["B4"])
                cur = nxt
            cp("pool", Yb[:C, :, :C], Yf[:C, :, :C], R=["B4"], W=["Yb"])
            P.stage("d_neu_%d" % l)
            pt_, ptk = ps_half()
            ptv = pt_.bitcast(BF16)[:C, :1024].rearrange("p (h x) -> p h x", h=8)
            for h in range(8):
                tr(ptv[:, h, :], vT[:, h, c0:c0 + C], identb[:], R=["vT", "identb"], W=[ptk])
            tt("dve", bv[:C, :, :], ptv, beta[:C, :].unsqueeze(2).to_broadcast([C, 8, 128]), ALU.mult,
               R=[ptk, "beta"], W=["bv"])
            pt2, pt2k = ps_half()
            pt2v = pt2.bitcast(BF16)[:C, :1024].rearrange("p (h x) -> p h x", h=8)
            for h in range(8):
                tr(pt2v[:, h, :], kv[:, h, :], identb[:], R=["qkT", "identb"], W=[pt2k])
            tt("dve", kg[:C, :, :], pt2v, egc[:C, 8:16].unsqueeze(2).to_broadcast([C, 8, 128]), ALU.mult,
               R=[pt2k, "egc"], W=["kg"])
            tt("pool", qg[:, :, :C], qv, egam[:, :, :C], ALU.mult, R=["qkT", "egam"], W=["qg"])
            P.stage("d_tok_%d" % l)
            Sl, Sbl = S[l], Sb[l]
            Sk, Sbk = "S%d" % l, "Sb%d" % l
            pks_, pksk = ps_pair()
            pksv = pks_[:C, :].rearrange("p (h x) -> p h x", h=8)
            for h in range(8):
                mm(pksv[:, h, :], kv[:, h, :], Sbl[:, h, :], True, True, R=["qkT", Sbk], W=[pksk[h // 4]],
                   inc=(h % 4 == 3))
            for f_ in fillers:
                f_(0)
            tt("dve", rtmp[:C, :, :], pksv, nbe[:C, :].unsqueeze(2).to_broadcast([C, 8, 128]), ALU.mult,
               R=pksk + ["nbe"], W=["B4"])
            tt("dve", rb[:C, :, :], rtmp[:C, :, :], bv[:C, :, :], ALU.add, R=["B4", "bv"], W=["rb"])
            pvn, pvnk = ps_pair()
            pvnv = pvn[:C, :].rearrange("p (h x) -> p h x", h=8)
            for h in range(8):
                mm(pvnv[:, h, :], Yb[:C, h, :C], rb[:C, h, :], True, True, R=["Yb", "rb"], W=[pvnk[h // 4]],
                   inc=(h % 4 == 3))
            for f_ in fillers:
                f_(1)
            cp("act", vnw[:C, :, :], pvnv, R=pvnk, W=["vnw"])
            po, pok = ps_pair()
            pov = w8(po, 128)
            for h in range(8):
                k_ = pok[(h // hp) if nhalf > 1 else 0]
                mm(pov[:, h, :], Sbl[:, h, :], qg[:, h, :C], True, False, R=[Sbk, "qg"], W=[k_])
                mm(pov[:, h, :], vnw[:C, h, :], PT[:C, h, :C], False, True, R=["vnw", "PT"], W=[k_],
                   inc=(h % hp == hp - 1))
            cp("act", oT[:, :, c0:c0 + C], pov, R=pok, W=["oT"])
            psn, psnk = ps_pair()
            psnv = psn[:, :].rearrange("p (h x) -> p h x", h=8)
            for h in range(8):
                mm(psnv[:, h, :], kg[:C, h, :], vnw[:C, h, :], True, True, R=["kg", "vnw"], W=[psnk[h // 4]],
                   inc=(h % 4 == 3))
            tt("dve", Sl[:], Sl[:], egam[:, :, C - 1:C].to_broadcast([128, 8, 128]), ALU.mult, R=[Sk, "egam"], W=[Sk])
            tt("dve", Sl[:], Sl[:], psnv, ALU.add, R=[Sk] + psnk, W=[Sk])
            cp("act", Sbl[:], Sl[:], R=[Sk], W=[Sbk])

        seqs = [
            {"kind": "p", "idx": 0, "L": SEQ, "T": 128, "C": 128},
            {"kind": "p", "idx": 1, "L": SEQ, "T": 128, "C": 128},
            {"kind": "s", "idx": 0, "L": DEC_SEQ, "T": DEC_SEQ, "C": DEC_SEQ},
        ]
        for sq in seqs:
            ntile = sq["L"] // sq["T"]
            for _ in range(ntile):
                for l in range(DEPTH):
                    wplan.extend(layer_plan(l))

        for sq in seqs:
            T, C = sq["T"], sq["C"]
            ntile = sq["L"] // T
            for l in range(DEPTH):
                if sq["kind"] == "p":
                    P.op("pool", "memset", S[l][:], 0.0, W=["S%d" % l])
                    P.op("pool", "memset", Sb[l][:], 0.0, W=["Sb%d" % l])
                    P.op("pool", "memset", halo[l][:], 0.0, W=["halo%d" % l])
                else:
                    P.dma("sp", S[l][:], sdelta[l].rearrange("h d v -> d h v"), R=[], W=["S%d" % l], semkey="dma_Sin%d" % l)
                    cp("act", Sb[l][:], S[l][:], R=["S%d" % l], W=["Sb%d" % l])
                    P.dma("sp", cvf[:], sconv[l], R=[], W=["cvf"], semkey="dma_cvfin")
                    cp("dve", halo[l][:], cvf[:], R=["cvf"], W=["halo%d" % l])
            for ti in range(ntile):
                t0 = ti * T
                src = (xT_p[sq["idx"], :, :, t0:t0 + T] if sq["kind"] == "p" else xT_s[0, :, :, :])
                P.dma("sp", xT[:, :, :T], src.rearrange("kc p t -> p kc t"), R=[], W=["xT"], semkey="dma_xT")
                for l in range(DEPTH):
                    layer(l, T, C, sq, ti == 0, ti == ntile - 1)
                tt("dve", sqb[:, 0:8, :T], xT[:, :, :T], xT[:, :, :T], ALU.mult, R=["xT"], W=["sqb"])
                ph, pk = ps_half()
                for kc in range(8):
                    mm(ph[:, :T], onesb[:], sqb[:, kc, :T], kc == 0, kc == 7, R=["onesb", "sqb"], W=[pk])
                act(rstd[:, :T], ph[:, :T], AF.Ln, R=[pk, "epsc"], W=["rstd"], bias=epsc[:], scale=1.0 / D)
                act(rstd[:, :T], rstd[:, :T], AF.Exp, R=["rstd"], W=["rstd"], scale=-0.5)
                for kc in range(8):
                    stt("dve", oT[:, kc, :T], xT[:, kc, :T], fnc[:, kc:kc + 1], rstd[:, :T], ALU.mult, ALU.mult,
                        R=["xT", "rstd", "fnc"], W=["oT"])
                dst = (yT_p[sq["idx"], :, :, t0:t0 + T] if sq["kind"] == "p" else yT_s[0, :, :, :])
                P.dma("sp", dst.rearrange("kc p t -> p kc t"), oT[:, :, :T], R=["oT"], W=[], semkey="dma_y")
                P.stage("tile_end_%s_%d" % (sq["kind"] + str(sq["idx"]), ti))
            for l in range(DEPTH):
                dst = (ndl_p[l, sq["idx"]] if sq["kind"] == "p" else ndl_s[l, 0])
                P.dma("sp", dst.rearrange("h d v -> d h v"), S[l][:], R=["S%d" % l], W=[], semkey="dma_sout%d" % l)
        import os as _os
        if _os.environ.get("K_DELAY"):
            P.dead = False
            for _i in range(int(_os.environ["K_DELAY"])):
                P.op("pe", "matmul", pd[0][:, 0:512], lhsT=onesb[:], rhs=wbuf[0][:, 0, :], start=True, stop=True,
                     R=["onesb", "wbuf0"], W=["pd0a"], inc=(_i % 64 == 63))
            P.dead = True
        assert P.dead or wstate["consumed"] == len(wplan), (wstate, len(wplan))
        for sk, v in P.dma_cnt.items():
            P._need("sp", sk, v)
        for e_ in ("pe", "act", "dve", "pool"):
            P._need("sp", e_, P.cnt[e_])
        build.ninst = P.ninst
    return nc


def _prep_inputs(inputs, SEQ):
    f = lambda a: np.ascontiguousarray(np.asarray(a, dtype=np.float32))
    xp = f(inputs["x_prompt"])[:, :SEQ]
    xs = f(inputs["x_sample"])
    xpT = np.ascontiguousarray(xp.reshape(16, SEQ, 8, 128).transpose(0, 2, 3, 1))
    xsT = np.ascontiguousarray(xs.reshape(8, DEC_SEQ, 8, 128).transpose(0, 2, 3, 1))
    sc = f(inputs["state_conv"])
    scT = np.ascontiguousarray(sc.reshape(DEPTH, 8, 3, 24, 128).transpose(1, 0, 4, 3, 2))
    sd = f(inputs["state_delta"])
    rep = lambda a: np.ascontiguousarray(np.broadcast_to(a[None], (128,) + a.shape))
    common = {
        "ln1c": np.ascontiguousarray(f(inputs["ln1"]).reshape(DEPTH, 8, 128).transpose(2, 0, 1)),
        "ln2c": np.ascontiguousarray(f(inputs["ln2"]).reshape(DEPTH, 8, 128).transpose(2, 0, 1)),
        "fnc": np.ascontiguousarray(f(inputs["final_norm"]).reshape(8, 128).transpose(1, 0)),
        "cwc": np.ascontiguousarray(f(inputs["conv_w"]).reshape(DEPTH, 24, 128, CW).transpose(2, 0, 1, 3)),
        "onc": np.ascontiguousarray(f(inputs["o_norm"]).transpose(1, 0)),
        "algb": rep(f(inputs["a_ln_g"])),
        "albb": rep(f(inputs["a_ln_b"])),
        "wstT": np.ascontiguousarray(f(inputs["w_s"]).transpose(3, 0, 1, 2)),
        "bsr": np.ascontiguousarray(f(inputs["b_s"]).reshape(1, DEPTH, D)),
        "alogb": rep(f(inputs["a_log"])),
        "dtbb": rep(f(inputs["dt_bias"])),
        "w_in": f(inputs["w_in"]), "p_a": f(inputs["p_a"]), "p_b": f(inputs["p_b"]), "w_o": f(inputs["w_o"]),
        "w_up": f(inputs["w_up"]), "w_down": f(inputs["w_down"]),
    }
    maps = []
    for c in range(NCORES):
        m = dict(common)
        m["xT_p"] = np.ascontiguousarray(xpT[2 * c:2 * c + 2])
        m["xT_s"] = np.ascontiguousarray(xsT[c:c + 1])
        m["sconv"] = np.ascontiguousarray(scT[c])
        m["sdelta"] = np.ascontiguousarray(sd[:, c])
        maps.append(m)
    return maps


def _assemble(results, SEQ):
    yp = np.concatenate([r["yT_p"] for r in results], axis=0)
    y_prompt = np.ascontiguousarray(yp.transpose(0, 3, 1, 2).reshape(16, SEQ, D))
    ys = np.concatenate([r["yT_s"] for r in results], axis=0)
    y_sample = np.ascontiguousarray(ys.transpose(0, 3, 1, 2).reshape(8, DEC_SEQ, D))
    cvp = np.concatenate([r["ncv_p"] for r in results], axis=1)
    new_conv_prompt = np.ascontiguousarray(cvp.transpose(0, 1, 4, 3, 2).reshape(DEPTH, 16, 3, 3072))
    new_delta_prompt = np.ascontiguousarray(np.concatenate([r["ndl_p"] for r in results], axis=1))
    cvs = np.concatenate([r["ncv_s"] for r in results], axis=1)
    new_conv_sample = np.ascontiguousarray(cvs.transpose(0, 1, 4, 3, 2).reshape(DEPTH, 8, 3, 3072))
    new_delta_sample = np.ascontiguousarray(np.concatenate([r["ndl_s"] for r in results], axis=1))
    new_gv = np.ascontiguousarray(np.concatenate([r["ngv_s"] for r in results], axis=1))
    outs = (y_prompt, y_sample, new_conv_prompt, new_delta_prompt, new_conv_sample, new_delta_sample, new_gv)
    return tuple(np.asarray(o, dtype=np.float32) for o in outs)


def run(inputs, SEQ, stop_at=None, lite=False, start_at=None):
    nc = build(SEQ, stop_at, lite, start_at)
    maps = _prep_inputs(inputs, SEQ)
    if lite:
        for m in maps:
            for k in ("w_in", "p_a", "p_b", "w_o", "w_up", "w_down"):
                m.pop(k)
    res = run_bass_kernel_spmd(nc, maps, core_ids=list(range(NCORES)))
    return _assemble(res.results, SEQ)


def kernel(**inputs):
    return run(inputs, SEQ_FULL)
```

```python
import numpy as np
from contextlib import ExitStack
import concourse.bass as bass
import concourse.mybir as mybir
from concourse.bass_utils import run_bass_kernel_spmd

F32 = mybir.dt.float32
BF16 = mybir.dt.bfloat16
AF = mybir.ActivationFunctionType
ALU = mybir.AluOpType
AX = mybir.AxisListType

NCORES = 8
D = 1024
DEPTH = 2
NIN = 8208
DFF = 4096
EPS = 1e-6
NEG = -30000.0
import os as _os0
NOSELF = set((_os0.environ.get("K_NOSELF") or "").split(",")) - {""}
SEQ_FULL = 4096
DEC_SEQ = 16
CW = 4


class Prog:
    def __init__(self, nc, es):
        self.nc = nc
        self.es = es
        self.eng = {"pe": nc.tensor, "act": nc.scalar, "dve": nc.vector, "pool": nc.gpsimd, "sp": nc.sync}
        self.sem = {k: es.enter_context(nc.semaphore("sem_" + k)) for k in self.eng}
        self.cnt = {k: 0 for k in self.eng}
        self.waited = {k: {} for k in self.eng}
        self.lastw = {}
        self.readers = {}
        self.dma_sems = {}
        self.dma_cnt = {}
        self.ninst = 0
        self.dead = False
        self.stop_at = None

    def stage(self, name):
        if self.stop_at is not None and name == self.stop_at:
            self.dead = True
            self.stopped = True
        elif getattr(self, "start_at", None) is not None and not getattr(self, "stopped", False):
            if name == "setup":
                self.dead = True
            elif name == self.start_at:
                self.dead = False

    def _need(self, e, semkey, val):
        if val <= 0:
            return
        w = self.waited[e]
        if w.get(semkey, 0) >= val:
            return
        w[semkey] = val
        s = self.sem[semkey] if semkey in self.sem else self.dma_sems[semkey]
        self.eng[e].wait_ge(s, val)
        self.ninst += 1

    def _deps(self, e, reads, writes):
        skip_self = (e == "pe") or (e in NOSELF)
        for k in reads:
            lw = self.lastw.get(k)
            if lw and not (skip_self and lw[0] == e):
                self._need(e, lw[0], lw[1])
        for k in writes:
            lw = self.lastw.get(k)
            if lw and not (skip_self and lw[0] == e):
                self._need(e, lw[0], lw[1])
            for sk, v in self.readers.get(k, {}).items():
                if not (skip_self and sk == e):
                    self._need(e, sk, v)

    def _commit(self, semkey, val, reads, writes):
        for k in writes:
            self.lastw[k] = (semkey, val)
            self.readers[k] = {}
        for k in reads:
            r = self.readers.setdefault(k, {})
            if r.get(semkey, 0) < val:
                r[semkey] = val

    def op(self, e, fn, *args, R=(), W=(), inc=True, **kw):
        if self.dead:
            return None
        W = list(W) + [k for k in R if k.startswith("pd") and k not in W]
        self._deps(e, R, W)
        ins = getattr(self.eng[e], fn)(*args, **kw)
        self.ninst += 1
        if inc:
            self.cnt[e] += 1
            ins.then_inc(self.sem[e], 1)
            self._commit(e, self.cnt[e], R, W)
        else:
            self._commit(e, self.cnt[e] + 1, R, W)
        return ins

    def dma(self, e, out, in_, R, W, semkey):
        if self.dead:
            return None
        if semkey not in self.dma_sems:
            self.dma_sems[semkey] = self.es.enter_context(self.nc.semaphore(semkey))
            self.dma_cnt[semkey] = 0
        self._deps(e, R, W)
        ins = self.eng[e].dma_start(out=out, in_=in_)
        self.dma_cnt[semkey] += 16
        ins.then_inc(self.dma_sems[semkey], 16)
        self._commit(semkey, self.dma_cnt[semkey], R, W)
        self.ninst += 1
        return ins

    def finish(self, e="sp"):
        for k, (sk, v) in list(self.lastw.items()):
            self._need(e, sk, v)


def build(SEQ, stop_at=None, lite=False, start_at=None):
    nc = bass.Bass("TRN2", target_bir_lowering=False)
    es = ExitStack()

    def din(name, shape, dt=F32):
        return nc.dram_tensor(name, list(shape), dt, kind="ExternalInput").ap()

    def dout(name, shape, dt=F32):
        return nc.dram_tensor(name, list(shape), dt, kind="ExternalOutput").ap()

    def dscr(name, shape, dt=BF16):
        return nc.dram_tensor(name, list(shape), dt, kind="Internal").ap()

    xT_p = din("xT_p", [2, 8, 128, SEQ])
    xT_s = din("xT_s", [1, 8, 128, DEC_SEQ])
    sconv = din("sconv", [DEPTH, 128, 24, 3])
    sdelta = din("sdelta", [DEPTH, 8, 128, 128])
    ln1_d = din("ln1c", [128, DEPTH, 8])
    ln2_d = din("ln2c", [128, DEPTH, 8])
    fn_d = din("fnc", [128, 8])
    cw_d = din("cwc", [128, DEPTH, 24, 4])
    on_d = din("onc", [128, DEPTH])
    alg_d = din("algb", [128, DEPTH, D])
    alb_d = din("albb", [128, DEPTH, D])
    wst_d = din("wstT", [128, DEPTH, 8, 128])
    bs_d = din("bsr", [1, DEPTH, D])
    alog_d = din("alogb", [128, DEPTH, 8])
    dtb_d = din("dtbb", [128, DEPTH, 8])
    wdecl = (lambda n, s_: dscr(n + "_lite", s_, F32)) if lite else din
    w_in_d = wdecl("w_in", [DEPTH, D, NIN])
    p_a_d = wdecl("p_a", [DEPTH, D, D])
    p_b_d = wdecl("p_b", [DEPTH, D, D])
    w_o_d = wdecl("w_o", [DEPTH, D, D])
    w_up_d = wdecl("w_up", [DEPTH, D, DFF])
    w_dn_d = wdecl("w_down", [DEPTH, DFF, D])

    yT_p = dout("yT_p", [2, 8, 128, SEQ])
    yT_s = dout("yT_s", [1, 8, 128, DEC_SEQ])
    ncv_p = dout("ncv_p", [DEPTH, 2, 128, 24, 3])
    ndl_p = dout("ndl_p", [DEPTH, 2, 8, 128, 128])
    ncv_s = dout("ncv_s", [DEPTH, 1, 128, 24, 3])
    ndl_s = dout("ndl_s", [DEPTH, 1, 8, 128, 128])
    ngv_s = dout("ngv_s", [DEPTH, 1, DEC_SEQ, D])

    w_in_b = dscr("w_in_b", [DEPTH, D, NIN])
    p_a_b = dscr("p_a_b", [DEPTH, D, D])
    p_b_b = dscr("p_b_b", [DEPTH, D, D])
    w_o_b = dscr("w_o_b", [DEPTH, D, D])
    w_up_b = dscr("w_up_b", [DEPTH, D, DFF])
    w_dn_b = dscr("w_dn_b", [DEPTH, DFF, D])

    with es:
        P = Prog(nc, es)
        P.stop_at = stop_at
        P.start_at = start_at

        def sb(name, shape, dt):
            return es.enter_context(nc.sbuf_tensor(name, list(shape), dt))

        TM = 128

        identf = sb("identf", [128, 128], F32)
        identb = sb("identb", [128, 128], BF16)
        trif = sb("trif", [128, 128], F32)
        ntrif = sb("ntrif", [128, 128], F32)
        trigt = sb("trigt", [128, 128], F32)
        onesf = sb("onesf", [128, 128], F32)
        offd = sb("offd", [128, 128], F32)
        nonesf = sb("nonesf", [128, 128], F32)
        onesb = sb("onesb", [128, 128], BF16)
        maskb = sb("maskb", [128, 8, 128], BF16)
        maskc = sb("maskc", [128, 8, 128], BF16)
        epsc = sb("epsc", [128, 1], F32)
        epsq = sb("epsq", [128, 1], F32)
        ln1c = sb("ln1c_s", [128, DEPTH, 8], F32)
        ln2c = sb("ln2c_s", [128, DEPTH, 8], F32)
        fnc = sb("fnc_s", [128, 8], F32)
        cwc = sb("cwc_s", [128, DEPTH, 24, 4], F32)
        onc = sb("onc_s", [128, DEPTH], F32)
        algb = sb("algb_s", [128, DEPTH, D], F32)
        albb = sb("albb_s", [128, DEPTH, D], F32)
        wstb = sb("wstb", [128, DEPTH, 8, 128], BF16)
        bsb = sb("bsb", [1, DEPTH, D], BF16)
        alogb = sb("alogb_s", [128, DEPTH, 8], F32)
        nexpA = sb("nexpA", [128, DEPTH, 8], F32)
        dtbb = sb("dtbb_s", [128, DEPTH, 8], F32)
        wba = sb("wba", [128, DEPTH, 8, 16], BF16)

        NA = [sb("NA%d" % i, [128, 8, 128], F32) for i in range(2)]
        NTA1 = sb("NTA1", [128, 8, 128], F32)
        xT = sb("xT", [128, 8, TM], F32)
        hT = sb("hT", [128, 8, TM], BF16)
        sqb = sb("sqb", [128, 16, TM], BF16)
        rstd = sb("rstd", [128, 4 * TM], F32)
        uT = sb("uT", [128, 8, TM], BF16)
        vf = sb("vf", [128, D], F32)
        vc = sb("vc", [128, D], F32)
        vnb = sb("vnb", [128, D], BF16)
        st4 = sb("st4", [128, 8], F32)
        qkvpre = sb("qkvpre", [128, 24, 3 + TM], BF16)
        halo = [sb("halo%d" % l, [128, 24, 3], BF16) for l in range(DEPTH)]
        cvf = sb("cvf", [128, 24, 3], F32)
        qkf = sb("qkf", [128, 16, TM], F32)
        qkT = sb("qkT", [128, 16, TM], BF16)
        vT = sb("vT", [128, 8, TM], BF16)
        zsT = sb("zsT", [128, 8, TM], BF16)
        sga = sb("sga", [128, 8, TM], BF16)
        sgb = sb("sgb", [128, 8, TM], BF16)
        oT = sb("oT", [128, 8, TM], F32)
        zbT = sb("zbT", [128, 8, TM], BF16)
        mixT = sb("mixT", [128, 8, TM], BF16)
        mtf = sb("mtf", [128, 4, TM], F32)
        hid = [sb("hid%d" % i, [128, 4, TM], BF16) for i in range(2)]
        rl = sb("rl", [128, 4, TM], BF16)
        rl2 = sb("rl2", [128, 4, TM], BF16)
        S = [sb("S%d" % l, [128, 8, 128], F32) for l in range(DEPTH)]
        Sb = [sb("Sb%d" % l, [128, 8, 128], BF16) for l in range(DEPTH)]
        ba = sb("ba", [128, 16], F32)
        beta = sb("beta", [128, 8], F32)
        gx = sb("gx", [128, 8], F32)
        gm = sb("gm", [128, 8], F32)
        gn = sb("gn", [128, 8], F32)
        gg = sb("gg", [128, 8], F32)
        egc = sb("egc", [128, 16], F32)
        nbe = sb("nbe", [128, 8], F32)
        B1 = sb("B1", [128, 8, 128], F32)
        B2 = sb("B2", [128, 8, 128], F32)
        egam = sb("egam", [128, 8, 128], F32)
        B4 = sb("B4", [128, 8, 128], F32)
        PT = sb("PT", [128, 8, 128], BF16)
        Yb = sb("Yb", [128, 8, 128], BF16)
        bv = sb("bv", [128, 8, 128], BF16)
        kg = sb("kg", [128, 8, 128], BF16)
        qg = sb("qg", [128, 8, 128], BF16)
        rb = sb("rb", [128, 8, 128], BF16)
        vnw = sb("vnw", [128, 8, 128], BF16)
        NWB = 6
        wbuf = [sb("wbuf%d" % i, [128, 8, 512], BF16) for i in range(NWB)]

        def cdma(dst, src, key):
            P.dma("sp", dst, src, R=[], W=[key], semkey="dma_c_" + key)

        cdma(ln1c[:], ln1_d, "ln1c"); cdma(ln2c[:], ln2_d, "ln2c"); cdma(fnc[:], fn_d, "fnc")
        cdma(cwc[:], cw_d, "cwc"); cdma(onc[:], on_d, "onc"); cdma(algb[:], alg_d, "algb")
        cdma(albb[:], alb_d, "albb")
        cdma(alogb[:], alog_d, "alogb"); cdma(dtbb[:], dtb_d, "dtbb")

        def cast_w(dst, src, rows, key):
            for l in range(DEPTH):
                for r0 in range(0, rows, 128):
                    P.dma("pool", dst[l, r0:r0 + 128, :], src[l, r0:r0 + 128, :], R=[], W=[key], semkey="dma_" + key)

        cast_w(w_in_b, w_in_d, D, "w_in_b")
        cast_w(p_a_b, p_a_d, D, "p_a_b")
        cast_w(p_b_b, p_b_d, D, "p_b_b")
        cast_w(w_o_b, w_o_d, D, "w_o_b")
        cast_w(w_up_b, w_up_d, D, "w_up_b")
        cast_w(w_dn_b, w_dn_d, DFF, "w_dn_b")

        def gp(fn, *a, R=(), W=(), **kw):
            return P.op("pool", fn, *a, R=R, W=W, **kw)

        gp("memset", onesf[:], 1.0, W=["onesf"])
        gp("memset", nonesf[:], -1.0, W=["nonesf"])
        gp("memset", onesb[:], 1.0, W=["onesb"])
        gp("memset", epsc[:], EPS, W=["epsc"])
        gp("memset", epsq[:], EPS * 128.0, W=["epsq"])
        gp("affine_select", out=identf[:], in_=onesf[:], pattern=[[1, 128]], compare_op=ALU.is_equal, fill=0.0,
           base=0, channel_multiplier=-1, R=["onesf"], W=["identf"])
        gp("affine_select", out=trif[:], in_=onesf[:], pattern=[[1, 128]], compare_op=ALU.is_ge, fill=0.0,
           base=0, channel_multiplier=-1, R=["onesf"], W=["trif"])
        gp("affine_select", out=ntrif[:], in_=nonesf[:], pattern=[[1, 128]], compare_op=ALU.is_ge, fill=0.0,
           base=0, channel_multiplier=-1, R=["nonesf"], W=["ntrif"])
        gp("affine_select", out=trigt[:], in_=onesf[:], pattern=[[-1, 128]], compare_op=ALU.is_gt, fill=0.0,
           base=0, channel_multiplier=1, R=["onesf"], W=["trigt"])
        P.op("dve", "tensor_copy", out=identb[:], in_=identf[:], R=["identf"], W=["identb"])
        P.op("dve", "tensor_tensor", out=offd[:], in0=onesf[:], in1=identf[:], op=ALU.subtract, R=["onesf", "identf"], W=["offd"])
        mt0, mt1, mt2 = B1[:, 0, :], B2[:, 0, :], B4[:, 0, :]
        gp("memset", mt0, NEG, W=["B1"])
        gp("affine_select", out=mt1, in_=mt0, pattern=[[-1, 128]], compare_op=ALU.is_gt, fill=0.0,
           base=0, channel_multiplier=1, R=["B1"], W=["B2"])
        gp("affine_select", out=mt2, in_=mt0, pattern=[[1, 128]], compare_op=ALU.is_ge, fill=0.0,
           base=0, channel_multiplier=-1, R=["B1"], W=["B4"])
        for h in range(8):
            P.op("dve", "tensor_copy", out=maskb[:, h, :], in_=mt1, R=["B2"], W=["maskb"])
            P.op("dve", "tensor_copy", out=maskc[:, h, :], in_=mt2, R=["B4"], W=["maskc"])
        for l in range(DEPTH):
            P.dma("sp", egam[:], wst_d[:, l, :, :], R=[], W=["egam"], semkey="dma_c2")
            for g in range(8):
                P.op("dve", "tensor_tensor", out=wstb[:, l, g, :], in0=egam[:, g, :], in1=trif[:], op=ALU.mult,
                     R=["egam", "trif"], W=["wstb"])
            P.dma("sp", vf[0:1, :], bs_d[0:1, l, :], R=[], W=["vf"], semkey="dma_c3")
            P.op("dve", "tensor_copy", out=bsb[0:1, l, :], in_=vf[0:1, :], R=["vf"], W=["bsb"])
        P.op("act", "activation", out=nexpA[:], in_=alogb[:], func=AF.Exp, R=["alogb"], W=["nexpA"])
        P.op("dve", "tensor_scalar", out=nexpA[:], in0=nexpA[:], scalar1=-1.0, scalar2=None, op0=ALU.mult,
             R=["nexpA"], W=["nexpA"])
        for l in range(DEPTH):
            P.dma("sp", wba[:, l, :, :], w_in_b[l, :, 6144:6160].rearrange("(kc p) n -> p kc n", p=128),
                  R=["w_in_b"], W=["wba"], semkey="dma_c_wba")

        P.stage("setup")
        pd = [es.enter_context(nc.psum_tensor("pd%d" % i, [128, 1024], F32)) for i in range(4)]
        ps_state = {"i": 0}

        def ps_half():
            i = ps_state["i"] % 6
            ps_state["i"] += 1
            t = pd[i // 2]
            return t[:, (i % 2) * 512:(i % 2) * 512 + 512], "pd%d%s" % (i // 2, "ab"[i % 2])

        def ps_pair():
            if ps_state["i"] % 2:
                ps_state["i"] += 1
            i = ps_state["i"] % 6
            ps_state["i"] += 2
            return pd[i // 2][:], ["pd%da" % (i // 2), "pd%db" % (i // 2)]

        wplan = []
        wstate = {"issued": 0, "consumed": 0}

        def layer_plan(l):
            pl = []
            wi = w_in_b[l].rearrange("(kc p) n -> p kc n", p=128)
            for g in range(2):
                pl.append(("u%d" % g, wi[:, :, g * 512:(g + 1) * 512], "w_in_b"))
            for g in range(2):
                pl.append(("v%d" % g, wi[:, :, 1024 + g * 512:1024 + (g + 1) * 512], "w_in_b"))
            for g in range(6):
                pl.append(("qkv%d" % g, wi[:, :, 2048 + g * 512:2048 + (g + 1) * 512], "w_in_b"))
            for g in range(2):
                pl.append(("z%d" % g, wi[:, :, 5120 + g * 512:5120 + (g + 1) * 512], "w_in_b"))
            for g in range(2):
                pl.append(("ga%d" % g, wi[:, :, 6160 + g * 512:6160 + (g + 1) * 512], "w_in_b"))
                pl.append(("gb%d" % g, wi[:, :, 7184 + g * 512:7184 + (g + 1) * 512], "w_in_b"))
                pl.append(("pa%d" % g, p_a_b[l].rearrange("(kc p) n -> p kc n", p=128)[:, :, g * 512:(g + 1) * 512], "p_a_b"))
                pl.append(("pb%d" % g, p_b_b[l].rearrange("(kc p) n -> p kc n", p=128)[:, :, g * 512:(g + 1) * 512], "p_b_b"))
            for g in range(2):
                pl.append(("wo%d" % g, w_o_b[l].rearrange("(kc p) n -> p kc n", p=128)[:, :, g * 512:(g + 1) * 512], "w_o_b"))
            def up_e(g):
                return ("up%d" % g, w_up_b[l].rearrange("(kc p) n -> p kc n", p=128)[:, :, g * 512:(g + 1) * 512], "w_up_b")

            def dn_e(g):
                return ("dn%d" % g, w_dn_b[l, g * 512:(g + 1) * 512, :].rearrange("(kc p) n -> p kc n", p=128), "w_dn_b")
            pl.append(up_e(0))
            for g in range(8):
                if g + 1 < 8:
                    pl.append(up_e(g + 1))
                pl.append(dn_e(g))
            return pl

        def w_issue_upto(n):
            while wstate["issued"] < min(n, len(wplan)):
                i = wstate["issued"]
                tag, view, srckey = wplan[i]
                slot = i % NWB
                if tag.startswith("dn"):
                    dst = wbuf[slot][:].rearrange("p a b -> p (a b)").rearrange("p (kc n) -> p kc n", kc=4)
                else:
                    dst = wbuf[slot][:]
                P.dma("sp", dst, view, R=[srckey], W=["wbuf%d" % slot], semkey="dma_wbuf%d" % slot)
                wstate["issued"] += 1

        def w_get(tag):
            i = wstate["consumed"]
            assert wplan[i][0] == tag, (wplan[i][0], tag)
            w_issue_upto(i + 1)
            wstate["consumed"] += 1
            slot = i % NWB
            return wbuf[slot], "wbuf%d" % slot, i

        def w_done(i):
            w_issue_upto(i + NWB)

        def mm(out, lhsT, rhs, start, stop, R, W, inc=None):
            if inc is None:
                inc = stop
            return P.op("pe", "matmul", out, lhsT=lhsT, rhs=rhs, start=start, stop=stop, R=R, W=W, inc=inc)

        def tr(out, in_, ident, R, W):
            return P.op("pe", "transpose", out=out, in_=in_, identity=ident, R=R, W=W)

        def act(out, in_, func, R, W, **kw):
            return P.op("act", "activation", out=out, in_=in_, func=func, R=R, W=W, **kw)

        def tt(e, out, in0, in1, op, R, W):
            return P.op(e, "tensor_tensor", out=out, in0=in0, in1=in1, op=op, R=R, W=W)

        def ts(e, out, in0, s1, op0, R, W, s2=None, op1=None):
            if op1 is None:
                return P.op(e, "tensor_scalar", out=out, in0=in0, scalar1=s1, scalar2=None, op0=op0, R=R, W=W)
            return P.op(e, "tensor_scalar", out=out, in0=in0, scalar1=s1, scalar2=s2, op0=op0, op1=op1, R=R, W=W)

        def stt(e, out, in0, scalar, in1, op0, op1, R, W):
            return P.op(e, "scalar_tensor_tensor", out=out, in0=in0, scalar=scalar, in1=in1, op0=op0, op1=op1,
                        R=R, W=W)

        def cp(e, out, in_, R, W):
            if e == "act":
                return act(out, in_, AF.Identity, R, W)
            return P.op(e, "tensor_copy", out=out, in_=in_, R=R, W=W)

        def rmsnorm_fm(T, gcol, gkey, dst, dstkey):
            tt("dve", sqb[:, 0:8, :T], xT[:, :, :T], xT[:, :, :T], ALU.mult, R=["xT"], W=["sqb"])
            ph, pk = ps_half()
            for kc in range(8):
                mm(ph[:, :T], onesb[:], sqb[:, kc, :T], kc == 0, kc == 7, R=["onesb", "sqb"], W=[pk])
            act(rstd[:, :T], ph[:, :T], AF.Ln, R=[pk, "epsc"], W=["rstd"], bias=epsc[:], scale=1.0 / D)
            act(rstd[:, :T], rstd[:, :T], AF.Exp, R=["rstd"], W=["rstd"], scale=-0.5)
            for kc in range(8):
                stt("dve", dst[:, kc, :T], xT[:, kc, :T], gcol[:, kc:kc + 1], rstd[:, :T], ALU.mult, ALU.mult,
                    R=["xT", "rstd", gkey], W=[dstkey])

        def fm_proj(T, wt, wkey, src, srckey, nblk, evac):
            bpb = 512 // T if T >= 128 else 4
            bpb = min(bpb, nblk)
            for m0 in range(0, nblk, bpb):
                ph, pk = ps_half()
                nb = min(bpb, nblk - m0)
                pv = ph[:, :nb * T].rearrange("p (b t) -> p b t", b=nb)
                for b in range(nb):
                    for kc in range(8):
                        mm(pv[:, b, :], wt[:, kc, (m0 + b) * 128:(m0 + b + 1) * 128], src[:, kc, :T], kc == 0, kc == 7,
                           R=[wkey, srckey], W=[pk])
                evac(m0, nb, pv, pk)

        def layer(l, T, C, seq, first_tile, last_tile):
            NCH = T // C
            rmsnorm_fm(T, ln1c[:, l, :], "ln1c", hT, "hT")
            prep_gen = delta_prep(l, T, C, 0) if NCH == 1 else None
            if prep_gen is not None:
                next(prep_gen)
            P.stage("norm1_%d" % l)
            for g in range(2):
                wt, wk, wi_ = w_get("u%d" % g)

                def ev(m0, nb, pv, pk, g=g):
                    act(uT[:, g * 4 + m0:g * 4 + m0 + nb, :T], pv, AF.Gelu, R=[pk], W=["uT"])
                fm_proj(T, wt, wk, hT, "hT", 4, ev)
                w_done(wi_)
            if prep_gen is not None:
                next(prep_gen)
            P.stage("u_%d" % l)
            wv = [w_get("v%d" % g) for g in range(2)]
            for c in range(NCH):
                c0 = c * C
                for g in range(2):
                    wt, wk, _ = wv[g]
                    ph, pk = ps_half()
                    for kc in range(8):
                        mm(ph[:C, :], hT[:, kc, c0:c0 + C], wt[:, kc, :], kc == 0, kc == 7, R=["hT", wk], W=[pk])
                    act(vf[:C, g * 512:(g + 1) * 512], ph[:C, :], AF.Gelu, R=[pk], W=["vf"])
                P.op("dve", "reduce_sum", out=st4[:C, 0:1], in_=vf[:C, :], axis=AX.X, R=["vf"], W=["st4"])
                ts("dve", st4[:C, 1:2], st4[:C, 0:1], -1.0 / D, ALU.mult, R=["st4"], W=["st4"])
                ts("dve", vc[:C, :], vf[:C, :], st4[:C, 1:2], ALU.add, R=["vf", "st4"], W=["vc"])
                tt("dve", vf[:C, :], vc[:C, :], vc[:C, :], ALU.mult, R=["vc"], W=["vf"])
                P.op("dve", "reduce_sum", out=st4[:C, 2:3], in_=vf[:C, :], axis=AX.X, R=["vf"], W=["st4"])
                act(st4[:C, 3:4], st4[:C, 2:3], AF.Ln, R=["st4", "epsc"], W=["st4"], bias=epsc[:C, :], scale=1.0 / D)
                act(st4[:C, 4:5], st4[:C, 3:4], AF.Exp, R=["st4"], W=["st4"], scale=-0.5)
                stt("dve", vc[:C, :], vc[:C, :], st4[:C, 4:5], algb[:C, l, :], ALU.mult, ALU.mult,
                    R=["vc", "st4", "algb"], W=["vc"])
                tt("dve", vc[:C, :], vc[:C, :], albb[:C, l, :], ALU.add, R=["vc", "albb"], W=["vc"])
                if seq["kind"] == "s":
                    P.dma("sp", ngv_s[l, 0, :, :], vc[:C, :], R=["vc"], W=[], semkey="dma_ngv")
                cp("act", vnb[:C, :], vc[:C, :], R=["vc"], W=["vnb"])
                pp, pks = ps_pair()
                ppv = pp.rearrange("p (g t) -> p g t", g=8)
                for g in range(8):
                    k = pks[g // 4]
                    mm(ppv[:, g, :C], vnb[:C, g * 128:(g + 1) * 128], wstb[:C, l, g, :C], True, False,
                       R=["vnb", "wstb"], W=[k])
                    mm(ppv[:, g, :C], onesb[0:1, :], bsb[0:1, l, g * 128:g * 128 + C], False, True,
                       R=["onesb", "bsb"], W=[k])
                tt("dve", uT[:, :, c0:c0 + C], ppv[:, :, :C], uT[:, :, c0:c0 + C], ALU.mult, R=pks + ["uT"], W=["uT"])
            for g in range(2):
                w_done(wv[g][2])
            if prep_gen is not None:
                for _ in prep_gen:
                    pass
            P.stage("gmlp_%d" % l)
            qp = qkvpre
            qk_ = "qkvpre"
            cp("pool", qp[:, :, 0:3], halo[l][:], R=["halo%d" % l], W=[qk_])
            for g in range(6):
                wt, wk, wi_ = w_get("qkv%d" % g)

                def ev(m0, nb, pv, pk, g=g):
                    cp("act", qp[:, g * 4 + m0:g * 4 + m0 + nb, 3:3 + T], pv, R=[pk], W=[qk_])
                    import os as _os
                    if last_tile and not _os.environ.get("K_NOCVF"):
                        cp("dve", cvf[:, g * 4 + m0:g * 4 + m0 + nb, :], pv[:, :, T - 3:T], R=[pk], W=["cvf"])
                fm_proj(T, wt, wk, hT, "hT", 4, ev)
                w_done(wi_)
            P.stage("qkv_%d" % l)
            if last_tile:
                dst = (ncv_p[l, seq["idx"]] if seq["kind"] == "p" else ncv_s[l, 0])
                P.dma("sp", dst, cvf[:], R=["cvf"], W=[], semkey="dma_cvf")
            P.stage("cvfdma_%d" % l)
            cp("pool", halo[l][:], qp[:, :, T:T + 3], R=[qk_], W=["halo%d" % l])
            P.stage("halo_%d" % l)
            tmpv = vf[:, :8 * T].rearrange("p (b t) -> p b t", b=8)
            vacc = vc[:, :8 * T].rearrange("p (b t) -> p b t", b=8)
            for gi in range(3):
                e_ = "dve" if gi < 2 else "pool"
                accv = qkf[:, gi * 8:(gi + 1) * 8, :T] if gi < 2 else vacc
                acck = "qkf" if gi < 2 else "vc"
                for j in range(CW):
                    wj = cwc[:, l, gi * 8:(gi + 1) * 8, j:j + 1].to_broadcast([128, 8, T])
                    src = qp[:, gi * 8:(gi + 1) * 8, j:j + T]
                    if j == 0:
                        tt(e_, accv, src, wj, ALU.mult, R=[qk_, "cwc"], W=[acck])
                    else:
                        tt(e_, tmpv, src, wj, ALU.mult, R=[qk_, "cwc"], W=["vf"])
                        tt(e_, accv, accv, tmpv, ALU.add, R=[acck, "vf"], W=[acck])
            P.stage("taps_%d" % l)
            act(qkf[:, :, :T], qkf[:, :, :T], AF.Silu, R=["qkf"], W=["qkf"])
            act(vT[:, :, :T], vacc, AF.Silu, R=["vc"], W=["vT"])
            P.stage("silu_%d" % l)
            tt("dve", sqb[:, :, :T], qkf[:, :, :T], qkf[:, :, :T], ALU.mult, R=["qkf"], W=["sqb"])
            hpg = 4
            for hg in range(0, 16, hpg):
                ph, pk = ps_half()
                pv = ph[:, :hpg * T].rearrange("p (b t) -> p b t", b=hpg)
                for b in range(hpg):
                    mm(pv[:, b, :], onesb[:], sqb[:, hg + b, :T], True, True, R=["onesb", "sqb"], W=[pk])
                rv = rstd[:, :hpg * T].rearrange("p (b t) -> p b t", b=hpg)
                if hg < 8:
                    act(rv, pv, AF.Ln, R=[pk, "epsq"], W=["rstd"], bias=epsq[:], scale=128.0)
                else:
                    act(rv, pv, AF.Ln, R=[pk, "epsc"], W=["rstd"], bias=epsc[:], scale=1.0)
                act(rv, rv, AF.Exp, R=["rstd"], W=["rstd"], scale=-0.5)
                tt("dve", qkT[:, hg:hg + hpg, :T], qkf[:, hg:hg + hpg, :T], rv, ALU.mult, R=["qkf", "rstd"], W=["qkT"])
            P.stage("conv_%d" % l)
            def zfill(g):
                wt, wk, wi_ = w_get("z%d" % g)

                def ev(m0, nb, pv, pk, g=g):
                    act(zsT[:, g * 4 + m0:g * 4 + m0 + nb, :T], pv, AF.Silu, R=[pk], W=["zsT"])
                fm_proj(T, wt, wk, hT, "hT", 4, ev)
                w_done(wi_)
            for c in range(NCH):
                delta_chunk(l, T, C, c * C, hoisted=(NCH == 1), fillers=([zfill] if c == NCH - 1 else []))

            P.stage("delta_%d" % l)
            tt("dve", sqb[:, 0:8, :T], oT[:, :, :T], oT[:, :, :T], ALU.mult, R=["oT"], W=["sqb"])
            for hg in range(0, 8, hpg):
                ph, pk = ps_half()
                pv = ph[:, :hpg * T].rearrange("p (b t) -> p b t", b=hpg)
                for b in range(hpg):
                    mm(pv[:, b, :], onesb[:], sqb[:, hg + b, :T], True, True, R=["onesb", "sqb"], W=[pk])
                rv = rstd[:, :hpg * T].rearrange("p (b t) -> p b t", b=hpg)
                act(rv, pv, AF.Ln, R=[pk, "epsc"], W=["rstd"], bias=epsc[:], scale=1.0 / 128.0)
                act(rv, rv, AF.Exp, R=["rstd"], W=["rstd"], scale=-0.5)
                tt("dve", oT[:, hg:hg + hpg, :T], oT[:, hg:hg + hpg, :T], rv, ALU.mult, R=["oT", "rstd"], W=["oT"])
            stt("dve", zbT[:, :, :T], oT[:, :, :T], onc[:, l:l + 1], zsT[:, :, :T], ALU.mult, ALU.mult,
                R=["oT", "onc", "zsT"], W=["zbT"])
            P.stage("onorm_%d" % l)
            for g in range(2):
                wt, wk, wi_ = w_get("ga%d" % g)

                def ev(m0, nb, pv, pk, g=g):
                    act(sga[:, g * 4 + m0:g * 4 + m0 + nb, :T], pv, AF.Sigmoid, R=[pk], W=["sga"])
                fm_proj(T, wt, wk, hT, "hT", 4, ev)
                w_done(wi_)
                wt, wk, wi_ = w_get("gb%d" % g)

                def ev(m0, nb, pv, pk, g=g):
                    act(sgb[:, g * 4 + m0:g * 4 + m0 + nb, :T], pv, AF.Sigmoid, R=[pk], W=["sgb"])
                fm_proj(T, wt, wk, hT, "hT", 4, ev)
                w_done(wi_)
                wt, wk, wi_ = w_get("pa%d" % g)

                def ev(m0, nb, pv, pk, g=g):
                    tt("dve", mtf[:, m0:m0 + nb, :T], pv, sga[:, g * 4 + m0:g * 4 + m0 + nb, :T], ALU.mult,
                       R=[pk, "sga"], W=["mtf"])
                fm_proj(T, wt, wk, uT, "uT", 4, ev)
                w_done(wi_)
                wt, wk, wi_ = w_get("pb%d" % g)

                def ev(m0, nb, pv, pk, g=g):
                    tt("dve", mixT[:, g * 4 + m0:g * 4 + m0 + nb, :T], pv, sgb[:, g * 4 + m0:g * 4 + m0 + nb, :T], ALU.mult,
                       R=[pk, "sgb"], W=["mixT"])
                    tt("pool", mixT[:, g * 4 + m0:g * 4 + m0 + nb, :T], mixT[:, g * 4 + m0:g * 4 + m0 + nb, :T],
                       mtf[:, m0:m0 + nb, :T], ALU.add, R=["mixT", "mtf"], W=["mixT"])
                fm_proj(T, wt, wk, zbT, "zbT", 4, ev)
                w_done(wi_)
            for g in range(2):
                wt, wk, wi_ = w_get("wo%d" % g)

                def ev(m0, nb, pv, pk, g=g):
                    tt("dve", xT[:, g * 4 + m0:g * 4 + m0 + nb, :T], pv, xT[:, g * 4 + m0:g * 4 + m0 + nb, :T], ALU.add,
                       R=[pk, "xT"], W=["xT"])
                fm_proj(T, wt, wk, mixT, "mixT", 4, ev)
                w_done(wi_)
            P.stage("merge_%d" % l)
            rmsnorm_fm(T, ln2c[:, l, :], "ln2c", hT, "hT")
            acc = pd[3][:, :8 * T].rearrange("p (m t) -> p m t", m=8)
            rlb = [rl, rl2]

            def ffn_up(g):
                wt, wk, wi_ = w_get("up%d" % g)
                hb = hid[g % 2]
                hk = "hid%d" % (g % 2)
                rb_ = rlb[g % 2]
                rk_ = "rl%d" % (g % 2)

                def ev(m0, nb, pv, pk, hb=hb, hk=hk):
                    act(rb_[:, m0:m0 + nb, :T], pv, AF.Relu, R=[pk], W=[rk_])
                    tt("pool", hb[:, m0:m0 + nb, :T], rb_[:, m0:m0 + nb, :T], rb_[:, m0:m0 + nb, :T], ALU.mult,
                       R=[rk_], W=[hk])
                fm_proj(T, wt, wk, hT, "hT", 4, ev)
                w_done(wi_)

            def ffn_down(g):
                hb = hid[g % 2]
                hk = "hid%d" % (g % 2)
                wt, wk, wi_ = w_get("dn%d" % g)
                wdv = wt[:].rearrange("p a b -> p (a b)").rearrange("p (kc n) -> p kc n", kc=4)
                for m in range(8):
                    for kc in range(4):
                        last = (m == 7 and kc == 3)
                        first_in_bank = (g == 0 and kc == 0 and (m * T) % 512 == 0)
                        P.op("pe", "matmul", acc[:, m, :], lhsT=wdv[:, kc, m * 128:(m + 1) * 128], rhs=hb[:, kc, :T],
                             start=first_in_bank, stop=(g == 7 and kc == 3), skip_group_check=True,
                             R=[wk, hk], W=["pd3a", "pd3b"], inc=(last or (g == 7 and kc == 3)))
                w_done(wi_)

            ffn_up(0)
            for g in range(8):
                if g + 1 < 8:
                    ffn_up(g + 1)
                ffn_down(g)
            tt("dve", xT[:, :, :T], acc, xT[:, :, :T], ALU.add, R=["pd3a", "pd3b", "xT"], W=["xT"])
            P.stage("ffn_end_%d_%s" % (l, seq["kind"] + str(seq["idx"])))

        def delta_prep(l, T, C, c0):
            hp = min(8, 512 // C)
            nhalf = 8 // hp
            gtri, DTi = B1, B1
            ngb, Ds, Atmp = B2, B2, B2

            def w8(ps_ap, np_):
                return ps_ap[:np_, :8 * C].rearrange("p (h x) -> p h x", h=8)
            ph, pk = ps_half()
            for kc in range(8):
                mm(ph[:C, :16], hT[:, kc, c0:c0 + C], wba[:, l, kc, :], kc == 0, kc == 7, R=["hT", "wba"], W=[pk])
            cp("dve", ba[:C, :], ph[:C, :16], R=[pk], W=["ba"])
            act(beta[:C, :], ba[:C, 0:8], AF.Sigmoid, R=["ba"], W=["beta"])
            tt("dve", gx[:C, :], ba[:C, 8:16], dtbb[:C, l, :], ALU.add, R=["ba", "dtbb"], W=["gx"])
            ts("dve", gm[:C, :], gx[:C, :], 0.0, ALU.max, R=["gx"], W=["gm"])
            stt("dve", gn[:C, :], gm[:C, :], -2.0, gx[:C, :], ALU.mult, ALU.add, R=["gm", "gx"], W=["gn"])
            act(gn[:C, :], gn[:C, :], AF.Exp, R=["gn"], W=["gn"])
            act(gn[:C, :], gn[:C, :], AF.Ln, R=["gn"], W=["gn"], bias=1.0, scale=1.0)
            tt("dve", gn[:C, :], gn[:C, :], gm[:C, :], ALU.add, R=["gn", "gm"], W=["gn"])
            tt("dve", gg[:C, :], gn[:C, :], nexpA[:C, l, :], ALU.mult, R=["gn", "nexpA"], W=["gg"])
            yield
            P.stage("d_g_%d" % l)
            ph, pk = ps_half()
            mm(ph[:C, 0:8], trif[:C, :C], gg[:C, :], True, True, R=["trif", "gg"], W=[pk], inc=False)
            mm(ph[:C, 8:16], trigt[:C, :C], gg[:C, :], True, True, R=["trigt", "gg"], W=[pk])
            act(egc[:C, :], ph[:C, 0:16], AF.Exp, R=[pk], W=["egc"])
            stt("dve", nbe[:C, :], beta[:C, :], -1.0, egc[:C, 0:8], ALU.mult, ALU.mult, R=["beta", "egc"], W=["nbe"])
            P.stage("d_col_%d" % l)
            tt("pool", gtri[:C, :, :C], trif[:C, :C].unsqueeze(1).to_broadcast([C, 8, C]),
               gg[:C, :].unsqueeze(2).to_broadcast([C, 8, C]), ALU.mult, R=["trif", "gg"], W=["B1"])
            ts("dve", ngb[:C, :, :C], gg[:C, :].unsqueeze(2).to_broadcast([C, 8, C]), -1.0, ALU.mult,
               R=["gg"], W=["B2"])
            yield
            P.stage("d_rhs_%d" % l)
            pa, pak = ps_pair()
            pb, pbk = ps_pair()
            pc, pck = ps_pair()

            pav, pbv, pcv = w8(pa, 128), w8(pb, C), w8(pc, C)
            for hh in range(nhalf):
                hs = slice(hh * hp, (hh + 1) * hp)
                ka, kb_, kc_ = pak[hh if nhalf > 1 else 0], pbk[hh if nhalf > 1 else 0], pck[hh if nhalf > 1 else 0]
                mm(pav[:, hs, :], onesf[:C, :], gtri[:C, hs, :C], True, True, R=["onesf", "B1"], W=[ka])
                mm(pbv[:, hs, :], onesf[:C, :C], gtri[:C, hs, :C], True, False, R=["onesf", "B1"], W=[kb_])
                mm(pbv[:, hs, :], trif[:C, :C], ngb[:C, hs, :C], False, False, R=["trif", "B2"], W=[kb_])
                mm(pbv[:, hs, :], identb[:C, :C], maskb[:C, hs, :C], False, True, R=["identb", "maskb"], W=[kb_])
                mm(pcv[:, hs, :], nonesf[:C, :C], gtri[:C, hs, :C], True, False, R=["nonesf", "B1"], W=[kc_])
                mm(pcv[:, hs, :], ntrif[:C, :C], ngb[:C, hs, :C], False, False, R=["ntrif", "B2"], W=[kc_])
                mm(pcv[:, hs, :], identb[:C, :C], maskc[:C, hs, :C], False, True, R=["identb", "maskc"], W=[kc_])
            act(egam[:, :, :C], pav, AF.Exp, R=pak, W=["egam"])
            act(DTi[:C, :, :C], pbv, AF.Exp, R=pbk, W=["B1"])
            act(Ds[:C, :, :C], pcv, AF.Exp, R=pck, W=["B2"])
            tt("pool", NA[1][:C, :, :C], identf[:C, :C].unsqueeze(1).to_broadcast([C, 8, C]),
               beta[:C, :].unsqueeze(2).to_broadcast([C, 8, C]), ALU.mult, R=["identf", "beta"], W=["NA1"])
            pbb, pbbk = ps_pair()
            pbbv = w8(pbb, C)
            for hh in range(nhalf):
                hs = slice(hh * hp, (hh + 1) * hp)
                mm(pbbv[:, hs, :], onesf[:C, :C], NA[1][:C, hs, :C], True, True, R=["onesf", "NA1"],
                   W=[pbbk[hh if nhalf > 1 else 0]])
            cp("act", NA[1][:C, :, :C], pbbv, R=pbbk, W=["NA1"])
            yield

        def delta_chunk(l, T, C, c0, hoisted=False, fillers=()):
            hp = min(8, 512 // C)
            nhalf = 8 // hp

            def wide(ps, C2):
                return ps[:, :].rearrange("p (h x) -> p h x", h=8) if False else None

            qv = qkT[:, 0:8, c0:c0 + C]
            kv = qkT[:, 8:16, c0:c0 + C]
            gtri, DTi = B1, B1
            ngb, Ds, Atmp = B2, B2, B2
            Yf, rtmp = B4, B4

            def w8(ps_ap, np_):
                return ps_ap[:np_, :8 * C].rearrange("p (h x) -> p h x", h=8)
            if not hoisted:
                for _ in delta_prep(l, T, C, c0):
                    pass
            P.stage("d_exp_%d" % l)
            pkk, pkkk = ps_pair()
            pqk, pqkk = ps_pair()
            pkkv, pqkv = w8(pkk, C), w8(pqk, C)
            for h in range(8):
                k_ = pkkk[(h // hp) if nhalf > 1 else 0]
                mm(pkkv[:, h, :], kv[:, h, :], kv[:, h, :], True, True, R=["qkT"], W=[k_], inc=(h % hp == hp - 1))
            for h in range(8):
                k_ = pqkk[(h // hp) if nhalf > 1 else 0]
                mm(pqkv[:, h, :], kv[:, h, :], qv[:, h, :], True, True, R=["qkT"], W=[k_], inc=(h % hp == hp - 1))
            tt("dve", PT[:C, :, :C], pqkv, DTi[:C, :, :C], ALU.mult, R=pqkk + ["B1"], W=["PT"])
            tt("dve", Atmp[:C, :, :C], pkkv, Ds[:C, :, :C], ALU.mult, R=pkkk + ["B2"], W=["B2"])
            P.stage("d_kk_%d" % l)
            NTA = [B2, NTA1]
            NTk = ["B2", "NTA1"]
            NAk = ["NA0", "NA1"]
            tt("dve", B2[:C, :, :C], B2[:C, :, :C], beta[:C, :].unsqueeze(2).to_broadcast([C, 8, C]), ALU.mult,
               R=["B2", "beta"], W=["B2"])
            P.stage("d_A_%d" % l)
            tt("dve", NA[0][:C, :, :C], pkkv, DTi[:C, :, :C], ALU.mult, R=pkkk + ["B1"], W=["NA0"])
            tt("dve", NA[0][:C, :, :C], NA[0][:C, :, :C], NA[1][:C, :, :C], ALU.mult, R=["NA0", "NA1"], W=["NA0"])
            tt("dve", NA[0][:C, :, :C], NA[0][:C, :, :C], offd[:C, :C].unsqueeze(1).to_broadcast([C, 8, C]), ALU.mult,
               R=["NA0", "offd"], W=["NA0"])
            P.stage("d_tr_%d" % l)
            tt("dve", Yf[:C, :, :C], identf[:C, :C].unsqueeze(1).to_broadcast([C, 8, C]), NA[0][:C, :, :C], ALU.subtract,
               R=["identf", "NA0"], W=["B4"])
            nlev = max(1, int(np.ceil(np.log2(C))) - 1)
            cur = 0
            for lev in range(nlev):
                nxt = 1 - cur
                lastlev = (lev == nlev - 1)
                Nc, NTc = NA[cur], NTA[cur]
                Nck, NTck = NAk[cur], NTk[cur]
                p2t, p2tk = ps_pair()
                p2tv = w8(p2t, C)
                for h in range(8):
                    k_ = p2tk[(h // hp) if nhalf > 1 else 0]
                    mm(p2tv[:, h, :], Nc[:C, h, :C], NTc[:C, h, :C], True, True, R=[Nck, NTck], W=[k_],
                       inc=(h % hp == hp - 1))
                if not lastlev:
                    p2, p2k = ps_pair()
                    p2v = w8(p2, C)
                    for h in range(8):
                        k_ = p2k[(h // hp) if nhalf > 1 else 0]
                        mm(p2v[:, h, :], NTc[:C, h, :C], Nc[:C, h, :C], True, True, R=[Nck, NTck], W=[k_],
                           inc=(h % hp == hp - 1))
                cp("act", NTA[nxt][:C, :, :C], p2tv, R=p2tk, W=[NTk[nxt]])
                if not lastlev:
                    cp("dve", NA[nxt][:C, :, :C], p2v, R=p2k, W=[NAk[nxt]])
                py, pyk = ps_pair()
                pyv = w8(py, C)
                for h in range(8):
                    k_ = pyk[(h // hp) if nhalf > 1 else 0]
                    mm(pyv[:, h, :], NTA[nxt][:C, h, :C], Yf[:C, h, :C], True, True, R=[NTk[nxt], "B4"], W=[k_],
                       inc=(h % hp == hp - 1))
                tt("dve", Yf[:C, :, :C], Yf[:C, :, :C], pyv, ALU.add, R=["B4"] + pyk, W=["B4"])
                cur = nxt
            cp("pool", Yb[:C, :, :C], Yf[:C, :, :C], R=["B4"], W=["Yb"])
            P.stage("d_neu_%d" % l)
            pt_, ptk = ps_half()
            ptv = pt_.bitcast(BF16)[:C, :1024].rearrange("p (h x) -> p h x", h=8)
            for h in range(8):
                tr(ptv[:, h, :], vT[:, h, c0:c0 + C], identb[:], R=["vT", "identb"], W=[ptk])
            tt("dve", bv[:C, :, :], ptv, beta[:C, :].unsqueeze(2).to_broadcast([C, 8, 128]), ALU.mult,
               R=[ptk, "beta"], W=["bv"])
            pt2, pt2k = ps_half()
            pt2v = pt2.bitcast(BF16)[:C, :1024].rearrange("p (h x) -> p h x", h=8)
            for h in range(8):
                tr(pt2v[:, h, :], kv[:, h, :], identb[:], R=["qkT", "identb"], W=[pt2k])
            tt("dve", kg[:C, :, :], pt2v, egc[:C, 8:16].unsqueeze(2).to_broadcast([C, 8, 128]), ALU.mult,
               R=[pt2k, "egc"], W=["kg"])
            tt("pool", qg[:, :, :C], qv, egam[:, :, :C], ALU.mult, R=["qkT", "egam"], W=["qg"])
            P.stage("d_tok_%d" % l)
            Sl, Sbl = S[l], Sb[l]
            Sk, Sbk = "S%d" % l, "Sb%d" % l
            pks_, pksk = ps_pair()
            pksv = pks_[:C, :].rearrange("p (h x) -> p h x", h=8)
            for h in range(8):
                mm(pksv[:, h, :], kv[:, h, :], Sbl[:, h, :], True, True, R=["qkT", Sbk], W=[pksk[h // 4]],
                   inc=(h % 4 == 3))
            for f_ in fillers:
                f_(0)
            tt("dve", rtmp[:C, :, :], pksv, nbe[:C, :].unsqueeze(2).to_broadcast([C, 8, 128]), ALU.mult,
               R=pksk + ["nbe"], W=["B4"])
            tt("dve", rb[:C, :, :], rtmp[:C, :, :], bv[:C, :, :], ALU.add, R=["B4", "bv"], W=["rb"])
            pvn, pvnk = ps_pair()
            pvnv = pvn[:C, :].rearrange("p (h x) -> p h x", h=8)
            for h in range(8):
                mm(pvnv[:, h, :], Yb[:C, h, :C], rb[:C, h, :], True, True, R=["Yb", "rb"], W=[pvnk[h // 4]],
                   inc=(h % 4 == 3))
            for f_ in fillers:
                f_(1)
            cp("act", vnw[:C, :, :], pvnv, R=pvnk, W=["vnw"])
            po, pok = ps_pair()
            pov = w8(po, 128)
            for h in range(8):
                k_ = pok[(h // hp) if nhalf > 1 else 0]
                mm(pov[:, h, :], Sbl[:, h, :], qg[:, h, :C], True, False, R=[Sbk, "qg"], W=[k_])
                mm(pov[:, h, :], vnw[:C, h, :], PT[:C, h, :C], False, True, R=["vnw", "PT"], W=[k_],
                   inc=(h % hp == hp - 1))
            cp("act", oT[:, :, c0:c0 + C], pov, R=pok, W=["oT"])
            psn, psnk = ps_pair()
            psnv = psn[:, :].rearrange("p (h x) -> p h x", h=8)
            for h in range(8):
                mm(psnv[:, h, :], kg[:C, h, :], vnw[:C, h, :], True, True, R=["kg", "vnw"], W=[psnk[h // 4]],
                   inc=(h % 4 == 3))
            tt("dve", Sl[:], Sl[:], egam[:, :, C - 1:C].to_broadcast([128, 8, 128]), ALU.mult, R=[Sk, "egam"], W=[Sk])
            tt("dve", Sl[:], Sl[:], psnv, ALU.add, R=[Sk] + psnk, W=[Sk])
            cp("act", Sbl[:], Sl[:], R=[Sk], W=[Sbk])

        seqs = [
            {"kind": "p", "idx": 0, "L": SEQ, "T": 128, "C": 128},
            {"kind": "p", "idx": 1, "L": SEQ, "T": 128, "C": 128},
            {"kind": "s", "idx": 0, "L": DEC_SEQ, "T": DEC_SEQ, "C": DEC_SEQ},
        ]
        for sq in seqs:
            ntile = sq["L"] // sq["T"]
            for _ in range(ntile):
                for l in range(DEPTH):
                    wplan.extend(layer_plan(l))

        for sq in seqs:
            T, C = sq["T"], sq["C"]
            ntile = sq["L"] // T
            for l in range(DEPTH):
                if sq["kind"] == "p":
                    P.op("pool", "memset", S[l][:], 0.0, W=["S%d" % l])
                    P.op("pool", "memset", Sb[l][:], 0.0, W=["Sb%d" % l])
                    P.op("pool", "memset", halo[l][:], 0.0, W=["halo%d" % l])
                else:
                    P.dma("sp", S[l][:], sdelta[l].rearrange("h d v -> d h v"), R=[], W=["S%d" % l], semkey="dma_Sin%d" % l)
                    cp("act", Sb[l][:], S[l][:], R=["S%d" % l], W=["Sb%d" % l])
                    P.dma("sp", cvf[:], sconv[l], R=[], W=["cvf"], semkey="dma_cvfin")
                    cp("dve", halo[l][:], cvf[:], R=["cvf"], W=["halo%d" % l])
            for ti in range(ntile):
                t0 = ti * T
                src = (xT_p[sq["idx"], :, :, t0:t0 + T] if sq["kind"] == "p" else xT_s[0, :, :, :])
                P.dma("sp", xT[:, :, :T], src.rearrange("kc p t -> p kc t"), R=[], W=["xT"], semkey="dma_xT")
                for l in range(DEPTH):
                    layer(l, T, C, sq, ti == 0, ti == ntile - 1)
                tt("dve", sqb[:, 0:8, :T], xT[:, :, :T], xT[:, :, :T], ALU.mult, R=["xT"], W=["sqb"])
                ph, pk = ps_half()
                for kc in range(8):
                    mm(ph[:, :T], onesb[:], sqb[:, kc, :T], kc == 0, kc == 7, R=["onesb", "sqb"], W=[pk])
                act(rstd[:, :T], ph[:, :T], AF.Ln, R=[pk, "epsc"], W=["rstd"], bias=epsc[:], scale=1.0 / D)
                act(rstd[:, :T], rstd[:, :T], AF.Exp, R=["rstd"], W=["rstd"], scale=-0.5)
                for kc in range(8):
                    stt("dve", oT[:, kc, :T], xT[:, kc, :T], fnc[:, kc:kc + 1], rstd[:, :T], ALU.mult, ALU.mult,
                        R=["xT", "rstd", "fnc"], W=["oT"])
                dst = (yT_p[sq["idx"], :, :, t0:t0 + T] if sq["kind"] == "p" else yT_s[0, :, :, :])
                P.dma("sp", dst.rearrange("kc p t -> p kc t"), oT[:, :, :T], R=["oT"], W=[], semkey="dma_y")
                P.stage("tile_end_%s_%d" % (sq["kind"] + str(sq["idx"]), ti))
            for l in range(DEPTH):
                dst = (ndl_p[l, sq["idx"]] if sq["kind"] == "p" else ndl_s[l, 0])
                P.dma("sp", dst.rearrange("h d v -> d h v"), S[l][:], R=["S%d" % l], W=[], semkey="dma_sout%d" % l)
        import os as _os
        if _os.environ.get("K_DELAY"):
            P.dead = False
            for _i in range(int(_os.environ["K_DELAY"])):
                P.op("pe", "matmul", pd[0][:, 0:512], lhsT=onesb[:], rhs=wbuf[0][:, 0, :], start=True, stop=True,
                     R=["onesb", "wbuf0"], W=["pd0a"], inc=(_i % 64 == 63))
            P.dead = True
        assert P.dead or wstate["consumed"] == len(wplan), (wstate, len(wplan))
        for sk, v in P.dma_cnt.items():
            P._need("sp", sk, v)
        for e_ in ("pe", "act", "dve", "pool"):
            P._need("sp", e_, P.cnt[e_])
        build.ninst = P.ninst
    return nc


def _prep_inputs(inputs, SEQ):
    f = lambda a: np.ascontiguousarray(np.asarray(a, dtype=np.float32))
    xp = f(inputs["x_prompt"])[:, :SEQ]
    xs = f(inputs["x_sample"])
    xpT = np.ascontiguousarray(xp.reshape(16, SEQ, 8, 128).transpose(0, 2, 3, 1))
    xsT = np.ascontiguousarray(xs.reshape(8, DEC_SEQ, 8, 128).transpose(0, 2, 3, 1))
    sc = f(inputs["state_conv"])
    scT = np.ascontiguousarray(sc.reshape(DEPTH, 8, 3, 24, 128).transpose(1, 0, 4, 3, 2))
    sd = f(inputs["state_delta"])
    rep = lambda a: np.ascontiguousarray(np.broadcast_to(a[None], (128,) + a.shape))
    common = {
        "ln1c": np.ascontiguousarray(f(inputs["ln1"]).reshape(DEPTH, 8, 128).transpose(2, 0, 1)),
        "ln2c": np.ascontiguousarray(f(inputs["ln2"]).reshape(DEPTH, 8, 128).transpose(2, 0, 1)),
        "fnc": np.ascontiguousarray(f(inputs["final_norm"]).reshape(8, 128).transpose(1, 0)),
        "cwc": np.ascontiguousarray(f(inputs["conv_w"]).reshape(DEPTH, 24, 128, CW).transpose(2, 0, 1, 3)),
        "onc": np.ascontiguousarray(f(inputs["o_norm"]).transpose(1, 0)),
        "algb": rep(f(inputs["a_ln_g"])),
        "albb": rep(f(inputs["a_ln_b"])),
        "wstT": np.ascontiguousarray(f(inputs["w_s"]).transpose(3, 0, 1, 2)),
        "bsr": np.ascontiguousarray(f(inputs["b_s"]).reshape(1, DEPTH, D)),
        "alogb": rep(f(inputs["a_log"])),
        "dtbb": rep(f(inputs["dt_bias"])),
        "w_in": f(inputs["w_in"]), "p_a": f(inputs["p_a"]), "p_b": f(inputs["p_b"]), "w_o": f(inputs["w_o"]),
        "w_up": f(inputs["w_up"]), "w_down": f(inputs["w_down"]),
    }
    maps = []
    for c in range(NCORES):
        m = dict(common)
        m["xT_p"] = np.ascontiguousarray(xpT[2 * c:2 * c + 2])
        m["xT_s"] = np.ascontiguousarray(xsT[c:c + 1])
        m["sconv"] = np.ascontiguousarray(scT[c])
        m["sdelta"] = np.ascontiguousarray(sd[:, c])
        maps.append(m)
    return maps


def _assemble(results, SEQ):
    yp = np.concatenate([r["yT_p"] for r in results], axis=0)
    y_prompt = np.ascontiguousarray(yp.transpose(0, 3, 1, 2).reshape(16, SEQ, D))
    ys = np.concatenate([r["yT_s"] for r in results], axis=0)
    y_sample = np.ascontiguousarray(ys.transpose(0, 3, 1, 2).reshape(8, DEC_SEQ, D))
    cvp = np.concatenate([r["ncv_p"] for r in results], axis=1)
    new_conv_prompt = np.ascontiguousarray(cvp.transpose(0, 1, 4, 3, 2).reshape(DEPTH, 16, 3, 3072))
    new_delta_prompt = np.ascontiguousarray(np.concatenate([r["ndl_p"] for r in results], axis=1))
    cvs = np.concatenate([r["ncv_s"] for r in results], axis=1)
    new_conv_sample = np.ascontiguousarray(cvs.transpose(0, 1, 4, 3, 2).reshape(DEPTH, 8, 3, 3072))
    new_delta_sample = np.ascontiguousarray(np.concatenate([r["ndl_s"] for r in results], axis=1))
    new_gv = np.ascontiguousarray(np.concatenate([r["ngv_s"] for r in results], axis=1))
    outs = (y_prompt, y_sample, new_conv_prompt, new_delta_prompt, new_conv_sample, new_delta_sample, new_gv)
    return tuple(np.asarray(o, dtype=np.float32) for o in outs)


def run(inputs, SEQ, stop_at=None, lite=False, start_at=None):
    nc = build(SEQ, stop_at, lite, start_at)
    maps = _prep_inputs(inputs, SEQ)
    if lite:
        for m in maps:
            for k in ("w_in", "p_a", "p_b", "w_o", "w_up", "w_down"):
                m.pop(k)
    res = run_bass_kernel_spmd(nc, maps, core_ids=list(range(NCORES)))
    return _assemble(res.results, SEQ)


def kernel(**inputs):
    return run(inputs, SEQ_FULL)
```

```python
import numpy as np
from contextlib import ExitStack
import concourse.bass as bass
import concourse.mybir as mybir
from concourse.bass_utils import run_bass_kernel_spmd

F32 = mybir.dt.float32
BF16 = mybir.dt.bfloat16
AF = mybir.ActivationFunctionType
ALU = mybir.AluOpType
AX = mybir.AxisListType

NCORES = 8
D = 1024
DEPTH = 2
NIN = 8208
DFF = 4096
EPS = 1e-6
NEG = -30000.0
import os as _os0
NOSELF = set((_os0.environ.get("K_NOSELF") or "").split(",")) - {""}
SEQ_FULL = 4096
DEC_SEQ = 16
CW = 4


class Prog:
    def __init__(self, nc, es):
        self.nc = nc
        self.es = es
        self.eng = {"pe": nc.tensor, "act": nc.scalar, "dve": nc.vector, "pool": nc.gpsimd, "sp": nc.sync}
        self.sem = {k: es.enter_context(nc.semaphore("sem_" + k)) for k in self.eng}
        self.cnt = {k: 0 for k in self.eng}
        self.waited = {k: {} for k in self.eng}
        self.lastw = {}
        self.readers = {}
        self.dma_sems = {}
        self.dma_cnt = {}
        self.ninst = 0
        self.dead = False
        self.stop_at = None

    def stage(self, name):
        if self.stop_at is not None and name == self.stop_at:
            self.dead = True
            self.stopped = True
        elif getattr(self, "start_at", None) is not None and not getattr(self, "stopped", False):
            if name == "setup":
                self.dead = True
            elif name == self.start_at:
                self.dead = False

    def _need(self, e, semkey, val):
        if val <= 0:
            return
        w = self.waited[e]
        if w.get(semkey, 0) >= val:
            return
        w[semkey] = val
        s = self.sem[semkey] if semkey in self.sem else self.dma_sems[semkey]
        self.eng[e].wait_ge(s, val)
        self.ninst += 1

    def _deps(self, e, reads, writes):
        skip_self = (e == "pe") or (e in NOSELF)
        for k in reads:
            lw = self.lastw.get(k)
            if lw and not (skip_self and lw[0] == e):
                self._need(e, lw[0], lw[1])
        for k in writes:
            lw = self.lastw.get(k)
            if lw and not (skip_self and lw[0] == e):
                self._need(e, lw[0], lw[1])
            for sk, v in self.readers.get(k, {}).items():
                if not (skip_self and sk == e):
                    self._need(e, sk, v)

    def _commit(self, semkey, val, reads, writes):
        for k in writes:
            self.lastw[k] = (semkey, val)
            self.readers[k] = {}
        for k in reads:
            r = self.readers.setdefault(k, {})
            if r.get(semkey, 0) < val:
                r[semkey] = val

    def op(self, e, fn, *args, R=(), W=(), inc=True, **kw):
        if self.dead:
            return None
        W = list(W) + [k for k in R if k.startswith("pd") and k not in W]
        self._deps(e, R, W)
        ins = getattr(self.eng[e], fn)(*args, **kw)
        self.ninst += 1
        if inc:
            self.cnt[e] += 1
            ins.then_inc(self.sem[e], 1)
            self._commit(e, self.cnt[e], R, W)
        else:
            self._commit(e, self.cnt[e] + 1, R, W)
        return ins

    def dma(self, e, out, in_, R, W, semkey):
        if self.dead:
            return None
        if semkey not in self.dma_sems:
            self.dma_sems[semkey] = self.es.enter_context(self.nc.semaphore(semkey))
            self.dma_cnt[semkey] = 0
        self._deps(e, R, W)
        ins = self.eng[e].dma_start(out=out, in_=in_)
        self.dma_cnt[semkey] += 16
        ins.then_inc(self.dma_sems[semkey], 16)
        self._commit(semkey, self.dma_cnt[semkey], R, W)
        self.ninst += 1
        return ins

    def finish(self, e="sp"):
        for k, (sk, v) in list(self.lastw.items()):
            self._need(e, sk, v)


def build(SEQ, stop_at=None, lite=False, start_at=None):
    nc = bass.Bass("TRN2", target_bir_lowering=False)
    es = ExitStack()

    def din(name, shape, dt=F32):
        return nc.dram_tensor(name, list(shape), dt, kind="ExternalInput").ap()

    def dout(name, shape, dt=F32):
        return nc.dram_tensor(name, list(shape), dt, kind="ExternalOutput").ap()

    def dscr(name, shape, dt=BF16):
        return nc.dram_tensor(name, list(shape), dt, kind="Internal").ap()

    xT_p = din("xT_p", [2, 8, 128, SEQ])
    xT_s = din("xT_s", [1, 8, 128, DEC_SEQ])
    sconv = din("sconv", [DEPTH, 128, 24, 3])
    sdelta = din("sdelta", [DEPTH, 8, 128, 128])
    ln1_d = din("ln1c", [128, DEPTH, 8])
    ln2_d = din("ln2c", [128, DEPTH, 8])
    fn_d = din("fnc", [128, 8])
    cw_d = din("cwc", [128, DEPTH, 24, 4])
    on_d = din("onc", [128, DEPTH])
    alg_d = din("algb", [128, DEPTH, D])
    alb_d = din("albb", [128, DEPTH, D])
    wst_d = din("wstT", [128, DEPTH, 8, 128])
    bs_d = din("bsr", [1, DEPTH, D])
    alog_d = din("alogb", [128, DEPTH, 8])
    dtb_d = din("dtbb", [128, DEPTH, 8])
    wdecl = (lambda n, s_: dscr(n + "_lite", s_, F32)) if lite else din
    w_in_d = wdecl("w_in", [DEPTH, D, NIN])
    p_a_d = wdecl("p_a", [DEPTH, D, D])
    p_b_d = wdecl("p_b", [DEPTH, D, D])
    w_o_d = wdecl("w_o", [DEPTH, D, D])
    w_up_d = wdecl("w_up", [DEPTH, D, DFF])
    w_dn_d = wdecl("w_down", [DEPTH, DFF, D])

    yT_p = dout("yT_p", [2, 8, 128, SEQ])
    yT_s = dout("yT_s", [1, 8, 128, DEC_SEQ])
    ncv_p = dout("ncv_p", [DEPTH, 2, 128, 24, 3])
    ndl_p = dout("ndl_p", [DEPTH, 2, 8, 128, 128])
    ncv_s = dout("ncv_s", [DEPTH, 1, 128, 24, 3])
    ndl_s = dout("ndl_s", [DEPTH, 1, 8, 128, 128])
    ngv_s = dout("ngv_s", [DEPTH, 1, DEC_SEQ, D])

    w_in_b = dscr("w_in_b", [DEPTH, 16, 128, 8, 512])
    wba_b = dscr("wba_b", [DEPTH, D, 16])
    p_a_b = dscr("p_a_b", [DEPTH, 2, 128, 8, 512])
    p_b_b = dscr("p_b_b", [DEPTH, 2, 128, 8, 512])
    w_o_b = dscr("w_o_b", [DEPTH, 2, 128, 8, 512])
    w_up_b = dscr("w_up_b", [DEPTH, 8, 128, 8, 512])
    w_dn_b = dscr("w_dn_b", [DEPTH, 8, 128, 4, 1024])

    with es:
        P = Prog(nc, es)
        P.stop_at = stop_at
        P.start_at = start_at

        def sb(name, shape, dt):
            return es.enter_context(nc.sbuf_tensor(name, list(shape), dt))

        TM = 128

        identf = sb("identf", [128, 128], F32)
        identb = sb("identb", [128, 128], BF16)
        trif = sb("trif", [128, 128], F32)
        ntrif = sb("ntrif", [128, 128], F32)
        trigt = sb("trigt", [128, 128], F32)
        onesf = sb("onesf", [128, 128], F32)
        offd = sb("offd", [128, 128], F32)
        nonesf = sb("nonesf", [128, 128], F32)
        onesb = sb("onesb", [128, 128], BF16)
        maskb = sb("maskb", [128, 8, 128], BF16)
        maskc = sb("maskc", [128, 8, 128], BF16)
        epsc = sb("epsc", [128, 1], F32)
        epsq = sb("epsq", [128, 1], F32)
        ln1c = sb("ln1c_s", [128, DEPTH, 8], F32)
        ln2c = sb("ln2c_s", [128, DEPTH, 8], F32)
        fnc = sb("fnc_s", [128, 8], F32)
        cwc = sb("cwc_s", [128, DEPTH, 24, 4], F32)
        onc = sb("onc_s", [128, DEPTH], F32)
        algb = sb("algb_s", [128, DEPTH, D], F32)
        albb = sb("albb_s", [128, DEPTH, D], F32)
        wstb = sb("wstb", [128, DEPTH, 8, 128], BF16)
        bsb = sb("bsb", [1, DEPTH, D], BF16)
        alogb = sb("alogb_s", [128, DEPTH, 8], F32)
        nexpA = sb("nexpA", [128, DEPTH, 8], F32)
        dtbb = sb("dtbb_s", [128, DEPTH, 8], F32)
        wba = sb("wba", [128, DEPTH, 8, 16], BF16)

        NA = [sb("NA%d" % i, [128, 8, 128], F32) for i in range(2)]
        NTA1 = sb("NTA1", [128, 8, 128], F32)
        xT = sb("xT", [128, 8, TM], F32)
        hT = sb("hT", [128, 8, TM], BF16)
        sqb = sb("sqb", [128, 16, TM], BF16)
        rstd = sb("rstd", [128, 4 * TM], F32)
        uT = sb("uT", [128, 8, TM], BF16)
        vf = sb("vf", [128, D], F32)
        vc = sb("vc", [128, D], F32)
        vnb = sb("vnb", [128, D], BF16)
        st4 = sb("st4", [128, 8], F32)
        qkvpre = sb("qkvpre", [128, 24, 3 + TM], BF16)
        halo = [sb("halo%d" % l, [128, 24, 3], BF16) for l in range(DEPTH)]
        cvf = sb("cvf", [128, 24, 3], F32)
        qkf = sb("qkf", [128, 16, TM], F32)
        qkT = sb("qkT", [128, 16, TM], BF16)
        vT = sb("vT", [128, 8, TM], BF16)
        zsT = sb("zsT", [128, 8, TM], BF16)
        sga = sb("sga", [128, 8, TM], BF16)
        sgb = sb("sgb", [128, 8, TM], BF16)
        oT = sb("oT", [128, 8, TM], F32)
        zbT = sb("zbT", [128, 8, TM], BF16)
        mixT = sb("mixT", [128, 8, TM], BF16)
        mtf = sb("mtf", [128, 4, TM], F32)
        hid = [sb("hid%d" % i, [128, 4, TM], BF16) for i in range(2)]
        rl = sb("rl", [128, 4, TM], BF16)
        rl2 = sb("rl2", [128, 4, TM], BF16)
        S = [sb("S%d" % l, [128, 8, 128], F32) for l in range(DEPTH)]
        Sb = [sb("Sb%d" % l, [128, 8, 128], BF16) for l in range(DEPTH)]
        ba = sb("ba", [128, 16], F32)
        beta = sb("beta", [128, 8], F32)
        gx = sb("gx", [128, 8], F32)
        gm = sb("gm", [128, 8], F32)
        gn = sb("gn", [128, 8], F32)
        gg = sb("gg", [128, 8], F32)
        egc = sb("egc", [128, 16], F32)
        nbe = sb("nbe", [128, 8], F32)
        B1 = sb("B1", [128, 8, 128], F32)
        B2 = sb("B2", [128, 8, 128], F32)
        egam = sb("egam", [128, 8, 128], F32)
        B4 = sb("B4", [128, 8, 128], F32)
        PT = sb("PT", [128, 8, 128], BF16)
        Yb = sb("Yb", [128, 8, 128], BF16)
        bv = sb("bv", [128, 8, 128], BF16)
        kg = sb("kg", [128, 8, 128], BF16)
        qg = sb("qg", [128, 8, 128], BF16)
        rb = sb("rb", [128, 8, 128], BF16)
        vnw = sb("vnw", [128, 8, 128], BF16)
        NWB = 6
        wbuf = [sb("wbuf%d" % i, [128, 8, 512], BF16) for i in range(NWB)]

        def cdma(dst, src, key):
            P.dma("sp", dst, src, R=[], W=[key], semkey="dma_c_" + key)

        cdma(ln1c[:], ln1_d, "ln1c"); cdma(ln2c[:], ln2_d, "ln2c"); cdma(fnc[:], fn_d, "fnc")
        cdma(cwc[:], cw_d, "cwc"); cdma(onc[:], on_d, "onc"); cdma(algb[:], alg_d, "algb")
        cdma(albb[:], alb_d, "albb")
        cdma(alogb[:], alog_d, "alogb"); cdma(dtbb[:], dtb_d, "dtbb")

        def cast_tiled(dst, src, key, ntile, col0, t0):
            for l in range(DEPTH):
                for kc in range(8):
                    P.dma("pool", dst[l, t0:t0 + ntile, :, kc, :].rearrange("t p n -> p t n"),
                          src[l, kc * 128:(kc + 1) * 128, col0:col0 + ntile * 512].rearrange("p (t n) -> p t n", t=ntile),
                          R=[], W=[key], semkey="dma_" + key)

        cast_tiled(w_in_b, w_in_d, "w_in_b", 12, 0, 0)
        cast_tiled(w_in_b, w_in_d, "w_in_b", 4, 6160, 12)
        for l in range(DEPTH):
            for kc in range(8):
                P.dma("pool", wba_b[l, kc * 128:(kc + 1) * 128, :], w_in_d[l, kc * 128:(kc + 1) * 128, 6144:6160],
                      R=[], W=["wba_b"], semkey="dma_wba_b")
        cast_tiled(p_a_b, p_a_d, "p_a_b", 2, 0, 0)
        cast_tiled(p_b_b, p_b_d, "p_b_b", 2, 0, 0)
        cast_tiled(w_o_b, w_o_d, "w_o_b", 2, 0, 0)
        cast_tiled(w_up_b, w_up_d, "w_up_b", 8, 0, 0)
        for l in range(DEPTH):
            for rblk in range(32):
                P.dma("pool", w_dn_b[l, rblk // 4, :, rblk % 4, :], w_dn_d[l, rblk * 128:(rblk + 1) * 128, :],
                      R=[], W=["w_dn_b"], semkey="dma_w_dn_b")

        def gp(fn, *a, R=(), W=(), **kw):
            return P.op("pool", fn, *a, R=R, W=W, **kw)

        gp("memset", onesf[:], 1.0, W=["onesf"])
        gp("memset", nonesf[:], -1.0, W=["nonesf"])
        gp("memset", onesb[:], 1.0, W=["onesb"])
        gp("memset", epsc[:], EPS, W=["epsc"])
        gp("memset", epsq[:], EPS * 128.0, W=["epsq"])
        gp("affine_select", out=identf[:], in_=onesf[:], pattern=[[1, 128]], compare_op=ALU.is_equal, fill=0.0,
           base=0, channel_multiplier=-1, R=["onesf"], W=["identf"])
        gp("affine_select", out=trif[:], in_=onesf[:], pattern=[[1, 128]], compare_op=ALU.is_ge, fill=0.0,
           base=0, channel_multiplier=-1, R=["onesf"], W=["trif"])
        gp("affine_select", out=ntrif[:], in_=nonesf[:], pattern=[[1, 128]], compare_op=ALU.is_ge, fill=0.0,
           base=0, channel_multiplier=-1, R=["nonesf"], W=["ntrif"])
        gp("affine_select", out=trigt[:], in_=onesf[:], pattern=[[-1, 128]], compare_op=ALU.is_gt, fill=0.0,
           base=0, channel_multiplier=1, R=["onesf"], W=["trigt"])
        P.op("dve", "tensor_copy", out=identb[:], in_=identf[:], R=["identf"], W=["identb"])
        P.op("dve", "tensor_tensor", out=offd[:], in0=onesf[:], in1=identf[:], op=ALU.subtract, R=["onesf", "identf"], W=["offd"])
        mt0, mt1, mt2 = B1[:, 0, :], B2[:, 0, :], B4[:, 0, :]
        gp("memset", mt0, NEG, W=["B1"])
        gp("affine_select", out=mt1, in_=mt0, pattern=[[-1, 128]], compare_op=ALU.is_gt, fill=0.0,
           base=0, channel_multiplier=1, R=["B1"], W=["B2"])
        gp("affine_select", out=mt2, in_=mt0, pattern=[[1, 128]], compare_op=ALU.is_ge, fill=0.0,
           base=0, channel_multiplier=-1, R=["B1"], W=["B4"])
        for h in range(8):
            P.op("dve", "tensor_copy", out=maskb[:, h, :], in_=mt1, R=["B2"], W=["maskb"])
            P.op("dve", "tensor_copy", out=maskc[:, h, :], in_=mt2, R=["B4"], W=["maskc"])
        for l in range(DEPTH):
            P.dma("sp", egam[:], wst_d[:, l, :, :], R=[], W=["egam"], semkey="dma_c2")
            for g in range(8):
                P.op("dve", "tensor_tensor", out=wstb[:, l, g, :], in0=egam[:, g, :], in1=trif[:], op=ALU.mult,
                     R=["egam", "trif"], W=["wstb"])
            P.dma("sp", vf[0:1, :], bs_d[0:1, l, :], R=[], W=["vf"], semkey="dma_c3")
            P.op("dve", "tensor_copy", out=bsb[0:1, l, :], in_=vf[0:1, :], R=["vf"], W=["bsb"])
        P.op("act", "activation", out=nexpA[:], in_=alogb[:], func=AF.Exp, R=["alogb"], W=["nexpA"])
        P.op("dve", "tensor_scalar", out=nexpA[:], in0=nexpA[:], scalar1=-1.0, scalar2=None, op0=ALU.mult,
             R=["nexpA"], W=["nexpA"])
        for l in range(DEPTH):
            P.dma("sp", wba[:, l, :, :], wba_b[l].rearrange("(kc p) n -> p kc n", p=128),
                  R=["wba_b"], W=["wba"], semkey="dma_c_wba")

        P.stage("setup")
        pd = [es.enter_context(nc.psum_tensor("pd%d" % i, [128, 1024], F32)) for i in range(4)]
        ps_state = {"i": 0}

        def ps_half():
            i = ps_state["i"] % 6
            ps_state["i"] += 1
            t = pd[i // 2]
            return t[:, (i % 2) * 512:(i % 2) * 512 + 512], "pd%d%s" % (i // 2, "ab"[i % 2])

        def ps_pair():
            if ps_state["i"] % 2:
                ps_state["i"] += 1
            i = ps_state["i"] % 6
            ps_state["i"] += 2
            return pd[i // 2][:], ["pd%da" % (i // 2), "pd%db" % (i // 2)]

        wplan = []
        wstate = {"issued": 0, "consumed": 0}

        def layer_plan(l):
            pl = []
            names = ["u0", "u1", "v0", "v1"] + ["qkv%d" % g for g in range(6)] + ["z0", "z1"]
            for t, nm in enumerate(names):
                pl.append((nm, w_in_b[l, t], "w_in_b"))
            for g in range(2):
                pl.append(("ga%d" % g, w_in_b[l, 12 + g], "w_in_b"))
                pl.append(("gb%d" % g, w_in_b[l, 14 + g], "w_in_b"))
                pl.append(("pa%d" % g, p_a_b[l, g], "p_a_b"))
                pl.append(("pb%d" % g, p_b_b[l, g], "p_b_b"))
            for g in range(2):
                pl.append(("wo%d" % g, w_o_b[l, g], "w_o_b"))

            def up_e(g):
                return ("up%d" % g, w_up_b[l, g], "w_up_b")

            def dn_e(g):
                return ("dn%d" % g, w_dn_b[l, g], "w_dn_b")
            pl.append(up_e(0))
            for g in range(8):
                if g + 1 < 8:
                    pl.append(up_e(g + 1))
                pl.append(dn_e(g))
            return pl

        def w_issue_upto(n):
            while wstate["issued"] < min(n, len(wplan)):
                i = wstate["issued"]
                tag, view, srckey = wplan[i]
                slot = i % NWB
                if tag.startswith("dn"):
                    dst = wbuf[slot][:].rearrange("p a b -> p (a b)").rearrange("p (kc n) -> p kc n", kc=4)
                else:
                    dst = wbuf[slot][:]
                P.dma("sp", dst, view, R=[srckey], W=["wbuf%d" % slot], semkey="dma_wbuf%d" % slot)
                wstate["issued"] += 1

        def w_get(tag):
            i = wstate["consumed"]
            assert wplan[i][0] == tag, (wplan[i][0], tag)
            w_issue_upto(i + 1)
            wstate["consumed"] += 1
            slot = i % NWB
            return wbuf[slot], "wbuf%d" % slot, i

        def w_done(i):
            w_issue_upto(i + NWB)

        def mm(out, lhsT, rhs, start, stop, R, W, inc=None):
            if inc is None:
                inc = stop
            return P.op("pe", "matmul", out, lhsT=lhsT, rhs=rhs, start=start, stop=stop, R=R, W=W, inc=inc)

        def tr(out, in_, ident, R, W):
            return P.op("pe", "transpose", out=out, in_=in_, identity=ident, R=R, W=W)

        def act(out, in_, func, R, W, **kw):
            return P.op("act", "activation", out=out, in_=in_, func=func, R=R, W=W, **kw)

        def tt(e, out, in0, in1, op, R, W):
            return P.op(e, "tensor_tensor", out=out, in0=in0, in1=in1, op=op, R=R, W=W)

        def ts(e, out, in0, s1, op0, R, W, s2=None, op1=None):
            if op1 is None:
                return P.op(e, "tensor_scalar", out=out, in0=in0, scalar1=s1, scalar2=None, op0=op0, R=R, W=W)
            return P.op(e, "tensor_scalar", out=out, in0=in0, scalar1=s1, scalar2=s2, op0=op0, op1=op1, R=R, W=W)

        def stt(e, out, in0, scalar, in1, op0, op1, R, W):
            return P.op(e, "scalar_tensor_tensor", out=out, in0=in0, scalar=scalar, in1=in1, op0=op0, op1=op1,
                        R=R, W=W)

        def cp(e, out, in_, R, W):
            if e == "act":
                return act(out, in_, AF.Identity, R, W)
            return P.op(e, "tensor_copy", out=out, in_=in_, R=R, W=W)

        def rmsnorm_fm(T, gcol, gkey, dst, dstkey):
            tt("dve", sqb[:, 0:8, :T], xT[:, :, :T], xT[:, :, :T], ALU.mult, R=["xT"], W=["sqb"])
            ph, pk = ps_half()
            for kc in range(8):
                mm(ph[:, :T], onesb[:], sqb[:, kc, :T], kc == 0, kc == 7, R=["onesb", "sqb"], W=[pk])
            act(rstd[:, :T], ph[:, :T], AF.Ln, R=[pk, "epsc"], W=["rstd"], bias=epsc[:], scale=1.0 / D)
            act(rstd[:, :T], rstd[:, :T], AF.Exp, R=["rstd"], W=["rstd"], scale=-0.5)
            for kc in range(8):
                stt("dve", dst[:, kc, :T], xT[:, kc, :T], gcol[:, kc:kc + 1], rstd[:, :T], ALU.mult, ALU.mult,
                    R=["xT", "rstd", gkey], W=[dstkey])

        def fm_proj(T, wt, wkey, src, srckey, nblk, evac):
            bpb = 512 // T if T >= 128 else 4
            bpb = min(bpb, nblk)
            for m0 in range(0, nblk, bpb):
                ph, pk = ps_half()
                nb = min(bpb, nblk - m0)
                pv = ph[:, :nb * T].rearrange("p (b t) -> p b t", b=nb)
                for b in range(nb):
                    for kc in range(8):
                        mm(pv[:, b, :], wt[:, kc, (m0 + b) * 128:(m0 + b + 1) * 128], src[:, kc, :T], kc == 0, kc == 7,
                           R=[wkey, srckey], W=[pk])
                evac(m0, nb, pv, pk)

        def layer(l, T, C, seq, first_tile, last_tile):
            NCH = T // C
            rmsnorm_fm(T, ln1c[:, l, :], "ln1c", hT, "hT")
            prep_gen = delta_prep(l, T, C, 0) if NCH == 1 else None
            if prep_gen is not None:
                next(prep_gen)
            P.stage("norm1_%d" % l)
            for g in range(2):
                wt, wk, wi_ = w_get("u%d" % g)

                def ev(m0, nb, pv, pk, g=g):
                    act(uT[:, g * 4 + m0:g * 4 + m0 + nb, :T], pv, AF.Gelu, R=[pk], W=["uT"])
                fm_proj(T, wt, wk, hT, "hT", 4, ev)
                w_done(wi_)
            if prep_gen is not None:
                next(prep_gen)
            P.stage("u_%d" % l)
            wv = [w_get("v%d" % g) for g in range(2)]
            for c in range(NCH):
                c0 = c * C
                for g in range(2):
                    wt, wk, _ = wv[g]
                    ph, pk = ps_half()
                    for kc in range(8):
                        mm(ph[:C, :], hT[:, kc, c0:c0 + C], wt[:, kc, :], kc == 0, kc == 7, R=["hT", wk], W=[pk])
                    act(vf[:C, g * 512:(g + 1) * 512], ph[:C, :], AF.Gelu, R=[pk], W=["vf"])
                P.op("dve", "reduce_sum", out=st4[:C, 0:1], in_=vf[:C, :], axis=AX.X, R=["vf"], W=["st4"])
                ts("dve", st4[:C, 1:2], st4[:C, 0:1], -1.0 / D, ALU.mult, R=["st4"], W=["st4"])
                ts("dve", vc[:C, :], vf[:C, :], st4[:C, 1:2], ALU.add, R=["vf", "st4"], W=["vc"])
                tt("dve", vf[:C, :], vc[:C, :], vc[:C, :], ALU.mult, R=["vc"], W=["vf"])
                P.op("dve", "reduce_sum", out=st4[:C, 2:3], in_=vf[:C, :], axis=AX.X, R=["vf"], W=["st4"])
                act(st4[:C, 3:4], st4[:C, 2:3], AF.Ln, R=["st4", "epsc"], W=["st4"], bias=epsc[:C, :], scale=1.0 / D)
                act(st4[:C, 4:5], st4[:C, 3:4], AF.Exp, R=["st4"], W=["st4"], scale=-0.5)
                stt("dve", vc[:C, :], vc[:C, :], st4[:C, 4:5], algb[:C, l, :], ALU.mult, ALU.mult,
                    R=["vc", "st4", "algb"], W=["vc"])
                tt("dve", vc[:C, :], vc[:C, :], albb[:C, l, :], ALU.add, R=["vc", "albb"], W=["vc"])
                if seq["kind"] == "s":
                    P.dma("sp", ngv_s[l, 0, :, :], vc[:C, :], R=["vc"], W=[], semkey="dma_ngv")
                cp("act", vnb[:C, :], vc[:C, :], R=["vc"], W=["vnb"])
                pp, pks = ps_pair()
                ppv = pp.rearrange("p (g t) -> p g t", g=8)
                for g in range(8):
                    k = pks[g // 4]
                    mm(ppv[:, g, :C], vnb[:C, g * 128:(g + 1) * 128], wstb[:C, l, g, :C], True, False,
                       R=["vnb", "wstb"], W=[k])
                    mm(ppv[:, g, :C], onesb[0:1, :], bsb[0:1, l, g * 128:g * 128 + C], False, True,
                       R=["onesb", "bsb"], W=[k])
                tt("dve", uT[:, :, c0:c0 + C], ppv[:, :, :C], uT[:, :, c0:c0 + C], ALU.mult, R=pks + ["uT"], W=["uT"])
            for g in range(2):
                w_done(wv[g][2])
            if prep_gen is not None:
                for _ in prep_gen:
                    pass
            P.stage("gmlp_%d" % l)
            qp = qkvpre
            qk_ = "qkvpre"
            cp("pool", qp[:, :, 0:3], halo[l][:], R=["halo%d" % l], W=[qk_])
            for g in range(6):
                wt, wk, wi_ = w_get("qkv%d" % g)

                def ev(m0, nb, pv, pk, g=g):
                    cp("act", qp[:, g * 4 + m0:g * 4 + m0 + nb, 3:3 + T], pv, R=[pk], W=[qk_])
                    import os as _os
                    if last_tile and not _os.environ.get("K_NOCVF"):
                        cp("dve", cvf[:, g * 4 + m0:g * 4 + m0 + nb, :], pv[:, :, T - 3:T], R=[pk], W=["cvf"])
                fm_proj(T, wt, wk, hT, "hT", 4, ev)
                w_done(wi_)
            P.stage("qkv_%d" % l)
            if last_tile:
                dst = (ncv_p[l, seq["idx"]] if seq["kind"] == "p" else ncv_s[l, 0])
                P.dma("sp", dst, cvf[:], R=["cvf"], W=[], semkey="dma_cvf")
            P.stage("cvfdma_%d" % l)
            cp("pool", halo[l][:], qp[:, :, T:T + 3], R=[qk_], W=["halo%d" % l])
            P.stage("halo_%d" % l)
            tmpv = vf[:, :8 * T].rearrange("p (b t) -> p b t", b=8)
            vacc = vc[:, :8 * T].rearrange("p (b t) -> p b t", b=8)
            for gi in range(3):
                e_ = "dve" if gi < 2 else "pool"
                accv = qkf[:, gi * 8:(gi + 1) * 8, :T] if gi < 2 else vacc
                acck = "qkf" if gi < 2 else "vc"
                for j in range(CW):
                    wj = cwc[:, l, gi * 8:(gi + 1) * 8, j:j + 1].to_broadcast([128, 8, T])
                    src = qp[:, gi * 8:(gi + 1) * 8, j:j + T]
                    if j == 0:
                        tt(e_, accv, src, wj, ALU.mult, R=[qk_, "cwc"], W=[acck])
                    else:
                        tt(e_, tmpv, src, wj, ALU.mult, R=[qk_, "cwc"], W=["vf"])
                        tt(e_, accv, accv, tmpv, ALU.add, R=[acck, "vf"], W=[acck])
            P.stage("taps_%d" % l)
            act(qkf[:, :, :T], qkf[:, :, :T], AF.Silu, R=["qkf"], W=["qkf"])
            act(vT[:, :, :T], vacc, AF.Silu, R=["vc"], W=["vT"])
            P.stage("silu_%d" % l)
            tt("dve", sqb[:, :, :T], qkf[:, :, :T], qkf[:, :, :T], ALU.mult, R=["qkf"], W=["sqb"])
            hpg = 4
            for hg in range(0, 16, hpg):
                ph, pk = ps_half()
                pv = ph[:, :hpg * T].rearrange("p (b t) -> p b t", b=hpg)
                for b in range(hpg):
                    mm(pv[:, b, :], onesb[:], sqb[:, hg + b, :T], True, True, R=["onesb", "sqb"], W=[pk])
                rv = rstd[:, :hpg * T].rearrange("p (b t) -> p b t", b=hpg)
                if hg < 8:
                    act(rv, pv, AF.Ln, R=[pk, "epsq"], W=["rstd"], bias=epsq[:], scale=128.0)
                else:
                    act(rv, pv, AF.Ln, R=[pk, "epsc"], W=["rstd"], bias=epsc[:], scale=1.0)
                act(rv, rv, AF.Exp, R=["rstd"], W=["rstd"], scale=-0.5)
                tt("dve", qkT[:, hg:hg + hpg, :T], qkf[:, hg:hg + hpg, :T], rv, ALU.mult, R=["qkf", "rstd"], W=["qkT"])
            P.stage("conv_%d" % l)
            def zfill(g):
                wt, wk, wi_ = w_get("z%d" % g)

                def ev(m0, nb, pv, pk, g=g):
                    act(zsT[:, g * 4 + m0:g * 4 + m0 + nb, :T], pv, AF.Silu, R=[pk], W=["zsT"])
                fm_proj(T, wt, wk, hT, "hT", 4, ev)
                w_done(wi_)
            for c in range(NCH):
                delta_chunk(l, T, C, c * C, hoisted=(NCH == 1), fillers=([zfill] if c == NCH - 1 else []))

            P.stage("delta_%d" % l)
            tt("dve", sqb[:, 0:8, :T], oT[:, :, :T], oT[:, :, :T], ALU.mult, R=["oT"], W=["sqb"])
            for hg in range(0, 8, hpg):
                ph, pk = ps_half()
                pv = ph[:, :hpg * T].rearrange("p (b t) -> p b t", b=hpg)
                for b in range(hpg):
                    mm(pv[:, b, :], onesb[:], sqb[:, hg + b, :T], True, True, R=["onesb", "sqb"], W=[pk])
                rv = rstd[:, :hpg * T].rearrange("p (b t) -> p b t", b=hpg)
                act(rv, pv, AF.Ln, R=[pk, "epsc"], W=["rstd"], bias=epsc[:], scale=1.0 / 128.0)
                act(rv, rv, AF.Exp, R=["rstd"], W=["rstd"], scale=-0.5)
                tt("dve", oT[:, hg:hg + hpg, :T], oT[:, hg:hg + hpg, :T], rv, ALU.mult, R=["oT", "rstd"], W=["oT"])
            stt("dve", zbT[:, :, :T], oT[:, :, :T], onc[:, l:l + 1], zsT[:, :, :T], ALU.mult, ALU.mult,
                R=["oT", "onc", "zsT"], W=["zbT"])
            P.stage("onorm_%d" % l)
            for g in range(2):
                wt, wk, wi_ = w_get("ga%d" % g)

                def ev(m0, nb, pv, pk, g=g):
                    act(sga[:, g * 4 + m0:g * 4 + m0 + nb, :T], pv, AF.Sigmoid, R=[pk], W=["sga"])
                fm_proj(T, wt, wk, hT, "hT", 4, ev)
                w_done(wi_)
                wt, wk, wi_ = w_get("gb%d" % g)

                def ev(m0, nb, pv, pk, g=g):
                    act(sgb[:, g * 4 + m0:g * 4 + m0 + nb, :T], pv, AF.Sigmoid, R=[pk], W=["sgb"])
                fm_proj(T, wt, wk, hT, "hT", 4, ev)
                w_done(wi_)
                wt, wk, wi_ = w_get("pa%d" % g)

                def ev(m0, nb, pv, pk, g=g):
                    tt("dve", mtf[:, m0:m0 + nb, :T], pv, sga[:, g * 4 + m0:g * 4 + m0 + nb, :T], ALU.mult,
                       R=[pk, "sga"], W=["mtf"])
                fm_proj(T, wt, wk, uT, "uT", 4, ev)
                w_done(wi_)
                wt, wk, wi_ = w_get("pb%d" % g)

                def ev(m0, nb, pv, pk, g=g):
                    tt("dve", mixT[:, g * 4 + m0:g * 4 + m0 + nb, :T], pv, sgb[:, g * 4 + m0:g * 4 + m0 + nb, :T], ALU.mult,
                       R=[pk, "sgb"], W=["mixT"])
                    tt("pool", mixT[:, g * 4 + m0:g * 4 + m0 + nb, :T], mixT[:, g * 4 + m0:g * 4 + m0 + nb, :T],
                       mtf[:, m0:m0 + nb, :T], ALU.add, R=["mixT", "mtf"], W=["mixT"])
                fm_proj(T, wt, wk, zbT, "zbT", 4, ev)
                w_done(wi_)
            for g in range(2):
                wt, wk, wi_ = w_get("wo%d" % g)

                def ev(m0, nb, pv, pk, g=g):
                    tt("dve", xT[:, g * 4 + m0:g * 4 + m0 + nb, :T], pv, xT[:, g * 4 + m0:g * 4 + m0 + nb, :T], ALU.add,
                       R=[pk, "xT"], W=["xT"])
                fm_proj(T, wt, wk, mixT, "mixT", 4, ev)
                w_done(wi_)
            P.stage("merge_%d" % l)
            rmsnorm_fm(T, ln2c[:, l, :], "ln2c", hT, "hT")
            acc = pd[3][:, :8 * T].rearrange("p (m t) -> p m t", m=8)
            rlb = [rl, rl2]

            def ffn_up(g):
                wt, wk, wi_ = w_get("up%d" % g)
                hb = hid[g % 2]
                hk = "hid%d" % (g % 2)
                rb_ = rlb[g % 2]
                rk_ = "rl%d" % (g % 2)

                def ev(m0, nb, pv, pk, hb=hb, hk=hk):
                    act(rb_[:, m0:m0 + nb, :T], pv, AF.Relu, R=[pk], W=[rk_])
                    tt("pool", hb[:, m0:m0 + nb, :T], rb_[:, m0:m0 + nb, :T], rb_[:, m0:m0 + nb, :T], ALU.mult,
                       R=[rk_], W=[hk])
                fm_proj(T, wt, wk, hT, "hT", 4, ev)
                w_done(wi_)

            def ffn_down(g):
                hb = hid[g % 2]
                hk = "hid%d" % (g % 2)
                wt, wk, wi_ = w_get("dn%d" % g)
                wdv = wt[:].rearrange("p a b -> p (a b)").rearrange("p (kc n) -> p kc n", kc=4)
                for m in range(8):
                    for kc in range(4):
                        last = (m == 7 and kc == 3)
                        first_in_bank = (g == 0 and kc == 0 and (m * T) % 512 == 0)
                        P.op("pe", "matmul", acc[:, m, :], lhsT=wdv[:, kc, m * 128:(m + 1) * 128], rhs=hb[:, kc, :T],
                             start=first_in_bank, stop=(g == 7 and kc == 3), skip_group_check=True,
                             R=[wk, hk], W=["pd3a", "pd3b"], inc=(last or (g == 7 and kc == 3)))
                w_done(wi_)

            ffn_up(0)
            for g in range(8):
                if g + 1 < 8:
                    ffn_up(g + 1)
                ffn_down(g)
            tt("dve", xT[:, :, :T], acc, xT[:, :, :T], ALU.add, R=["pd3a", "pd3b", "xT"], W=["xT"])
            P.stage("ffn_end_%d_%s" % (l, seq["kind"] + str(seq["idx"])))

        def delta_prep(l, T, C, c0):
            hp = min(8, 512 // C)
            nhalf = 8 // hp
            gtri, DTi = B1, B1
            ngb, Ds, Atmp = B2, B2, B2

            def w8(ps_ap, np_):
                return ps_ap[:np_, :8 * C].rearrange("p (h x) -> p h x", h=8)
            ph, pk = ps_half()
            for kc in range(8):
                mm(ph[:C, :16], hT[:, kc, c0:c0 + C], wba[:, l, kc, :], kc == 0, kc == 7, R=["hT", "wba"], W=[pk])
            cp("dve", ba[:C, :], ph[:C, :16], R=[pk], W=["ba"])
            act(beta[:C, :], ba[:C, 0:8], AF.Sigmoid, R=["ba"], W=["beta"])
            tt("dve", gx[:C, :], ba[:C, 8:16], dtbb[:C, l, :], ALU.add, R=["ba", "dtbb"], W=["gx"])
            ts("dve", gm[:C, :], gx[:C, :], 0.0, ALU.max, R=["gx"], W=["gm"])
            stt("dve", gn[:C, :], gm[:C, :], -2.0, gx[:C, :], ALU.mult, ALU.add, R=["gm", "gx"], W=["gn"])
            act(gn[:C, :], gn[:C, :], AF.Exp, R=["gn"], W=["gn"])
            act(gn[:C, :], gn[:C, :], AF.Ln, R=["gn"], W=["gn"], bias=1.0, scale=1.0)
            tt("dve", gn[:C, :], gn[:C, :], gm[:C, :], ALU.add, R=["gn", "gm"], W=["gn"])
            tt("dve", gg[:C, :], gn[:C, :], nexpA[:C, l, :], ALU.mult, R=["gn", "nexpA"], W=["gg"])
            yield
            P.stage("d_g_%d" % l)
            ph, pk = ps_half()
            mm(ph[:C, 0:8], trif[:C, :C], gg[:C, :], True, True, R=["trif", "gg"], W=[pk], inc=False)
            mm(ph[:C, 8:16], trigt[:C, :C], gg[:C, :], True, True, R=["trigt", "gg"], W=[pk])
            act(egc[:C, :], ph[:C, 0:16], AF.Exp, R=[pk], W=["egc"])
            stt("dve", nbe[:C, :], beta[:C, :], -1.0, egc[:C, 0:8], ALU.mult, ALU.mult, R=["beta", "egc"], W=["nbe"])
            P.stage("d_col_%d" % l)
            tt("pool", gtri[:C, :, :C], trif[:C, :C].unsqueeze(1).to_broadcast([C, 8, C]),
               gg[:C, :].unsqueeze(2).to_broadcast([C, 8, C]), ALU.mult, R=["trif", "gg"], W=["B1"])
            ts("dve", ngb[:C, :, :C], gg[:C, :].unsqueeze(2).to_broadcast([C, 8, C]), -1.0, ALU.mult,
               R=["gg"], W=["B2"])
            yield
            P.stage("d_rhs_%d" % l)
            pa, pak = ps_pair()
            pb, pbk = ps_pair()
            pc, pck = ps_pair()

            pav, pbv, pcv = w8(pa, 128), w8(pb, C), w8(pc, C)
            for hh in range(nhalf):
                hs = slice(hh * hp, (hh + 1) * hp)
                ka, kb_, kc_ = pak[hh if nhalf > 1 else 0], pbk[hh if nhalf > 1 else 0], pck[hh if nhalf > 1 else 0]
                mm(pav[:, hs, :], onesf[:C, :], gtri[:C, hs, :C], True, True, R=["onesf", "B1"], W=[ka])
                mm(pbv[:, hs, :], onesf[:C, :C], gtri[:C, hs, :C], True, False, R=["onesf", "B1"], W=[kb_])
                mm(pbv[:, hs, :], trif[:C, :C], ngb[:C, hs, :C], False, False, R=["trif", "B2"], W=[kb_])
                mm(pbv[:, hs, :], identb[:C, :C], maskb[:C, hs, :C], False, True, R=["identb", "maskb"], W=[kb_])
                mm(pcv[:, hs, :], nonesf[:C, :C], gtri[:C, hs, :C], True, False, R=["nonesf", "B1"], W=[kc_])
                mm(pcv[:, hs, :], ntrif[:C, :C], ngb[:C, hs, :C], False, False, R=["ntrif", "B2"], W=[kc_])
                mm(pcv[:, hs, :], identb[:C, :C], maskc[:C, hs, :C], False, True, R=["identb", "maskc"], W=[kc_])
            act(egam[:, :, :C], pav, AF.Exp, R=pak, W=["egam"])
            act(DTi[:C, :, :C], pbv, AF.Exp, R=pbk, W=["B1"])
            act(Ds[:C, :, :C], pcv, AF.Exp, R=pck, W=["B2"])
            tt("pool", NA[1][:C, :, :C], identf[:C, :C].unsqueeze(1).to_broadcast([C, 8, C]),
               beta[:C, :].unsqueeze(2).to_broadcast([C, 8, C]), ALU.mult, R=["identf", "beta"], W=["NA1"])
            pbb, pbbk = ps_pair()
            pbbv = w8(pbb, C)
            for hh in range(nhalf):
                hs = slice(hh * hp, (hh + 1) * hp)
                mm(pbbv[:, hs, :], onesf[:C, :C], NA[1][:C, hs, :C], True, True, R=["onesf", "NA1"],
                   W=[pbbk[hh if nhalf > 1 else 0]])
            cp("act", NA[1][:C, :, :C], pbbv, R=pbbk, W=["NA1"])
            yield

        def delta_chunk(l, T, C, c0, hoisted=False, fillers=()):
            hp = min(8, 512 // C)
            nhalf = 8 // hp

            def wide(ps, C2):
                return ps[:, :].rearrange("p (h x) -> p h x", h=8) if False else None

            qv = qkT[:, 0:8, c0:c0 + C]
            kv = qkT[:, 8:16, c0:c0 + C]
            gtri, DTi = B1, B1
            ngb, Ds, Atmp = B2, B2, B2
            Yf, rtmp = B4, B4

            def w8(ps_ap, np_):
                return ps_ap[:np_, :8 * C].rearrange("p (h x) -> p h x", h=8)
            if not hoisted:
                for _ in delta_prep(l, T, C, c0):
                    pass
            P.stage("d_exp_%d" % l)
            pkk, pkkk = ps_pair()
            pqk, pqkk = ps_pair()
            pkkv, pqkv = w8(pkk, C), w8(pqk, C)
            for h in range(8):
                k_ = pkkk[(h // hp) if nhalf > 1 else 0]
                mm(pkkv[:, h, :], kv[:, h, :], kv[:, h, :], True, True, R=["qkT"], W=[k_], inc=(h % hp == hp - 1))
            for h in range(8):
                k_ = pqkk[(h // hp) if nhalf > 1 else 0]
                mm(pqkv[:, h, :], kv[:, h, :], qv[:, h, :], True, True, R=["qkT"], W=[k_], inc=(h % hp == hp - 1))
            tt("dve", PT[:C, :, :C], pqkv, DTi[:C, :, :C], ALU.mult, R=pqkk + ["B1"], W=["PT"])
            tt("dve", Atmp[:C, :, :C], pkkv, Ds[:C, :, :C], ALU.mult, R=pkkk + ["B2"], W=["B2"])
            P.stage("d_kk_%d" % l)
            NTA = [B2, NTA1]
            NTk = ["B2", "NTA1"]
            NAk = ["NA0", "NA1"]
            tt("dve", B2[:C, :, :C], B2[:C, :, :C], beta[:C, :].unsqueeze(2).to_broadcast([C, 8, C]), ALU.mult,
               R=["B2", "beta"], W=["B2"])
            P.stage("d_A_%d" % l)
            tt("dve", NA[0][:C, :, :C], pkkv, DTi[:C, :, :C], ALU.mult, R=pkkk + ["B1"], W=["NA0"])
            tt("dve", NA[0][:C, :, :C], NA[0][:C, :, :C], NA[1][:C, :, :C], ALU.mult, R=["NA0", "NA1"], W=["NA0"])
            tt("dve", NA[0][:C, :, :C], NA[0][:C, :, :C], offd[:C, :C].unsqueeze(1).to_broadcast([C, 8, C]), ALU.mult,
               R=["NA0", "offd"], W=["NA0"])
            P.stage("d_tr_%d" % l)
            tt("dve", Yf[:C, :, :C], identf[:C, :C].unsqueeze(1).to_broadcast([C, 8, C]), NA[0][:C, :, :C], ALU.subtract,
               R=["identf", "NA0"], W=["B4"])
            nlev = max(1, int(np.ceil(np.log2(C))) - 1)
            cur = 0
            for lev in range(nlev):
                nxt = 1 - cur
                lastlev = (lev == nlev - 1)
                Nc, NTc = NA[cur], NTA[cur]
                Nck, NTck = NAk[cur], NTk[cur]
                p2t, p2tk = ps_pair()
                p2tv = w8(p2t, C)
                for h in range(8):
                    k_ = p2tk[(h // hp) if nhalf > 1 else 0]
                    mm(p2tv[:, h, :], Nc[:C, h, :C], NTc[:C, h, :C], True, True, R=[Nck, NTck], W=[k_],
                       inc=(h % hp == hp - 1))
                if not lastlev:
                    p2, p2k = ps_pair()
                    p2v = w8(p2, C)
                    for h in range(8):
                        k_ = p2k[(h // hp) if nhalf > 1 else 0]
                        mm(p2v[:, h, :], NTc[:C, h, :C], Nc[:C, h, :C], True, True, R=[Nck, NTck], W=[k_],
                           inc=(h % hp == hp - 1))
                cp("act", NTA[nxt][:C, :, :C], p2tv, R=p2tk, W=[NTk[nxt]])
                if not lastlev:
                    cp("dve", NA[nxt][:C, :, :C], p2v, R=p2k, W=[NAk[nxt]])
                py, pyk = ps_pair()
                pyv = w8(py, C)
                for h in range(8):
                    k_ = pyk[(h // hp) if nhalf > 1 else 0]
                    mm(pyv[:, h, :], NTA[nxt][:C, h, :C], Yf[:C, h, :C], True, True, R=[NTk[nxt], "B4"], W=[k_],
                       inc=(h % hp == hp - 1))
                tt("dve", Yf[:C, :, :C], Yf[:C, :, :C], pyv, ALU.add, R=["B4"] + pyk, W=["B4"])
                cur = nxt
            cp("pool", Yb[:C, :, :C], Yf[:C, :, :C], R=["B4"], W=["Yb"])
            P.stage("d_neu_%d" % l)
            pt_, ptk = ps_half()
            ptv = pt_.bitcast(BF16)[:C, :1024].rearrange("p (h x) -> p h x", h=8)
            for h in range(8):
                tr(ptv[:, h, :], vT[:, h, c0:c0 + C], identb[:], R=["vT", "identb"], W=[ptk])
            tt("dve", bv[:C, :, :], ptv, beta[:C, :].unsqueeze(2).to_broadcast([C, 8, 128]), ALU.mult,
               R=[ptk, "beta"], W=["bv"])
            pt2, pt2k = ps_half()
            pt2v = pt2.bitcast(BF16)[:C, :1024].rearrange("p (h x) -> p h x", h=8)
            for h in range(8):
                tr(pt2v[:, h, :], kv[:, h, :], identb[:], R=["qkT", "identb"], W=[pt2k])
            tt("dve", kg[:C, :, :], pt2v, egc[:C, 8:16].unsqueeze(2).to_broadcast([C, 8, 128]), ALU.mult,
               R=[pt2k, "egc"], W=["kg"])
            tt("pool", qg[:, :, :C], qv, egam[:, :, :C], ALU.mult, R=["qkT", "egam"], W=["qg"])
            P.stage("d_tok_%d" % l)
            Sl, Sbl = S[l], Sb[l]
            Sk, Sbk = "S%d" % l, "Sb%d" % l
            pks_, pksk = ps_pair()
            pksv = pks_[:C, :].rearrange("p (h x) -> p h x", h=8)
            for h in range(8):
                mm(pksv[:, h, :], kv[:, h, :], Sbl[:, h, :], True, True, R=["qkT", Sbk], W=[pksk[h // 4]],
                   inc=(h % 4 == 3))
            for f_ in fillers:
                f_(0)
            tt("dve", rtmp[:C, :, :], pksv, nbe[:C, :].unsqueeze(2).to_broadcast([C, 8, 128]), ALU.mult,
               R=pksk + ["nbe"], W=["B4"])
            tt("dve", rb[:C, :, :], rtmp[:C, :, :], bv[:C, :, :], ALU.add, R=["B4", "bv"], W=["rb"])
            pvn, pvnk = ps_pair()
            pvnv = pvn[:C, :].rearrange("p (h x) -> p h x", h=8)
            for h in range(8):
                mm(pvnv[:, h, :], Yb[:C, h, :C], rb[:C, h, :], True, True, R=["Yb", "rb"], W=[pvnk[h // 4]],
                   inc=(h % 4 == 3))
            for f_ in fillers:
                f_(1)
            cp("act", vnw[:C, :, :], pvnv, R=pvnk, W=["vnw"])
            po, pok = ps_pair()
            pov = w8(po, 128)
            for h in range(8):
                k_ = pok[(h // hp) if nhalf > 1 else 0]
                mm(pov[:, h, :], Sbl[:, h, :], qg[:, h, :C], True, False, R=[Sbk, "qg"], W=[k_])
                mm(pov[:, h, :], vnw[:C, h, :], PT[:C, h, :C], False, True, R=["vnw", "PT"], W=[k_],
                   inc=(h % hp == hp - 1))
            cp("act", oT[:, :, c0:c0 + C], pov, R=pok, W=["oT"])
            psn, psnk = ps_pair()
            psnv = psn[:, :].rearrange("p (h x) -> p h x", h=8)
            for h in range(8):
                mm(psnv[:, h, :], kg[:C, h, :], vnw[:C, h, :], True, True, R=["kg", "vnw"], W=[psnk[h // 4]],
                   inc=(h % 4 == 3))
            tt("dve", Sl[:], Sl[:], egam[:, :, C - 1:C].to_broadcast([128, 8, 128]), ALU.mult, R=[Sk, "egam"], W=[Sk])
            tt("dve", Sl[:], Sl[:], psnv, ALU.add, R=[Sk] + psnk, W=[Sk])
            cp("act", Sbl[:], Sl[:], R=[Sk], W=[Sbk])

        seqs = [
            {"kind": "p", "idx": 0, "L": SEQ, "T": 128, "C": 128},
            {"kind": "p", "idx": 1, "L": SEQ, "T": 128, "C": 128},
            {"kind": "s", "idx": 0, "L": DEC_SEQ, "T": DEC_SEQ, "C": DEC_SEQ},
        ]
        for sq in seqs:
            ntile = sq["L"] // sq["T"]
            for _ in range(ntile):
                for l in range(DEPTH):
                    wplan.extend(layer_plan(l))

        for sq in seqs:
            T, C = sq["T"], sq["C"]
            ntile = sq["L"] // T
            for l in range(DEPTH):
                if sq["kind"] == "p":
                    P.op("pool", "memset", S[l][:], 0.0, W=["S%d" % l])
                    P.op("pool", "memset", Sb[l][:], 0.0, W=["Sb%d" % l])
                    P.op("pool", "memset", halo[l][:], 0.0, W=["halo%d" % l])
                else:
                    P.dma("sp", S[l][:], sdelta[l].rearrange("h d v -> d h v"), R=[], W=["S%d" % l], semkey="dma_Sin%d" % l)
                    cp("act", Sb[l][:], S[l][:], R=["S%d" % l], W=["Sb%d" % l])
                    P.dma("sp", cvf[:], sconv[l], R=[], W=["cvf"], semkey="dma_cvfin")
                    cp("dve", halo[l][:], cvf[:], R=["cvf"], W=["halo%d" % l])
            for ti in range(ntile):
                t0 = ti * T
                src = (xT_p[sq["idx"], :, :, t0:t0 + T] if sq["kind"] == "p" else xT_s[0, :, :, :])
                P.dma("sp", xT[:, :, :T], src.rearrange("kc p t -> p kc t"), R=[], W=["xT"], semkey="dma_xT")
                for l in range(DEPTH):
                    layer(l, T, C, sq, ti == 0, ti == ntile - 1)
                tt("dve", sqb[:, 0:8, :T], xT[:, :, :T], xT[:, :, :T], ALU.mult, R=["xT"], W=["sqb"])
                ph, pk = ps_half()
                for kc in range(8):
                    mm(ph[:, :T], onesb[:], sqb[:, kc, :T], kc == 0, kc == 7, R=["onesb", "sqb"], W=[pk])
                act(rstd[:, :T], ph[:, :T], AF.Ln, R=[pk, "epsc"], W=["rstd"], bias=epsc[:], scale=1.0 / D)
                act(rstd[:, :T], rstd[:, :T], AF.Exp, R=["rstd"], W=["rstd"], scale=-0.5)
                for kc in range(8):
                    stt("dve", oT[:, kc, :T], xT[:, kc, :T], fnc[:, kc:kc + 1], rstd[:, :T], ALU.mult, ALU.mult,
                        R=["xT", "rstd", "fnc"], W=["oT"])
                dst = (yT_p[sq["idx"], :, :, t0:t0 + T] if sq["kind"] == "p" else yT_s[0, :, :, :])
                P.dma("sp", dst.rearrange("kc p t -> p kc t"), oT[:, :, :T], R=["oT"], W=[], semkey="dma_y")
                P.stage("tile_end_%s_%d" % (sq["kind"] + str(sq["idx"]), ti))
            for l in range(DEPTH):
                dst = (ndl_p[l, sq["idx"]] if sq["kind"] == "p" else ndl_s[l, 0])
                P.dma("sp", dst.rearrange("h d v -> d h v"), S[l][:], R=["S%d" % l], W=[], semkey="dma_sout%d" % l)
        import os as _os
        if _os.environ.get("K_DELAY"):
            P.dead = False
            for _i in range(int(_os.environ["K_DELAY"])):
                P.op("pe", "matmul", pd[0][:, 0:512], lhsT=onesb[:], rhs=wbuf[0][:, 0, :], start=True, stop=True,
                     R=["onesb", "wbuf0"], W=["pd0a"], inc=(_i % 64 == 63))
            P.dead = True
        assert P.dead or wstate["consumed"] == len(wplan), (wstate, len(wplan))
        for sk, v in P.dma_cnt.items():
            P._need("sp", sk, v)
        for e_ in ("pe", "act", "dve", "pool"):
            P._need("sp", e_, P.cnt[e_])
        build.ninst = P.ninst
    return nc


def _prep_inputs(inputs, SEQ):
    f = lambda a: np.ascontiguousarray(np.asarray(a, dtype=np.float32))
    xp = f(inputs["x_prompt"])[:, :SEQ]
    xs = f(inputs["x_sample"])
    xpT = np.ascontiguousarray(xp.reshape(16, SEQ, 8, 128).transpose(0, 2, 3, 1))
    xsT = np.ascontiguousarray(xs.reshape(8, DEC_SEQ, 8, 128).transpose(0, 2, 3, 1))
    sc = f(inputs["state_conv"])
    scT = np.ascontiguousarray(sc.reshape(DEPTH, 8, 3, 24, 128).transpose(1, 0, 4, 3, 2))
    sd = f(inputs["state_delta"])
    rep = lambda a: np.ascontiguousarray(np.broadcast_to(a[None], (128,) + a.shape))
    common = {
        "ln1c": np.ascontiguousarray(f(inputs["ln1"]).reshape(DEPTH, 8, 128).transpose(2, 0, 1)),
        "ln2c": np.ascontiguousarray(f(inputs["ln2"]).reshape(DEPTH, 8, 128).transpose(2, 0, 1)),
        "fnc": np.ascontiguousarray(f(inputs["final_norm"]).reshape(8, 128).transpose(1, 0)),
        "cwc": np.ascontiguousarray(f(inputs["conv_w"]).reshape(DEPTH, 24, 128, CW).transpose(2, 0, 1, 3)),
        "onc": np.ascontiguousarray(f(inputs["o_norm"]).transpose(1, 0)),
        "algb": rep(f(inputs["a_ln_g"])),
        "albb": rep(f(inputs["a_ln_b"])),
        "wstT": np.ascontiguousarray(f(inputs["w_s"]).transpose(3, 0, 1, 2)),
        "bsr": np.ascontiguousarray(f(inputs["b_s"]).reshape(1, DEPTH, D)),
        "alogb": rep(f(inputs["a_log"])),
        "dtbb": rep(f(inputs["dt_bias"])),
        "w_in": f(inputs["w_in"]), "p_a": f(inputs["p_a"]), "p_b": f(inputs["p_b"]), "w_o": f(inputs["w_o"]),
        "w_up": f(inputs["w_up"]), "w_down": f(inputs["w_down"]),
    }
    maps = []
    for c in range(NCORES):
        m = dict(common)
        m["xT_p"] = np.ascontiguousarray(xpT[2 * c:2 * c + 2])
        m["xT_s"] = np.ascontiguousarray(xsT[c:c + 1])
        m["sconv"] = np.ascontiguousarray(scT[c])
        m["sdelta"] = np.ascontiguousarray(sd[:, c])
        maps.append(m)
    return maps


def _assemble(results, SEQ):
    yp = np.concatenate([r["yT_p"] for r in results], axis=0)
    y_prompt = np.ascontiguousarray(yp.transpose(0, 3, 1, 2).reshape(16, SEQ, D))
    ys = np.concatenate([r["yT_s"] for r in results], axis=0)
    y_sample = np.ascontiguousarray(ys.transpose(0, 3, 1, 2).reshape(8, DEC_SEQ, D))
    cvp = np.concatenate([r["ncv_p"] for r in results], axis=1)
    new_conv_prompt = np.ascontiguousarray(cvp.transpose(0, 1, 4, 3, 2).reshape(DEPTH, 16, 3, 3072))
    new_delta_prompt = np.ascontiguousarray(np.concatenate([r["ndl_p"] for r in results], axis=1))
    cvs = np.concatenate([r["ncv_s"] for r in results], axis=1)
    new_conv_sample = np.ascontiguousarray(cvs.transpose(0, 1, 4, 3, 2).reshape(DEPTH, 8, 3, 3072))
    new_delta_sample = np.ascontiguousarray(np.concatenate([r["ndl_s"] for r in results], axis=1))
    new_gv = np.ascontiguousarray(np.concatenate([r["ngv_s"] for r in results], axis=1))
    outs = (y_prompt, y_sample, new_conv_prompt, new_delta_prompt, new_conv_sample, new_delta_sample, new_gv)
    return tuple(np.asarray(o, dtype=np.float32) for o in outs)


def run(inputs, SEQ, stop_at=None, lite=False, start_at=None):
    nc = build(SEQ, stop_at, lite, start_at)
    maps = _prep_inputs(inputs, SEQ)
    if lite:
        for m in maps:
            for k in ("w_in", "p_a", "p_b", "w_o", "w_up", "w_down"):
                m.pop(k)
    res = run_bass_kernel_spmd(nc, maps, core_ids=list(range(NCORES)))
    return _assemble(res.results, SEQ)


def kernel(**inputs):
    return run(inputs, SEQ_FULL)
```

```python
import numpy as np
from contextlib import ExitStack
import concourse.bass as bass
import concourse.mybir as mybir
from concourse.bass_utils import run_bass_kernel_spmd

F32 = mybir.dt.float32
BF16 = mybir.dt.bfloat16
AF = mybir.ActivationFunctionType
ALU = mybir.AluOpType
AX = mybir.AxisListType

NCORES = 8
D = 1024
DEPTH = 2
NIN = 8208
DFF = 4096
EPS = 1e-6
NEG = -30000.0
import os as _os0
NOSELF = set((_os0.environ.get("K_NOSELF") or "").split(",")) - {""}
SEQ_FULL = 4096
DEC_SEQ = 16
CW = 4


class Prog:
    def __init__(self, nc, es):
        self.nc = nc
        self.es = es
        self.eng = {"pe": nc.tensor, "act": nc.scalar, "dve": nc.vector, "pool": nc.gpsimd, "sp": nc.sync}
        self.sem = {k: es.enter_context(nc.semaphore("sem_" + k)) for k in self.eng}
        self.cnt = {k: 0 for k in self.eng}
        self.waited = {k: {} for k in self.eng}
        self.lastw = {}
        self.readers = {}
        self.dma_sems = {}
        self.dma_cnt = {}
        self.ninst = 0
        self.dead = False
        self.stop_at = None

    def stage(self, name):
        if self.stop_at is not None and name == self.stop_at:
            self.dead = True
            self.stopped = True
        elif getattr(self, "start_at", None) is not None and not getattr(self, "stopped", False):
            if name == "setup":
                self.dead = True
            elif name == self.start_at:
                self.dead = False

    def _need(self, e, semkey, val):
        if val <= 0:
            return
        w = self.waited[e]
        if w.get(semkey, 0) >= val:
            return
        w[semkey] = val
        s = self.sem[semkey] if semkey in self.sem else self.dma_sems[semkey]
        self.eng[e].wait_ge(s, val)
        self.ninst += 1

    def _deps(self, e, reads, writes):
        skip_self = (e == "pe") or (e in NOSELF)
        for k in reads:
            lw = self.lastw.get(k)
            if lw and not (skip_self and lw[0] == e):
                self._need(e, lw[0], lw[1])
        for k in writes:
            lw = self.lastw.get(k)
            if lw and not (skip_self and lw[0] == e):
                self._need(e, lw[0], lw[1])
            for sk, v in self.readers.get(k, {}).items():
                if not (skip_self and sk == e):
                    self._need(e, sk, v)

    def _commit(self, semkey, val, reads, writes):
        for k in writes:
            self.lastw[k] = (semkey, val)
            self.readers[k] = {}
        for k in reads:
            r = self.readers.setdefault(k, {})
            if r.get(semkey, 0) < val:
                r[semkey] = val

    def op(self, e, fn, *args, R=(), W=(), inc=True, **kw):
        if self.dead:
            return None
        W = list(W) + [k for k in R if k.startswith("pd") and k not in W]
        self._deps(e, R, W)
        ins = getattr(self.eng[e], fn)(*args, **kw)
        self.ninst += 1
        if inc:
            self.cnt[e] += 1
            ins.then_inc(self.sem[e], 1)
            self._commit(e, self.cnt[e], R, W)
        else:
            self._commit(e, self.cnt[e] + 1, R, W)
        return ins

    def dma(self, e, out, in_, R, W, semkey):
        if self.dead:
            return None
        if semkey not in self.dma_sems:
            self.dma_sems[semkey] = self.es.enter_context(self.nc.semaphore(semkey))
            self.dma_cnt[semkey] = 0
        self._deps(e, R, W)
        ins = self.eng[e].dma_start(out=out, in_=in_)
        self.dma_cnt[semkey] += 16
        ins.then_inc(self.dma_sems[semkey], 16)
        self._commit(semkey, self.dma_cnt[semkey], R, W)
        self.ninst += 1
        return ins

    def finish(self, e="sp"):
        for k, (sk, v) in list(self.lastw.items()):
            self._need(e, sk, v)


def build(SEQ, stop_at=None, lite=False, start_at=None):
    nc = bass.Bass("TRN2", target_bir_lowering=False)
    es = ExitStack()

    def din(name, shape, dt=F32):
        return nc.dram_tensor(name, list(shape), dt, kind="ExternalInput").ap()

    def dout(name, shape, dt=F32):
        return nc.dram_tensor(name, list(shape), dt, kind="ExternalOutput").ap()

    def dscr(name, shape, dt=BF16):
        return nc.dram_tensor(name, list(shape), dt, kind="Internal").ap()

    xT_p = din("xT_p", [2, 8, 128, SEQ])
    xT_s = din("xT_s", [1, 8, 128, DEC_SEQ])
    sconv = din("sconv", [DEPTH, 128, 24, 3])
    sdelta = din("sdelta", [DEPTH, 8, 128, 128])
    ln1_d = din("ln1c", [128, DEPTH, 8])
    ln2_d = din("ln2c", [128, DEPTH, 8])
    fn_d = din("fnc", [128, 8])
    cw_d = din("cwc", [128, DEPTH, 24, 4])
    on_d = din("onc", [128, DEPTH])
    alg_d = din("algb", [128, DEPTH, D])
    alb_d = din("albb", [128, DEPTH, D])
    wst_d = din("wstT", [128, DEPTH, 8, 128])
    bs_d = din("bsr", [1, DEPTH, D])
    alog_d = din("alogb", [128, DEPTH, 8])
    dtb_d = din("dtbb", [128, DEPTH, 8])
    wdecl = (lambda n, s_: dscr(n + "_lite", s_, F32)) if lite else din
    w_in_d = wdecl("w_in", [DEPTH, D, NIN])
    p_a_d = wdecl("p_a", [DEPTH, D, D])
    p_b_d = wdecl("p_b", [DEPTH, D, D])
    w_o_d = wdecl("w_o", [DEPTH, D, D])
    w_up_d = wdecl("w_up", [DEPTH, D, DFF])
    w_dn_d = wdecl("w_down", [DEPTH, DFF, D])

    yT_p = dout("yT_p", [2, 8, 128, SEQ])
    yT_s = dout("yT_s", [1, 8, 128, DEC_SEQ])
    ncv_p = dout("ncv_p", [DEPTH, 2, 128, 24, 3])
    ndl_p = dout("ndl_p", [DEPTH, 2, 8, 128, 128])
    ncv_s = dout("ncv_s", [DEPTH, 1, 128, 24, 3])
    ndl_s = dout("ndl_s", [DEPTH, 1, 8, 128, 128])
    ngv_s = dout("ngv_s", [DEPTH, 1, DEC_SEQ, D])

    w_in_b = dscr("w_in_b", [DEPTH, 16, 128, 8, 512])
    wba_b = dscr("wba_b", [DEPTH, D, 16])
    p_a_b = dscr("p_a_b", [DEPTH, 2, 128, 8, 512])
    p_b_b = dscr("p_b_b", [DEPTH, 2, 128, 8, 512])
    w_o_b = dscr("w_o_b", [DEPTH, 2, 128, 8, 512])
    w_up_b = dscr("w_up_b", [DEPTH, 8, 128, 8, 512])
    w_dn_b = dscr("w_dn_b", [DEPTH, 8, 128, 4, 1024])

    with es:
        P = Prog(nc, es)
        P.stop_at = stop_at
        P.start_at = start_at

        def sb(name, shape, dt):
            return es.enter_context(nc.sbuf_tensor(name, list(shape), dt))

        TM = 128

        identf = sb("identf", [128, 128], F32)
        identb = sb("identb", [128, 128], BF16)
        trif = sb("trif", [128, 128], F32)
        ntrif = sb("ntrif", [128, 128], F32)
        trigt = sb("trigt", [128, 128], F32)
        onesf = sb("onesf", [128, 128], F32)
        offd = sb("offd", [128, 128], F32)
        nonesf = sb("nonesf", [128, 128], F32)
        onesb = sb("onesb", [128, 128], BF16)
        maskb = sb("maskb", [128, 8, 128], BF16)
        maskc = sb("maskc", [128, 8, 128], BF16)
        epsc = sb("epsc", [128, 1], F32)
        epsq = sb("epsq", [128, 1], F32)
        ln1c = sb("ln1c_s", [128, DEPTH, 8], F32)
        ln2c = sb("ln2c_s", [128, DEPTH, 8], F32)
        fnc = sb("fnc_s", [128, 8], F32)
        cwc = sb("cwc_s", [128, DEPTH, 24, 4], F32)
        onc = sb("onc_s", [128, DEPTH], F32)
        algb = sb("algb_s", [128, DEPTH, D], F32)
        albb = sb("albb_s", [128, DEPTH, D], F32)
        wstb = sb("wstb", [128, DEPTH, 8, 128], BF16)
        bsb = sb("bsb", [1, DEPTH, D], BF16)
        alogb = sb("alogb_s", [128, DEPTH, 8], F32)
        nexpA = sb("nexpA", [128, DEPTH, 8], F32)
        dtbb = sb("dtbb_s", [128, DEPTH, 8], F32)
        wba = sb("wba", [128, DEPTH, 8, 16], BF16)

        NA = [sb("NA%d" % i, [128, 8, 128], F32) for i in range(2)]
        NTA1 = sb("NTA1", [128, 8, 128], F32)
        xT = sb("xT", [128, 8, TM], F32)
        hT = sb("hT", [128, 8, TM], BF16)
        sqb = sb("sqb", [128, 16, TM], BF16)
        rstd = sb("rstd", [128, 4 * TM], F32)
        uT = sb("uT", [128, 8, TM], BF16)
        vf = sb("vf", [128, D], F32)
        vc = sb("vc", [128, D], F32)
        vnb = sb("vnb", [128, D], BF16)
        st4 = sb("st4", [128, 8], F32)
        qkvpre = sb("qkvpre", [128, 24, 3 + TM], BF16)
        halo = [sb("halo%d" % l, [128, 24, 3], BF16) for l in range(DEPTH)]
        cvf = sb("cvf", [128, 24, 3], F32)
        qkf = sb("qkf", [128, 16, TM], F32)
        qkT = sb("qkT", [128, 16, TM], BF16)
        vT = sb("vT", [128, 8, TM], BF16)
        zsT = sb("zsT", [128, 8, TM], BF16)
        sga = sb("sga", [128, 4, TM], BF16)
        sgb = sb("sgb", [128, 4, TM], BF16)
        oT = sb("oT", [128, 8, TM], F32)
        zbT = sb("zbT", [128, 8, TM], BF16)
        mixT = sb("mixT", [128, 8, TM], BF16)
        mtf = sb("mtf", [128, 4, TM], F32)
        hid = [sb("hid%d" % i, [128, 4, TM], BF16) for i in range(2)]
        rl = sb("rl", [128, 4, TM], BF16)
        rl2 = sb("rl2", [128, 4, TM], BF16)
        S = [sb("S%d" % l, [128, 8, 128], F32) for l in range(DEPTH)]
        Sb = [sb("Sb%d" % l, [128, 8, 128], BF16) for l in range(DEPTH)]
        ba = sb("ba", [128, 16], F32)
        beta = sb("beta", [128, 8], F32)
        gx = sb("gx", [128, 8], F32)
        gm = sb("gm", [128, 8], F32)
        gn = sb("gn", [128, 8], F32)
        gg = sb("gg", [128, 8], F32)
        egc = sb("egc", [128, 16], F32)
        nbe = sb("nbe", [128, 8], F32)
        B1 = sb("B1", [128, 8, 128], F32)
        B2 = sb("B2", [128, 8, 128], F32)
        egam = sb("egam", [128, 8, 128], F32)
        B4 = sb("B4", [128, 8, 128], F32)
        PT = sb("PT", [128, 8, 128], BF16)
        Yb = sb("Yb", [128, 8, 128], BF16)
        bv = sb("bv", [128, 8, 128], BF16)
        kg = sb("kg", [128, 8, 128], BF16)
        qg = sb("qg", [128, 8, 128], BF16)
        rb = sb("rb", [128, 8, 128], BF16)
        vnw = sb("vnw", [128, 8, 128], BF16)
        NWB = 7
        wbuf = [sb("wbuf%d" % i, [128, 8, 512], BF16) for i in range(NWB)]

        def cdma(dst, src, key):
            P.dma("sp", dst, src, R=[], W=[key], semkey="dma_c_" + key)

        cdma(ln1c[:], ln1_d, "ln1c"); cdma(ln2c[:], ln2_d, "ln2c"); cdma(fnc[:], fn_d, "fnc")
        cdma(cwc[:], cw_d, "cwc"); cdma(onc[:], on_d, "onc"); cdma(algb[:], alg_d, "algb")
        cdma(albb[:], alb_d, "albb")
        cdma(alogb[:], alog_d, "alogb"); cdma(dtbb[:], dtb_d, "dtbb")

        def cast_tiled(l, dst, src, key, ntile, col0, t0):
            for kc in range(8):
                P.dma("pool", dst[l, t0:t0 + ntile, :, kc, :].rearrange("t p n -> p t n"),
                      src[l, kc * 128:(kc + 1) * 128, col0:col0 + ntile * 512].rearrange("p (t n) -> p t n", t=ntile),
                      R=[], W=[key + str(l)], semkey="dma_%s%d" % (key, l))

        for l in range(DEPTH):
            cast_tiled(l, w_in_b, w_in_d, "w_in_b", 12, 0, 0)
            cast_tiled(l, w_in_b, w_in_d, "w_in_b", 4, 6160, 12)
            for kc in range(8):
                P.dma("pool", wba_b[l, kc * 128:(kc + 1) * 128, :], w_in_d[l, kc * 128:(kc + 1) * 128, 6144:6160],
                      R=[], W=["wba_b"], semkey="dma_wba_b")
            cast_tiled(l, p_a_b, p_a_d, "p_a_b", 2, 0, 0)
            cast_tiled(l, p_b_b, p_b_d, "p_b_b", 2, 0, 0)
            cast_tiled(l, w_o_b, w_o_d, "w_o_b", 2, 0, 0)
            cast_tiled(l, w_up_b, w_up_d, "w_up_b", 8, 0, 0)
            for rblk in range(32):
                P.dma("pool", w_dn_b[l, rblk // 4, :, rblk % 4, :], w_dn_d[l, rblk * 128:(rblk + 1) * 128, :],
                      R=[], W=["w_dn_b%d" % l], semkey="dma_w_dn_b%d" % l)

        def gp(fn, *a, R=(), W=(), **kw):
            return P.op("pool", fn, *a, R=R, W=W, **kw)

        gp("memset", onesf[:], 1.0, W=["onesf"])
        gp("memset", nonesf[:], -1.0, W=["nonesf"])
        gp("memset", onesb[:], 1.0, W=["onesb"])
        gp("memset", epsc[:], EPS, W=["epsc"])
        gp("memset", epsq[:], EPS * 128.0, W=["epsq"])
        gp("affine_select", out=identf[:], in_=onesf[:], pattern=[[1, 128]], compare_op=ALU.is_equal, fill=0.0,
           base=0, channel_multiplier=-1, R=["onesf"], W=["identf"])
        gp("affine_select", out=trif[:], in_=onesf[:], pattern=[[1, 128]], compare_op=ALU.is_ge, fill=0.0,
           base=0, channel_multiplier=-1, R=["onesf"], W=["trif"])
        gp("affine_select", out=ntrif[:], in_=nonesf[:], pattern=[[1, 128]], compare_op=ALU.is_ge, fill=0.0,
           base=0, channel_multiplier=-1, R=["nonesf"], W=["ntrif"])
        gp("affine_select", out=trigt[:], in_=onesf[:], pattern=[[-1, 128]], compare_op=ALU.is_gt, fill=0.0,
           base=0, channel_multiplier=1, R=["onesf"], W=["trigt"])
        P.op("dve", "tensor_copy", out=identb[:], in_=identf[:], R=["identf"], W=["identb"])
        P.op("dve", "tensor_tensor", out=offd[:], in0=onesf[:], in1=identf[:], op=ALU.subtract, R=["onesf", "identf"], W=["offd"])
        mt0, mt1, mt2 = B1[:, 0, :], B2[:, 0, :], B4[:, 0, :]
        gp("memset", mt0, NEG, W=["B1"])
        gp("affine_select", out=mt1, in_=mt0, pattern=[[-1, 128]], compare_op=ALU.is_gt, fill=0.0,
           base=0, channel_multiplier=1, R=["B1"], W=["B2"])
        gp("affine_select", out=mt2, in_=mt0, pattern=[[1, 128]], compare_op=ALU.is_ge, fill=0.0,
           base=0, channel_multiplier=-1, R=["B1"], W=["B4"])
        for h in range(8):
            P.op("dve", "tensor_copy", out=maskb[:, h, :], in_=mt1, R=["B2"], W=["maskb"])
            P.op("dve", "tensor_copy", out=maskc[:, h, :], in_=mt2, R=["B4"], W=["maskc"])
        for l in range(DEPTH):
            P.dma("sp", egam[:], wst_d[:, l, :, :], R=[], W=["egam"], semkey="dma_c2")
            for g in range(8):
                P.op("dve", "tensor_tensor", out=wstb[:, l, g, :], in0=egam[:, g, :], in1=trif[:], op=ALU.mult,
                     R=["egam", "trif"], W=["wstb"])
            P.dma("sp", vf[0:1, :], bs_d[0:1, l, :], R=[], W=["vf"], semkey="dma_c3")
            P.op("dve", "tensor_copy", out=bsb[0:1, l, :], in_=vf[0:1, :], R=["vf"], W=["bsb"])
        P.op("act", "activation", out=nexpA[:], in_=alogb[:], func=AF.Exp, R=["alogb"], W=["nexpA"])
        P.op("dve", "tensor_scalar", out=nexpA[:], in0=nexpA[:], scalar1=-1.0, scalar2=None, op0=ALU.mult,
             R=["nexpA"], W=["nexpA"])
        for l in range(DEPTH):
            P.dma("sp", wba[:, l, :, :], wba_b[l].rearrange("(kc p) n -> p kc n", p=128),
                  R=["wba_b"], W=["wba"], semkey="dma_c_wba")

        P.stage("setup")
        pd = [es.enter_context(nc.psum_tensor("pd%d" % i, [128, 1024], F32)) for i in range(4)]
        ps_state = {"i": 0}

        def ps_half():
            i = ps_state["i"] % 6
            ps_state["i"] += 1
            t = pd[i // 2]
            return t[:, (i % 2) * 512:(i % 2) * 512 + 512], "pd%d%s" % (i // 2, "ab"[i % 2])

        def ps_pair():
            if ps_state["i"] % 2:
                ps_state["i"] += 1
            i = ps_state["i"] % 6
            ps_state["i"] += 2
            return pd[i // 2][:], ["pd%da" % (i // 2), "pd%db" % (i // 2)]

        wplan = []
        wstate = {"issued": 0, "consumed": 0}

        def layer_plan(l):
            pl = []
            names = ["u0", "u1", "v0", "v1"] + ["qkv%d" % g for g in range(6)] + ["z0", "z1"]
            for t, nm in enumerate(names):
                pl.append((nm, w_in_b[l, t], "w_in_b%d" % l))
            for g in range(2):
                pl.append(("ga%d" % g, w_in_b[l, 12 + g], "w_in_b%d" % l))
                pl.append(("gb%d" % g, w_in_b[l, 14 + g], "w_in_b%d" % l))
                pl.append(("pa%d" % g, p_a_b[l, g], "p_a_b%d" % l))
                pl.append(("pb%d" % g, p_b_b[l, g], "p_b_b%d" % l))
            for g in range(2):
                pl.append(("wo%d" % g, w_o_b[l, g], "w_o_b%d" % l))

            def up_e(g):
                return ("up%d" % g, w_up_b[l, g], "w_up_b%d" % l)

            def dn_e(g):
                return ("dn%d" % g, w_dn_b[l, g], "w_dn_b%d" % l)
            pl.append(up_e(0))
            for g in range(8):
                if g + 1 < 8:
                    pl.append(up_e(g + 1))
                pl.append(dn_e(g))
            return pl

        def w_issue_upto(n):
            while wstate["issued"] < min(n, len(wplan)):
                i = wstate["issued"]
                tag, view, srckey = wplan[i]
                slot = i % NWB
                if tag.startswith("dn"):
                    dst = wbuf[slot][:].rearrange("p a b -> p (a b)").rearrange("p (kc n) -> p kc n", kc=4)
                else:
                    dst = wbuf[slot][:]
                P.dma("sp", dst, view, R=[srckey], W=["wbuf%d" % slot], semkey="dma_wbuf%d" % slot)
                wstate["issued"] += 1

        def w_get(tag):
            i = wstate["consumed"]
            assert wplan[i][0] == tag, (wplan[i][0], tag)
            w_issue_upto(i + 1)
            wstate["consumed"] += 1
            slot = i % NWB
            return wbuf[slot], "wbuf%d" % slot, i

        def w_done(i):
            w_issue_upto(i + NWB)

        def mm(out, lhsT, rhs, start, stop, R, W, inc=None):
            if inc is None:
                inc = stop
            return P.op("pe", "matmul", out, lhsT=lhsT, rhs=rhs, start=start, stop=stop, R=R, W=W, inc=inc)

        def tr(out, in_, ident, R, W):
            return P.op("pe", "transpose", out=out, in_=in_, identity=ident, R=R, W=W)

        def act(out, in_, func, R, W, **kw):
            return P.op("act", "activation", out=out, in_=in_, func=func, R=R, W=W, **kw)

        def tt(e, out, in0, in1, op, R, W):
            return P.op(e, "tensor_tensor", out=out, in0=in0, in1=in1, op=op, R=R, W=W)

        def ts(e, out, in0, s1, op0, R, W, s2=None, op1=None):
            if op1 is None:
                return P.op(e, "tensor_scalar", out=out, in0=in0, scalar1=s1, scalar2=None, op0=op0, R=R, W=W)
            return P.op(e, "tensor_scalar", out=out, in0=in0, scalar1=s1, scalar2=s2, op0=op0, op1=op1, R=R, W=W)

        def stt(e, out, in0, scalar, in1, op0, op1, R, W):
            return P.op(e, "scalar_tensor_tensor", out=out, in0=in0, scalar=scalar, in1=in1, op0=op0, op1=op1,
                        R=R, W=W)

        def cp(e, out, in_, R, W):
            if e == "act":
                return act(out, in_, AF.Identity, R, W)
            return P.op(e, "tensor_copy", out=out, in_=in_, R=R, W=W)

        def rmsnorm_fm(T, gcol, gkey, dst, dstkey):
            tt("dve", sqb[:, 0:8, :T], xT[:, :, :T], xT[:, :, :T], ALU.mult, R=["xT"], W=["sqb"])
            ph, pk = ps_half()
            for kc in range(8):
                mm(ph[:, :T], onesb[:], sqb[:, kc, :T], kc == 0, kc == 7, R=["onesb", "sqb"], W=[pk])
            act(rstd[:, :T], ph[:, :T], AF.Ln, R=[pk, "epsc"], W=["rstd"], bias=epsc[:], scale=1.0 / D)
            act(rstd[:, :T], rstd[:, :T], AF.Exp, R=["rstd"], W=["rstd"], scale=-0.5)
            for kc in range(8):
                stt("dve", dst[:, kc, :T], xT[:, kc, :T], gcol[:, kc:kc + 1], rstd[:, :T], ALU.mult, ALU.mult,
                    R=["xT", "rstd", gkey], W=[dstkey])

        def fm_proj(T, wt, wkey, src, srckey, nblk, evac):
            bpb = 512 // T if T >= 128 else 4
            bpb = min(bpb, nblk)
            for m0 in range(0, nblk, bpb):
                ph, pk = ps_half()
                nb = min(bpb, nblk - m0)
                pv = ph[:, :nb * T].rearrange("p (b t) -> p b t", b=nb)
                for b in range(nb):
                    for kc in range(8):
                        mm(pv[:, b, :], wt[:, kc, (m0 + b) * 128:(m0 + b + 1) * 128], src[:, kc, :T], kc == 0, kc == 7,
                           R=[wkey, srckey], W=[pk])
                evac(m0, nb, pv, pk)

        def layer(l, T, C, seq, first_tile, last_tile):
            NCH = T // C
            rmsnorm_fm(T, ln1c[:, l, :], "ln1c", hT, "hT")
            prep_gen = delta_prep(l, T, C, 0) if NCH == 1 else None
            if prep_gen is not None:
                next(prep_gen)
            P.stage("norm1_%d" % l)
            for g in range(2):
                wt, wk, wi_ = w_get("u%d" % g)

                def ev(m0, nb, pv, pk, g=g):
                    act(uT[:, g * 4 + m0:g * 4 + m0 + nb, :T], pv, AF.Gelu, R=[pk], W=["uT"])
                fm_proj(T, wt, wk, hT, "hT", 4, ev)
                w_done(wi_)
            if prep_gen is not None:
                next(prep_gen)
            P.stage("u_%d" % l)
            wv = [w_get("v%d" % g) for g in range(2)]
            for c in range(NCH):
                c0 = c * C
                for g in range(2):
                    wt, wk, _ = wv[g]
                    ph, pk = ps_half()
                    for kc in range(8):
                        mm(ph[:C, :], hT[:, kc, c0:c0 + C], wt[:, kc, :], kc == 0, kc == 7, R=["hT", wk], W=[pk])
                    act(vf[:C, g * 512:(g + 1) * 512], ph[:C, :], AF.Gelu, R=[pk], W=["vf"])
                P.op("dve", "reduce_sum", out=st4[:C, 0:1], in_=vf[:C, :], axis=AX.X, R=["vf"], W=["st4"])
                ts("dve", st4[:C, 1:2], st4[:C, 0:1], -1.0 / D, ALU.mult, R=["st4"], W=["st4"])
                ts("dve", vc[:C, :], vf[:C, :], st4[:C, 1:2], ALU.add, R=["vf", "st4"], W=["vc"])
                tt("dve", vf[:C, :], vc[:C, :], vc[:C, :], ALU.mult, R=["vc"], W=["vf"])
                P.op("dve", "reduce_sum", out=st4[:C, 2:3], in_=vf[:C, :], axis=AX.X, R=["vf"], W=["st4"])
                act(st4[:C, 3:4], st4[:C, 2:3], AF.Ln, R=["st4", "epsc"], W=["st4"], bias=epsc[:C, :], scale=1.0 / D)
                act(st4[:C, 4:5], st4[:C, 3:4], AF.Exp, R=["st4"], W=["st4"], scale=-0.5)
                stt("dve", vc[:C, :], vc[:C, :], st4[:C, 4:5], algb[:C, l, :], ALU.mult, ALU.mult,
                    R=["vc", "st4", "algb"], W=["vc"])
                tt("dve", vc[:C, :], vc[:C, :], albb[:C, l, :], ALU.add, R=["vc", "albb"], W=["vc"])
                if seq["kind"] == "s":
                    P.dma("sp", ngv_s[l, 0, :, :], vc[:C, :], R=["vc"], W=[], semkey="dma_ngv")
                cp("act", vnb[:C, :], vc[:C, :], R=["vc"], W=["vnb"])
                pp, pks = ps_pair()
                ppv = pp.rearrange("p (g t) -> p g t", g=8)
                for g in range(8):
                    k = pks[g // 4]
                    mm(ppv[:, g, :C], vnb[:C, g * 128:(g + 1) * 128], wstb[:C, l, g, :C], True, False,
                       R=["vnb", "wstb"], W=[k])
                    mm(ppv[:, g, :C], onesb[0:1, :], bsb[0:1, l, g * 128:g * 128 + C], False, True,
                       R=["onesb", "bsb"], W=[k])
                tt("dve", uT[:, :, c0:c0 + C], ppv[:, :, :C], uT[:, :, c0:c0 + C], ALU.mult, R=pks + ["uT"], W=["uT"])
            for g in range(2):
                w_done(wv[g][2])
            if prep_gen is not None:
                for _ in prep_gen:
                    pass
            P.stage("gmlp_%d" % l)
            qp = qkvpre
            qk_ = "qkvpre"
            cp("pool", qp[:, :, 0:3], halo[l][:], R=["halo%d" % l], W=[qk_])
            for g in range(6):
                wt, wk, wi_ = w_get("qkv%d" % g)

                def ev(m0, nb, pv, pk, g=g):
                    cp("act", qp[:, g * 4 + m0:g * 4 + m0 + nb, 3:3 + T], pv, R=[pk], W=[qk_])
                    import os as _os
                    if last_tile and not _os.environ.get("K_NOCVF"):
                        cp("dve", cvf[:, g * 4 + m0:g * 4 + m0 + nb, :], pv[:, :, T - 3:T], R=[pk], W=["cvf"])
                fm_proj(T, wt, wk, hT, "hT", 4, ev)
                w_done(wi_)
            P.stage("qkv_%d" % l)
            if last_tile:
                dst = (ncv_p[l, seq["idx"]] if seq["kind"] == "p" else ncv_s[l, 0])
                P.dma("sp", dst, cvf[:], R=["cvf"], W=[], semkey="dma_cvf")
            P.stage("cvfdma_%d" % l)
            cp("pool", halo[l][:], qp[:, :, T:T + 3], R=[qk_], W=["halo%d" % l])
            P.stage("halo_%d" % l)
            tmpv = vf[:, :8 * T].rearrange("p (b t) -> p b t", b=8)
            vacc = vc[:, :8 * T].rearrange("p (b t) -> p b t", b=8)
            for gi in range(3):
                e_ = "dve" if gi < 2 else "pool"
                accv = qkf[:, gi * 8:(gi + 1) * 8, :T] if gi < 2 else vacc
                acck = "qkf" if gi < 2 else "vc"
                for j in range(CW):
                    wj = cwc[:, l, gi * 8:(gi + 1) * 8, j:j + 1].to_broadcast([128, 8, T])
                    src = qp[:, gi * 8:(gi + 1) * 8, j:j + T]
                    if j == 0:
                        tt(e_, accv, src, wj, ALU.mult, R=[qk_, "cwc"], W=[acck])
                    else:
                        tt(e_, tmpv, src, wj, ALU.mult, R=[qk_, "cwc"], W=["vf"])
                        tt(e_, accv, accv, tmpv, ALU.add, R=[acck, "vf"], W=[acck])
            P.stage("taps_%d" % l)
            act(qkf[:, :, :T], qkf[:, :, :T], AF.Silu, R=["qkf"], W=["qkf"])
            act(vT[:, :, :T], vacc, AF.Silu, R=["vc"], W=["vT"])
            P.stage("silu_%d" % l)
            tt("dve", sqb[:, :, :T], qkf[:, :, :T], qkf[:, :, :T], ALU.mult, R=["qkf"], W=["sqb"])
            hpg = 4
            for hg in range(0, 16, hpg):
                ph, pk = ps_half()
                pv = ph[:, :hpg * T].rearrange("p (b t) -> p b t", b=hpg)
                for b in range(hpg):
                    mm(pv[:, b, :], onesb[:], sqb[:, hg + b, :T], True, True, R=["onesb", "sqb"], W=[pk])
                rv = rstd[:, :hpg * T].rearrange("p (b t) -> p b t", b=hpg)
                if hg < 8:
                    act(rv, pv, AF.Ln, R=[pk, "epsq"], W=["rstd"], bias=epsq[:], scale=128.0)
                else:
                    act(rv, pv, AF.Ln, R=[pk, "epsc"], W=["rstd"], bias=epsc[:], scale=1.0)
                act(rv, rv, AF.Exp, R=["rstd"], W=["rstd"], scale=-0.5)
                tt("dve", qkT[:, hg:hg + hpg, :T], qkf[:, hg:hg + hpg, :T], rv, ALU.mult, R=["qkf", "rstd"], W=["qkT"])
            P.stage("conv_%d" % l)
            def zfill(g):
                wt, wk, wi_ = w_get("z%d" % g)

                def ev(m0, nb, pv, pk, g=g):
                    act(zsT[:, g * 4 + m0:g * 4 + m0 + nb, :T], pv, AF.Silu, R=[pk], W=["zsT"])
                fm_proj(T, wt, wk, hT, "hT", 4, ev)
                w_done(wi_)
            for c in range(NCH):
                delta_chunk(l, T, C, c * C, hoisted=(NCH == 1), fillers=([zfill] if c == NCH - 1 else []))

            P.stage("delta_%d" % l)
            tt("dve", sqb[:, 0:8, :T], oT[:, :, :T], oT[:, :, :T], ALU.mult, R=["oT"], W=["sqb"])
            for hg in range(0, 8, hpg):
                ph, pk = ps_half()
                pv = ph[:, :hpg * T].rearrange("p (b t) -> p b t", b=hpg)
                for b in range(hpg):
                    mm(pv[:, b, :], onesb[:], sqb[:, hg + b, :T], True, True, R=["onesb", "sqb"], W=[pk])
                rv = rstd[:, :hpg * T].rearrange("p (b t) -> p b t", b=hpg)
                act(rv, pv, AF.Ln, R=[pk, "epsc"], W=["rstd"], bias=epsc[:], scale=1.0 / 128.0)
                act(rv, rv, AF.Exp, R=["rstd"], W=["rstd"], scale=-0.5)
                tt("dve", oT[:, hg:hg + hpg, :T], oT[:, hg:hg + hpg, :T], rv, ALU.mult, R=["oT", "rstd"], W=["oT"])
            stt("dve", zbT[:, :, :T], oT[:, :, :T], onc[:, l:l + 1], zsT[:, :, :T], ALU.mult, ALU.mult,
                R=["oT", "onc", "zsT"], W=["zbT"])
            P.stage("onorm_%d" % l)
            for g in range(2):
                wt, wk, wi_ = w_get("ga%d" % g)

                def ev(m0, nb, pv, pk, g=g):
                    act(sga[:, m0:m0 + nb, :T], pv, AF.Sigmoid, R=[pk], W=["sga"])
                fm_proj(T, wt, wk, hT, "hT", 4, ev)
                w_done(wi_)
                wt, wk, wi_ = w_get("gb%d" % g)

                def ev(m0, nb, pv, pk, g=g):
                    act(sgb[:, m0:m0 + nb, :T], pv, AF.Sigmoid, R=[pk], W=["sgb"])
                fm_proj(T, wt, wk, hT, "hT", 4, ev)
                w_done(wi_)
                wt, wk, wi_ = w_get("pa%d" % g)

                def ev(m0, nb, pv, pk, g=g):
                    tt("dve", mtf[:, m0:m0 + nb, :T], pv, sga[:, m0:m0 + nb, :T], ALU.mult,
                       R=[pk, "sga"], W=["mtf"])
                fm_proj(T, wt, wk, uT, "uT", 4, ev)
                w_done(wi_)
                wt, wk, wi_ = w_get("pb%d" % g)

                def ev(m0, nb, pv, pk, g=g):
                    tt("dve", mixT[:, g * 4 + m0:g * 4 + m0 + nb, :T], pv, sgb[:, m0:m0 + nb, :T], ALU.mult,
                       R=[pk, "sgb"], W=["mixT"])
                    tt("pool", mixT[:, g * 4 + m0:g * 4 + m0 + nb, :T], mixT[:, g * 4 + m0:g * 4 + m0 + nb, :T],
                       mtf[:, m0:m0 + nb, :T], ALU.add, R=["mixT", "mtf"], W=["mixT"])
                fm_proj(T, wt, wk, zbT, "zbT", 4, ev)
                w_done(wi_)
            for g in range(2):
                wt, wk, wi_ = w_get("wo%d" % g)

                def ev(m0, nb, pv, pk, g=g):
                    tt("dve", xT[:, g * 4 + m0:g * 4 + m0 + nb, :T], pv, xT[:, g * 4 + m0:g * 4 + m0 + nb, :T], ALU.add,
                       R=[pk, "xT"], W=["xT"])
                fm_proj(T, wt, wk, mixT, "mixT", 4, ev)
                w_done(wi_)
            P.stage("merge_%d" % l)
            rmsnorm_fm(T, ln2c[:, l, :], "ln2c", hT, "hT")
            acc = pd[3][:, :8 * T].rearrange("p (m t) -> p m t", m=8)
            rlb = [rl, rl2]

            def ffn_up(g):
                wt, wk, wi_ = w_get("up%d" % g)
                hb = hid[g % 2]
                hk = "hid%d" % (g % 2)
                rb_ = rlb[g % 2]
                rk_ = "rl%d" % (g % 2)

                def ev(m0, nb, pv, pk, hb=hb, hk=hk):
                    act(rb_[:, m0:m0 + nb, :T], pv, AF.Relu, R=[pk], W=[rk_])
                    tt("pool", hb[:, m0:m0 + nb, :T], rb_[:, m0:m0 + nb, :T], rb_[:, m0:m0 + nb, :T], ALU.mult,
                       R=[rk_], W=[hk])
                fm_proj(T, wt, wk, hT, "hT", 4, ev)
                w_done(wi_)

            def ffn_down(g):
                hb = hid[g % 2]
                hk = "hid%d" % (g % 2)
                wt, wk, wi_ = w_get("dn%d" % g)
                wdv = wt[:].rearrange("p a b -> p (a b)").rearrange("p (kc n) -> p kc n", kc=4)
                for m in range(8):
                    for kc in range(4):
                        last = (m == 7 and kc == 3)
                        first_in_bank = (g == 0 and kc == 0 and (m * T) % 512 == 0)
                        P.op("pe", "matmul", acc[:, m, :], lhsT=wdv[:, kc, m * 128:(m + 1) * 128], rhs=hb[:, kc, :T],
                             start=first_in_bank, stop=(g == 7 and kc == 3), skip_group_check=True,
                             R=[wk, hk], W=["pd3a", "pd3b"], inc=(last or (g == 7 and kc == 3)))
                w_done(wi_)

            ffn_up(0)
            for g in range(8):
                if g + 1 < 8:
                    ffn_up(g + 1)
                ffn_down(g)
            tt("dve", xT[:, :, :T], acc, xT[:, :, :T], ALU.add, R=["pd3a", "pd3b", "xT"], W=["xT"])
            P.stage("ffn_end_%d_%s" % (l, seq["kind"] + str(seq["idx"])))

        def delta_prep(l, T, C, c0):
            hp = min(8, 512 // C)
            nhalf = 8 // hp
            gtri, DTi = B1, B1
            ngb, Ds, Atmp = B2, B2, B2

            def w8(ps_ap, np_):
                return ps_ap[:np_, :8 * C].rearrange("p (h x) -> p h x", h=8)
            ph, pk = ps_half()
            for kc in range(8):
                mm(ph[:C, :16], hT[:, kc, c0:c0 + C], wba[:, l, kc, :], kc == 0, kc == 7, R=["hT", "wba"], W=[pk])
            cp("dve", ba[:C, :], ph[:C, :16], R=[pk], W=["ba"])
            act(beta[:C, :], ba[:C, 0:8], AF.Sigmoid, R=["ba"], W=["beta"])
            tt("dve", gx[:C, :], ba[:C, 8:16], dtbb[:C, l, :], ALU.add, R=["ba", "dtbb"], W=["gx"])
            ts("dve", gm[:C, :], gx[:C, :], 0.0, ALU.max, R=["gx"], W=["gm"])
            stt("dve", gn[:C, :], gm[:C, :], -2.0, gx[:C, :], ALU.mult, ALU.add, R=["gm", "gx"], W=["gn"])
            act(gn[:C, :], gn[:C, :], AF.Exp, R=["gn"], W=["gn"])
            act(gn[:C, :], gn[:C, :], AF.Ln, R=["gn"], W=["gn"], bias=1.0, scale=1.0)
            tt("dve", gn[:C, :], gn[:C, :], gm[:C, :], ALU.add, R=["gn", "gm"], W=["gn"])
            tt("dve", gg[:C, :], gn[:C, :], nexpA[:C, l, :], ALU.mult, R=["gn", "nexpA"], W=["gg"])
            yield
            P.stage("d_g_%d" % l)
            ph, pk = ps_half()
            mm(ph[:C, 0:8], trif[:C, :C], gg[:C, :], True, True, R=["trif", "gg"], W=[pk], inc=False)
            mm(ph[:C, 8:16], trigt[:C, :C], gg[:C, :], True, True, R=["trigt", "gg"], W=[pk])
            act(egc[:C, :], ph[:C, 0:16], AF.Exp, R=[pk], W=["egc"])
            stt("dve", nbe[:C, :], beta[:C, :], -1.0, egc[:C, 0:8], ALU.mult, ALU.mult, R=["beta", "egc"], W=["nbe"])
            P.stage("d_col_%d" % l)
            tt("pool", gtri[:C, :, :C], trif[:C, :C].unsqueeze(1).to_broadcast([C, 8, C]),
               gg[:C, :].unsqueeze(2).to_broadcast([C, 8, C]), ALU.mult, R=["trif", "gg"], W=["B1"])
            ts("dve", ngb[:C, :, :C], gg[:C, :].unsqueeze(2).to_broadcast([C, 8, C]), -1.0, ALU.mult,
               R=["gg"], W=["B2"])
            yield
            P.stage("d_rhs_%d" % l)
            pa, pak = ps_pair()
            pb, pbk = ps_pair()
            pc, pck = ps_pair()

            pav, pbv, pcv = w8(pa, 128), w8(pb, C), w8(pc, C)
            for hh in range(nhalf):
                hs = slice(hh * hp, (hh + 1) * hp)
                ka, kb_, kc_ = pak[hh if nhalf > 1 else 0], pbk[hh if nhalf > 1 else 0], pck[hh if nhalf > 1 else 0]
                mm(pav[:, hs, :], onesf[:C, :], gtri[:C, hs, :C], True, True, R=["onesf", "B1"], W=[ka])
                mm(pbv[:, hs, :], onesf[:C, :C], gtri[:C, hs, :C], True, False, R=["onesf", "B1"], W=[kb_])
                mm(pbv[:, hs, :], trif[:C, :C], ngb[:C, hs, :C], False, False, R=["trif", "B2"], W=[kb_])
                mm(pbv[:, hs, :], identb[:C, :C], maskb[:C, hs, :C], False, True, R=["identb", "maskb"], W=[kb_])
                mm(pcv[:, hs, :], nonesf[:C, :C], gtri[:C, hs, :C], True, False, R=["nonesf", "B1"], W=[kc_])
                mm(pcv[:, hs, :], ntrif[:C, :C], ngb[:C, hs, :C], False, False, R=["ntrif", "B2"], W=[kc_])
                mm(pcv[:, hs, :], identb[:C, :C], maskc[:C, hs, :C], False, True, R=["identb", "maskc"], W=[kc_])
            act(egam[:, :, :C], pav, AF.Exp, R=pak, W=["egam"])
            act(DTi[:C, :, :C], pbv, AF.Exp, R=pbk, W=["B1"])
            act(Ds[:C, :, :C], pcv, AF.Exp, R=pck, W=["B2"])
            tt("pool", NA[1][:C, :, :C], identf[:C, :C].unsqueeze(1).to_broadcast([C, 8, C]),
               beta[:C, :].unsqueeze(2).to_broadcast([C, 8, C]), ALU.mult, R=["identf", "beta"], W=["NA1"])
            pbb, pbbk = ps_pair()
            pbbv = w8(pbb, C)
            for hh in range(nhalf):
                hs = slice(hh * hp, (hh + 1) * hp)
                mm(pbbv[:, hs, :], onesf[:C, :C], NA[1][:C, hs, :C], True, True, R=["onesf", "NA1"],
                   W=[pbbk[hh if nhalf > 1 else 0]])
            cp("act", NA[1][:C, :, :C], pbbv, R=pbbk, W=["NA1"])
            yield

        def delta_chunk(l, T, C, c0, hoisted=False, fillers=()):
            hp = min(8, 512 // C)
            nhalf = 8 // hp

            def wide(ps, C2):
                return ps[:, :].rearrange("p (h x) -> p h x", h=8) if False else None

            qv = qkT[:, 0:8, c0:c0 + C]
            kv = qkT[:, 8:16, c0:c0 + C]
            gtri, DTi = B1, B1
            ngb, Ds, Atmp = B2, B2, B2
            Yf, rtmp = B4, B4

            def w8(ps_ap, np_):
                return ps_ap[:np_, :8 * C].rearrange("p (h x) -> p h x", h=8)
            if not hoisted:
                for _ in delta_prep(l, T, C, c0):
                    pass
            P.stage("d_exp_%d" % l)
            pkk, pkkk = ps_pair()
            pqk, pqkk = ps_pair()
            pkkv, pqkv = w8(pkk, C), w8(pqk, C)
            for h in range(8):
                k_ = pkkk[(h // hp) if nhalf > 1 else 0]
                mm(pkkv[:, h, :], kv[:, h, :], kv[:, h, :], True, True, R=["qkT"], W=[k_], inc=(h % hp == hp - 1))
            for h in range(8):
                k_ = pqkk[(h // hp) if nhalf > 1 else 0]
                mm(pqkv[:, h, :], kv[:, h, :], qv[:, h, :], True, True, R=["qkT"], W=[k_], inc=(h % hp == hp - 1))
            tt("dve", PT[:C, :, :C], pqkv, DTi[:C, :, :C], ALU.mult, R=pqkk + ["B1"], W=["PT"])
            tt("dve", Atmp[:C, :, :C], pkkv, Ds[:C, :, :C], ALU.mult, R=pkkk + ["B2"], W=["B2"])
            P.stage("d_kk_%d" % l)
            NTA = [B2, NTA1]
            NTk = ["B2", "NTA1"]
            NAk = ["NA0", "NA1"]
            tt("dve", B2[:C, :, :C], B2[:C, :, :C], beta[:C, :].unsqueeze(2).to_broadcast([C, 8, C]), ALU.mult,
               R=["B2", "beta"], W=["B2"])
            P.stage("d_A_%d" % l)
            tt("dve", NA[0][:C, :, :C], pkkv, DTi[:C, :, :C], ALU.mult, R=pkkk + ["B1"], W=["NA0"])
            tt("dve", NA[0][:C, :, :C], NA[0][:C, :, :C], NA[1][:C, :, :C], ALU.mult, R=["NA0", "NA1"], W=["NA0"])
            tt("dve", NA[0][:C, :, :C], NA[0][:C, :, :C], offd[:C, :C].unsqueeze(1).to_broadcast([C, 8, C]), ALU.mult,
               R=["NA0", "offd"], W=["NA0"])
            P.stage("d_tr_%d" % l)
            tt("dve", Yf[:C, :, :C], identf[:C, :C].unsqueeze(1).to_broadcast([C, 8, C]), NA[0][:C, :, :C], ALU.subtract,
               R=["identf", "NA0"], W=["B4"])
            nlev = max(1, int(np.ceil(np.log2(C))) - 1)
            cur = 0
            for lev in range(nlev):
                nxt = 1 - cur
                lastlev = (lev == nlev - 1)
                Nc, NTc = NA[cur], NTA[cur]
                Nck, NTck = NAk[cur], NTk[cur]
                p2t, p2tk = ps_pair()
                p2tv = w8(p2t, C)
                for h in range(8):
                    k_ = p2tk[(h // hp) if nhalf > 1 else 0]
                    mm(p2tv[:, h, :], Nc[:C, h, :C], NTc[:C, h, :C], True, True, R=[Nck, NTck], W=[k_],
                       inc=(h % hp == hp - 1))
                if not lastlev:
                    p2, p2k = ps_pair()
                    p2v = w8(p2, C)
                    for h in range(8):
                        k_ = p2k[(h // hp) if nhalf > 1 else 0]
                        mm(p2v[:, h, :], NTc[:C, h, :C], Nc[:C, h, :C], True, True, R=[Nck, NTck], W=[k_],
                           inc=(h % hp == hp - 1))
                cp("act", NTA[nxt][:C, :, :C], p2tv, R=p2tk, W=[NTk[nxt]])
                if not lastlev:
                    cp("dve", NA[nxt][:C, :, :C], p2v, R=p2k, W=[NAk[nxt]])
                py, pyk = ps_pair()
                pyv = w8(py, C)
                for h in range(8):
                    k_ = pyk[(h // hp) if nhalf > 1 else 0]
                    mm(pyv[:, h, :], NTA[nxt][:C, h, :C], Yf[:C, h, :C], True, True, R=[NTk[nxt], "B4"], W=[k_],
                       inc=(h % hp == hp - 1))
                tt("dve", Yf[:C, :, :C], Yf[:C, :, :C], pyv, ALU.add, R=["B4"] + pyk, W=["B4"])
                cur = nxt
            cp("pool", Yb[:C, :, :C], Yf[:C, :, :C], R=["B4"], W=["Yb"])
            P.stage("d_neu_%d" % l)
            pt_, ptk = ps_half()
            ptv = pt_.bitcast(BF16)[:C, :1024].rearrange("p (h x) -> p h x", h=8)
            for h in range(8):
                tr(ptv[:, h, :], vT[:, h, c0:c0 + C], identb[:], R=["vT", "identb"], W=[ptk])
            tt("dve", bv[:C, :, :], ptv, beta[:C, :].unsqueeze(2).to_broadcast([C, 8, 128]), ALU.mult,
               R=[ptk, "beta"], W=["bv"])
            pt2, pt2k = ps_half()
            pt2v = pt2.bitcast(BF16)[:C, :1024].rearrange("p (h x) -> p h x", h=8)
            for h in range(8):
                tr(pt2v[:, h, :], kv[:, h, :], identb[:], R=["qkT", "identb"], W=[pt2k])
            tt("dve", kg[:C, :, :], pt2v, egc[:C, 8:16].unsqueeze(2).to_broadcast([C, 8, 128]), ALU.mult,
               R=[pt2k, "egc"], W=["kg"])
            tt("pool", qg[:, :, :C], qv, egam[:, :, :C], ALU.mult, R=["qkT", "egam"], W=["qg"])
            P.stage("d_tok_%d" % l)
            Sl, Sbl = S[l], Sb[l]
            Sk, Sbk = "S%d" % l, "Sb%d" % l
            pks_, pksk = ps_pair()
            pksv = pks_[:C, :].rearrange("p (h x) -> p h x", h=8)
            for h in range(8):
                mm(pksv[:, h, :], kv[:, h, :], Sbl[:, h, :], True, True, R=["qkT", Sbk], W=[pksk[h // 4]],
                   inc=(h % 4 == 3))
            for f_ in fillers:
                f_(0)
            tt("dve", rtmp[:C, :, :], pksv, nbe[:C, :].unsqueeze(2).to_broadcast([C, 8, 128]), ALU.mult,
               R=pksk + ["nbe"], W=["B4"])
            tt("dve", rb[:C, :, :], rtmp[:C, :, :], bv[:C, :, :], ALU.add, R=["B4", "bv"], W=["rb"])
            pvn, pvnk = ps_pair()
            pvnv = pvn[:C, :].rearrange("p (h x) -> p h x", h=8)
            for h in range(8):
                mm(pvnv[:, h, :], Yb[:C, h, :C], rb[:C, h, :], True, True, R=["Yb", "rb"], W=[pvnk[h // 4]],
                   inc=(h % 4 == 3))
            for f_ in fillers:
                f_(1)
            cp("act", vnw[:C, :, :], pvnv, R=pvnk, W=["vnw"])
            po, pok = ps_pair()
            pov = w8(po, 128)
            for h in range(8):
                k_ = pok[(h // hp) if nhalf > 1 else 0]
                mm(pov[:, h, :], Sbl[:, h, :], qg[:, h, :C], True, False, R=[Sbk, "qg"], W=[k_])
                mm(pov[:, h, :], vnw[:C, h, :], PT[:C, h, :C], False, True, R=["vnw", "PT"], W=[k_],
                   inc=(h % hp == hp - 1))
            cp("act", oT[:, :, c0:c0 + C], pov, R=pok, W=["oT"])
            psn, psnk = ps_pair()
            psnv = psn[:, :].rearrange("p (h x) -> p h x", h=8)
            for h in range(8):
                mm(psnv[:, h, :], kg[:C, h, :], vnw[:C, h, :], True, True, R=["kg", "vnw"], W=[psnk[h // 4]],
                   inc=(h % 4 == 3))
            tt("dve", Sl[:], Sl[:], egam[:, :, C - 1:C].to_broadcast([128, 8, 128]), ALU.mult, R=[Sk, "egam"], W=[Sk])
            tt("dve", Sl[:], Sl[:], psnv, ALU.add, R=[Sk] + psnk, W=[Sk])
            cp("act", Sbl[:], Sl[:], R=[Sk], W=[Sbk])

        seqs = [
            {"kind": "p", "idx": 0, "L": SEQ, "T": 128, "C": 128},
            {"kind": "p", "idx": 1, "L": SEQ, "T": 128, "C": 128},
            {"kind": "s", "idx": 0, "L": DEC_SEQ, "T": DEC_SEQ, "C": DEC_SEQ},
        ]
        for sq in seqs:
            ntile = sq["L"] // sq["T"]
            for _ in range(ntile):
                for l in range(DEPTH):
                    wplan.extend(layer_plan(l))

        for sq in seqs:
            T, C = sq["T"], sq["C"]
            ntile = sq["L"] // T
            for l in range(DEPTH):
                if sq["kind"] == "p":
                    P.op("pool", "memset", S[l][:], 0.0, W=["S%d" % l])
                    P.op("pool", "memset", Sb[l][:], 0.0, W=["Sb%d" % l])
                    P.op("pool", "memset", halo[l][:], 0.0, W=["halo%d" % l])
                else:
                    P.dma("sp", S[l][:], sdelta[l].rearrange("h d v -> d h v"), R=[], W=["S%d" % l], semkey="dma_Sin%d" % l)
                    cp("act", Sb[l][:], S[l][:], R=["S%d" % l], W=["Sb%d" % l])
                    P.dma("sp", cvf[:], sconv[l], R=[], W=["cvf"], semkey="dma_cvfin")
                    cp("dve", halo[l][:], cvf[:], R=["cvf"], W=["halo%d" % l])
            for ti in range(ntile):
                t0 = ti * T
                src = (xT_p[sq["idx"], :, :, t0:t0 + T] if sq["kind"] == "p" else xT_s[0, :, :, :])
                P.dma("sp", xT[:, :, :T], src.rearrange("kc p t -> p kc t"), R=[], W=["xT"], semkey="dma_xT")
                for l in range(DEPTH):
                    layer(l, T, C, sq, ti == 0, ti == ntile - 1)
                tt("dve", sqb[:, 0:8, :T], xT[:, :, :T], xT[:, :, :T], ALU.mult, R=["xT"], W=["sqb"])
                ph, pk = ps_half()
                for kc in range(8):
                    mm(ph[:, :T], onesb[:], sqb[:, kc, :T], kc == 0, kc == 7, R=["onesb", "sqb"], W=[pk])
                act(rstd[:, :T], ph[:, :T], AF.Ln, R=[pk, "epsc"], W=["rstd"], bias=epsc[:], scale=1.0 / D)
                act(rstd[:, :T], rstd[:, :T], AF.Exp, R=["rstd"], W=["rstd"], scale=-0.5)
                for kc in range(8):
                    stt("dve", oT[:, kc, :T], xT[:, kc, :T], fnc[:, kc:kc + 1], rstd[:, :T], ALU.mult, ALU.mult,
                        R=["xT", "rstd", "fnc"], W=["oT"])
                dst = (yT_p[sq["idx"], :, :, t0:t0 + T] if sq["kind"] == "p" else yT_s[0, :, :, :])
                P.dma("sp", dst.rearrange("kc p t -> p kc t"), oT[:, :, :T], R=["oT"], W=[], semkey="dma_y")
                P.stage("tile_end_%s_%d" % (sq["kind"] + str(sq["idx"]), ti))
            for l in range(DEPTH):
                dst = (ndl_p[l, sq["idx"]] if sq["kind"] == "p" else ndl_s[l, 0])
                P.dma("sp", dst.rearrange("h d v -> d h v"), S[l][:], R=["S%d" % l], W=[], semkey="dma_sout%d" % l)
        import os as _os
        if _os.environ.get("K_DELAY"):
            P.dead = False
            for _i in range(int(_os.environ["K_DELAY"])):
                P.op("pe", "matmul", pd[0][:, 0:512], lhsT=onesb[:], rhs=wbuf[0][:, 0, :], start=True, stop=True,
                     R=["onesb", "wbuf0"], W=["pd0a"], inc=(_i % 64 == 63))
            P.dead = True
        assert P.dead or wstate["consumed"] == len(wplan), (wstate, len(wplan))
        for sk, v in P.dma_cnt.items():
            P._need("sp", sk, v)
        for e_ in ("pe", "act", "dve", "pool"):
            P._need("sp", e_, P.cnt[e_])
        build.ninst = P.ninst
    return nc


def _prep_inputs(inputs, SEQ):
    f = lambda a: np.ascontiguousarray(np.asarray(a, dtype=np.float32))
    xp = f(inputs["x_prompt"])[:, :SEQ]
    xs = f(inputs["x_sample"])
    xpT = np.ascontiguousarray(xp.reshape(16, SEQ, 8, 128).transpose(0, 2, 3, 1))
    xsT = np.ascontiguousarray(xs.reshape(8, DEC_SEQ, 8, 128).transpose(0, 2, 3, 1))
    sc = f(inputs["state_conv"])
    scT = np.ascontiguousarray(sc.reshape(DEPTH, 8, 3, 24, 128).transpose(1, 0, 4, 3, 2))
    sd = f(inputs["state_delta"])
    rep = lambda a: np.ascontiguousarray(np.broadcast_to(a[None], (128,) + a.shape))
    common = {
        "ln1c": np.ascontiguousarray(f(inputs["ln1"]).reshape(DEPTH, 8, 128).transpose(2, 0, 1)),
        "ln2c": np.ascontiguousarray(f(inputs["ln2"]).reshape(DEPTH, 8, 128).transpose(2, 0, 1)),
        "fnc": np.ascontiguousarray(f(inputs["final_norm"]).reshape(8, 128).transpose(1, 0)),
        "cwc": np.ascontiguousarray(f(inputs["conv_w"]).reshape(DEPTH, 24, 128, CW).transpose(2, 0, 1, 3)),
        "onc": np.ascontiguousarray(f(inputs["o_norm"]).transpose(1, 0)),
        "algb": rep(f(inputs["a_ln_g"])),
        "albb": rep(f(inputs["a_ln_b"])),
        "wstT": np.ascontiguousarray(f(inputs["w_s"]).transpose(3, 0, 1, 2)),
        "bsr": np.ascontiguousarray(f(inputs["b_s"]).reshape(1, DEPTH, D)),
        "alogb": rep(f(inputs["a_log"])),
        "dtbb": rep(f(inputs["dt_bias"])),
        "w_in": f(inputs["w_in"]), "p_a": f(inputs["p_a"]), "p_b": f(inputs["p_b"]), "w_o": f(inputs["w_o"]),
        "w_up": f(inputs["w_up"]), "w_down": f(inputs["w_down"]),
    }
    maps = []
    for c in range(NCORES):
        m = dict(common)
        m["xT_p"] = np.ascontiguousarray(xpT[2 * c:2 * c + 2])
        m["xT_s"] = np.ascontiguousarray(xsT[c:c + 1])
        m["sconv"] = np.ascontiguousarray(scT[c])
        m["sdelta"] = np.ascontiguousarray(sd[:, c])
        maps.append(m)
    return maps


def _assemble(results, SEQ):
    yp = np.concatenate([r["yT_p"] for r in results], axis=0)
    y_prompt = np.ascontiguousarray(yp.transpose(0, 3, 1, 2).reshape(16, SEQ, D))
    ys = np.concatenate([r["yT_s"] for r in results], axis=0)
    y_sample = np.ascontiguousarray(ys.transpose(0, 3, 1, 2).reshape(8, DEC_SEQ, D))
    cvp = np.concatenate([r["ncv_p"] for r in results], axis=1)
    new_conv_prompt = np.ascontiguousarray(cvp.transpose(0, 1, 4, 3, 2).reshape(DEPTH, 16, 3, 3072))
    new_delta_prompt = np.ascontiguousarray(np.concatenate([r["ndl_p"] for r in results], axis=1))
    cvs = np.concatenate([r["ncv_s"] for r in results], axis=1)
    new_conv_sample = np.ascontiguousarray(cvs.transpose(0, 1, 4, 3, 2).reshape(DEPTH, 8, 3, 3072))
    new_delta_sample = np.ascontiguousarray(np.concatenate([r["ndl_s"] for r in results], axis=1))
    new_gv = np.ascontiguousarray(np.concatenate([r["ngv_s"] for r in results], axis=1))
    outs = (y_prompt, y_sample, new_conv_prompt, new_delta_prompt, new_conv_sample, new_delta_sample, new_gv)
    return tuple(np.asarray(o, dtype=np.float32) for o in outs)


def run(inputs, SEQ, stop_at=None, lite=False, start_at=None):
    nc = build(SEQ, stop_at, lite, start_at)
    maps = _prep_inputs(inputs, SEQ)
    if lite:
        for m in maps:
            for k in ("w_in", "p_a", "p_b", "w_o", "w_up", "w_down"):
                m.pop(k)
    res = run_bass_kernel_spmd(nc, maps, core_ids=list(range(NCORES)))
    return _assemble(res.results, SEQ)


def kernel(**inputs):
    return run(inputs, SEQ_FULL)
```
